# Optimizing a Trainium2 kernel written in Bass

```python
import numpy as np
import jax
import jax.numpy as jnp
from jax import lax

D_MODEL = 2048
BATCH = 1
SEQ = 8192
DEPTH = 2

GRID_W = 64
CTX_LEN = 256
HEAD_DIM = 64
N_BRANCH = 4
BRANCH_W = D_MODEL // N_BRANCH
NA_HEADS = BRANCH_W // HEAD_DIM
WIN_ROWS = 8
WIN_COLS = 16
RG_BLOCKS = BRANCH_W // HEAD_DIM
RG_BW = BRANCH_W // RG_BLOCKS
RG_C = 8.0
RG_CONV = 4
CONF_WIDTH = 31
GQA_HEADS = BRANCH_W // HEAD_DIM
GQA_KV_HEADS = 2
GQA_GROUP = GQA_HEADS // GQA_KV_HEADS
Q_BLOCK = 128
ROPE_THETA = 10000.0
D_FF = 256 * ((8 * D_MODEL // 3 + 255) // 256)
FFN_CONV = 3
EPS = 1e-6
SPLIT_SIZES = (BRANCH_W, BRANCH_W, BRANCH_W,
               BRANCH_W, BRANCH_W,
               2 * BRANCH_W,
               GQA_HEADS * HEAD_DIM, GQA_KV_HEADS * HEAD_DIM, GQA_KV_HEADS * HEAD_DIM,
               N_BRANCH * D_MODEL)
N_IN = sum(SPLIT_SIZES)

kernel_name = 'hybrid_natten_rglru_conformer_gqa_dit'


def rmsnorm(x, g):
    xf = x.astype(jnp.float32)
    y = xf * lax.rsqrt(jnp.mean(xf * xf, axis=-1, keepdims=True) + EPS)
    return (y * g.astype(jnp.float32)).astype(x.dtype)


def layernorm(x, g, b):
    xf = x.astype(jnp.float32)
    mu = jnp.mean(xf, axis=-1, keepdims=True)
    xc = xf - mu
    y = xc * lax.rsqrt(jnp.mean(xc * xc, axis=-1, keepdims=True) + EPS)
    return (y * g.astype(jnp.float32) + b.astype(jnp.float32)).astype(x.dtype)


def modulate(h, shift, scale):
    return h * (1.0 + scale[:, None, :]) + shift[:, None, :]


def dwconv(x, w, pad):
    return lax.conv_general_dilated(x, w[:, None, :].astype(x.dtype), (1,), [pad],
                                    dimension_numbers=('NWC', 'WIO', 'NWC'),
                                    feature_group_count=x.shape[-1])


def heads(t, n):
    return t.reshape(t.shape[0], t.shape[1], n, HEAD_DIM)


def axial_rope(x, pos_r, pos_c):
    half = x.shape[-1] // 2
    n_f = half // 2
    inv_freq = ROPE_THETA ** (-jnp.arange(n_f, dtype=jnp.float32) / n_f)

    def rot(xh, pos):
        ang = pos.astype(jnp.float32)[:, None] * inv_freq[None, :]
        cos = jnp.cos(ang)[None, :, None, :].astype(x.dtype)
        sin = jnp.sin(ang)[None, :, None, :].astype(x.dtype)
        x1, x2 = xh[..., :n_f], xh[..., n_f:]
        return jnp.concatenate([x1 * cos - x2 * sin, x2 * cos + x1 * sin], axis=-1)

    return jnp.concatenate([rot(x[..., :half], pos_r), rot(x[..., half:], pos_c)], axis=-1)


def dense_attn(q, k, v):
    bsz, s, hq, dh = q.shape
    hk = k.shape[2]
    qg = q.reshape(bsz, s, hk, hq // hk, dh)
    sc = jnp.einsum('bqkgd,bskd->bkgqs', qg, k).astype(jnp.float32) * dh ** -0.5
    p = jax.nn.softmax(sc, axis=-1).astype(v.dtype)
    o = jnp.einsum('bkgqs,bskd->bqkgd', p, v)
    return o.reshape(bsz, s, hq * dh)


def natten_latent(q, k, v, k_ctx, v_ctx, rpb, rows):
    bsz, seq, n_h, dh = q.shape
    kr = min(WIN_ROWS, rows)
    kc = WIN_COLS
    n_loc = kr * kc
    cols = jnp.arange(GRID_W)
    col_start = jnp.clip(cols - kc // 2, 0, GRID_W - kc)
    key_cols = col_start[:, None] + jnp.arange(kc)[None, :]
    dc = key_cols - cols[:, None] + (WIN_COLS - 1)
    rpb32 = rpb.astype(jnp.float32)
    q_rows = q.reshape(bsz, rows, GRID_W, n_h, dh).transpose(1, 0, 2, 3, 4)
    scale = dh ** -0.5

    def one_row(args):
        r, q_r = args
        row_start = jnp.clip(r - kr // 2, 0, rows - kr)
        key_rows = row_start + jnp.arange(kr)
        idx = (key_rows[None, :, None] * GRID_W + key_cols[:, None, :]).reshape(GRID_W, n_loc)
        k_g = k[:, idx]
        v_g = v[:, idx]
        dr = key_rows - r + (WIN_ROWS - 1)
        bias = rpb32[:, dr[None, :, None], dc[:, None, :]].reshape(n_h, GRID_W, n_loc)
        s_loc = jnp.einsum('bqhd,bqkhd->bhqk', q_r, k_g).astype(jnp.float32) * scale + bias[None]
        s_ctx = jnp.einsum('bqhd,bchd->bhqc', q_r, k_ctx).astype(jnp.float32) * scale
        p = jax.nn.softmax(jnp.concatenate([s_loc, s_ctx], axis=-1), axis=-1).astype(v.dtype)
        return (jnp.einsum('bhqk,bqkhd->bqhd', p[..., :n_loc], v_g)
                + jnp.einsum('bhqc,bchd->bqhd', p[..., n_loc:], v_ctx))

    out = lax.map(one_row, (jnp.arange(rows), q_rows))
    return out.transpose(1, 0, 2, 3, 4).reshape(bsz, seq, n_h * dh)


def gqa_latent(q, k_all, v_all):
    bsz, seq, n_h, dh = q.shape
    n_blk = seq // Q_BLOCK
    qb = q.reshape(bsz, n_blk, Q_BLOCK, GQA_KV_HEADS, GQA_GROUP, dh).transpose(1, 0, 2, 3, 4, 5)
    scale = dh ** -0.5

    def one_block(q_blk):
        sc = jnp.einsum('bqkgd,bskd->bkgqs', q_blk, k_all).astype(jnp.float32) * scale
        p = jax.nn.softmax(sc, axis=-1).astype(v_all.dtype)
        return jnp.einsum('bkgqs,bskd->bqkgd', p, v_all)

    o = lax.map(one_block, qb)
    return o.transpose(1, 0, 2, 3, 4, 5).reshape(bsz, seq, n_h * dh)


def rglru_gates(x, w_rg, b_rg, rg_lambda):
    bsz, s, _ = x.shape
    xb = x.reshape(bsz, s, RG_BLOCKS, RG_BW)
    g = jnp.einsum('bsni,dgnij->dgbsnj', xb, w_rg).reshape(2, 2, bsz, s, BRANCH_W)
    g = jax.nn.sigmoid((g + b_rg[:, :, None, None, :]).astype(jnp.float32))
    log_a = -RG_C * g[:, 0] * jax.nn.softplus(-rg_lambda.astype(jnp.float32))[:, None, None, :]
    a = jnp.exp(log_a)
    b = jnp.sqrt(-jnp.expm1(2.0 * log_a)) * g[:, 1] * x.astype(jnp.float32)[None]
    return a, b


def linear_scan(a, b, h0):
    def comb(lhs, rhs):
        return lhs[0] * rhs[0], rhs[0] * lhs[1] + rhs[1]
    a_cum, b_cum = lax.associative_scan(comb, (a, b), axis=1)
    return a_cum * h0[:, None, :] + b_cum


def rglru_branch(x_lat, gate_lat, x_ctx, gate_ctx, rg_conv, w_rg, b_rg, rg_lambda, need_ctx):
    pad = (RG_CONV // 2, RG_CONV - 1 - RG_CONV // 2)
    xl = dwconv(x_lat, rg_conv, pad)
    xc = dwconv(x_ctx, rg_conv, pad)
    a_c, b_c = rglru_gates(xc, w_rg, b_rg, rg_lambda)
    a_l, b_l = rglru_gates(xl, w_rg, b_rg, rg_lambda)
    h0 = jnp.zeros((x_ctx.shape[0], BRANCH_W), jnp.float32)
    hc_f = linear_scan(a_c[0], b_c[0], h0)
    hc_b = linear_scan(a_c[1][:, ::-1], b_c[1][:, ::-1], h0)
    hl_f = linear_scan(a_l[0], b_l[0], hc_f[:, -1])
    hl_b = linear_scan(a_l[1][:, ::-1], b_l[1][:, ::-1], hc_b[:, -1])
    y_lat = (hl_f + hl_b[:, ::-1]).astype(x_lat.dtype) * jax.nn.gelu(gate_lat)
    y_ctx = None
    if need_ctx:
        y_ctx = (hc_f + hc_b[:, ::-1]).astype(x_ctx.dtype) * jax.nn.gelu(gate_ctx)
    return y_lat, y_ctx


def conformer_branch(u, conf_dw, conf_ln_g, conf_ln_b):
    val, gt = jnp.split(u, 2, axis=-1)
    y = val * jax.nn.sigmoid(gt)
    y = dwconv(y, conf_dw, (CONF_WIDTH // 2, CONF_WIDTH // 2))
    return jax.nn.silu(layernorm(y, conf_ln_g, conf_ln_b))


def merge(branches, gate_logits, w_branch, w_out):
    bsz, s, _ = gate_logits.shape
    gates = jax.nn.sigmoid(gate_logits.reshape(bsz, s, N_BRANCH, D_MODEL))
    y = gates[:, :, 0] * (branches[0] @ w_branch[0])
    for n in range(1, N_BRANCH):
        y = y + gates[:, :, n] * (branches[n] @ w_branch[n])
    return y @ w_out


def mixer_sublayer(h_lat, h_ctx, w_in, na_rpb, rg_conv, w_rg, b_rg, rg_lambda, conf_dw, conf_ln_g,
                   conf_ln_b, q_norm_g, k_norm_g, w_branch, w_out, pos_r, pos_c, rows, need_ctx):
    split_idx = [int(v) for v in np.cumsum(SPLIT_SIZES)[:-1]]
    zl = jnp.split(h_lat @ w_in, split_idx, axis=-1)
    zc = jnp.split(h_ctx @ w_in, split_idx, axis=-1)
    qa, ka, va = heads(zl[0], NA_HEADS), heads(zl[1], NA_HEADS), heads(zl[2], NA_HEADS)
    qac, kac, vac = heads(zc[0], NA_HEADS), heads(zc[1], NA_HEADS), heads(zc[2], NA_HEADS)
    a_lat = natten_latent(qa, ka, va, kac, vac, na_rpb, rows)
    b_lat, b_ctx = rglru_branch(zl[3], zl[4], zc[3], zc[4], rg_conv, w_rg, b_rg, rg_lambda, need_ctx)
    c_lat = conformer_branch(zl[5], conf_dw, conf_ln_g, conf_ln_b)
    qd = axial_rope(rmsnorm(heads(zl[6], GQA_HEADS), q_norm_g), pos_r, pos_c)
    kd = axial_rope(rmsnorm(heads(zl[7], GQA_KV_HEADS), k_norm_g), pos_r, pos_c)
    vd = heads(zl[8], GQA_KV_HEADS)
    qdc = rmsnorm(heads(zc[6], GQA_HEADS), q_norm_g)
    kdc = rmsnorm(heads(zc[7], GQA_KV_HEADS), k_norm_g)
    vdc = heads(zc[8], GQA_KV_HEADS)
    d_lat = gqa_latent(qd, jnp.concatenate([kd, kdc], axis=1), jnp.concatenate([vd, vdc], axis=1))
    out_lat = merge([a_lat, b_lat, c_lat, d_lat], zl[9], w_branch, w_out)
    out_ctx = None
    if need_ctx:
        a_ctx = dense_attn(qac, kac, vac)
        c_ctx_b = conformer_branch(zc[5], conf_dw, conf_ln_g, conf_ln_b)
        d_ctx = dense_attn(qdc, kdc, vdc)
        out_ctx = merge([a_ctx, b_ctx, c_ctx_b, d_ctx], zc[9], w_branch, w_out)
    return out_lat, out_ctx


def conv_ffn(h, w_up, ffn_dw, w_down):
    u = dwconv(h @ w_up, ffn_dw, (FFN_CONV // 2, FFN_CONV // 2))
    g, v = jnp.split(u, 2, axis=-1)
    return (jax.nn.silu(g) * v) @ w_down


def setup_inputs(seed: int = 0) -> dict:
    key = jax.random.key(seed)
    ks = jax.random.split(key, 26)
    f32 = jnp.float32

    def nrm(k, shape, scale):
        return jax.random.normal(k, shape, f32) * scale

    a0 = jax.random.uniform(ks[12], (DEPTH, 2, BRANCH_W), f32, 0.9, 0.999)
    return {
        'x': nrm(ks[0], (BATCH, SEQ, D_MODEL), 1.0),
        'c': nrm(ks[1], (BATCH, D_MODEL), 1.0),
        'ctx': nrm(ks[2], (BATCH, CTX_LEN, D_MODEL), 1.0),
        'c_ctx': nrm(ks[3], (D_MODEL,), 1.0),
        'w_ada': nrm(ks[4], (DEPTH, D_MODEL, 6 * D_MODEL), 0.5 * D_MODEL ** -0.5),
        'b_ada': nrm(ks[5], (DEPTH, 6 * D_MODEL), 0.01),
        'g_mix': 1.0 + nrm(ks[6], (DEPTH, D_MODEL), 0.05),
        'w_in': nrm(ks[7], (DEPTH, D_MODEL, N_IN), D_MODEL ** -0.5),
        'na_rpb': nrm(ks[8], (DEPTH, NA_HEADS, 2 * WIN_ROWS - 1, 2 * WIN_COLS - 1), 0.1),
        'rg_conv': nrm(ks[9], (DEPTH, RG_CONV, BRANCH_W), RG_CONV ** -0.5),
        'w_rg': nrm(ks[10], (DEPTH, 2, 2, RG_BLOCKS, RG_BW, RG_BW), RG_BW ** -0.5),
        'b_rg': nrm(ks[11], (DEPTH, 2, 2, BRANCH_W), 0.1),
        'rg_lambda': jnp.log(a0) - jnp.log1p(-a0),
        'conf_dw': nrm(ks[13], (DEPTH, CONF_WIDTH, BRANCH_W), CONF_WIDTH ** -0.5),
        'conf_ln_g': 1.0 + nrm(ks[14], (DEPTH, BRANCH_W), 0.05),
        'conf_ln_b': nrm(ks[15], (DEPTH, BRANCH_W), 0.05),
        'q_norm_g': 1.0 + nrm(ks[16], (DEPTH, HEAD_DIM), 0.05),
        'k_norm_g': 1.0 + nrm(ks[17], (DEPTH, HEAD_DIM), 0.05),
        'w_branch': nrm(ks[18], (DEPTH, N_BRANCH, BRANCH_W, D_MODEL), BRANCH_W ** -0.5),
        'w_out': nrm(ks[19], (DEPTH, D_MODEL, D_MODEL), D_MODEL ** -0.5),
        'g_ffn': 1.0 + nrm(ks[20], (DEPTH, D_MODEL), 0.05),
        'w_up': nrm(ks[21], (DEPTH, D_MODEL, 2 * D_FF), D_MODEL ** -0.5),
        'ffn_dw': nrm(ks[22], (DEPTH, FFN_CONV, 2 * D_FF), FFN_CONV ** -0.5),
        'w_down': nrm(ks[23], (DEPTH, D_FF, D_MODEL), D_FF ** -0.5),
        'g_final': 1.0 + nrm(ks[24], (D_MODEL,), 0.05),
    }


def reference(x, c, ctx, c_ctx, w_ada, b_ada, g_mix, w_in, na_rpb, rg_conv, w_rg, b_rg, rg_lambda,
              conf_dw, conf_ln_g, conf_ln_b, q_norm_g, k_norm_g, w_branch, w_out, g_ffn, w_up,
              ffn_dw, w_down, g_final):
    seq = x.shape[1]
    rows = seq // GRID_W
    pos = jnp.arange(seq)
    pos_r = pos // GRID_W
    pos_c = pos % GRID_W
    silu_c = jax.nn.silu(c)
    silu_cc = jax.nn.silu(c_ctx)[None]
    for l in range(DEPTH):
        need_ctx = l < DEPTH - 1
        mod_l = jnp.split(silu_c @ w_ada[l] + b_ada[l], 6, axis=-1)
        mod_c = jnp.split(silu_cc @ w_ada[l] + b_ada[l], 6, axis=-1)
        h_l = modulate(rmsnorm(x, g_mix[l]), mod_l[0], mod_l[1])
        h_c = modulate(rmsnorm(ctx, g_mix[l]), mod_c[0], mod_c[1])
        y_l, y_c = mixer_sublayer(h_l, h_c, w_in[l], na_rpb[l], rg_conv[l], w_rg[l], b_rg[l], rg_lambda[l],
                                  conf_dw[l], conf_ln_g[l], conf_ln_b[l], q_norm_g[l], k_norm_g[l],
                                  w_branch[l], w_out[l], pos_r, pos_c, rows, need_ctx)
        x = x + mod_l[2][:, None, :] * y_l
        x = x + mod_l[5][:, None, :] * conv_ffn(modulate(rmsnorm(x, g_ffn[l]), mod_l[3], mod_l[4]),
                                                w_up[l], ffn_dw[l], w_down[l])
        if need_ctx:
            ctx = ctx + mod_c[2][:, None, :] * y_c
            ctx = ctx + mod_c[5][:, None, :] * conv_ffn(modulate(rmsnorm(ctx, g_ffn[l]), mod_c[3], mod_c[4]),
                                                        w_up[l], ffn_dw[l], w_down[l])
    return rmsnorm(x, g_final)
```

```python
import numpy as np
import concourse.bass as bass
import concourse.mybir as mybir
from concourse.bass_utils import run_bass_kernel_spmd

F32 = mybir.dt.float32
BF16 = mybir.dt.bfloat16
AF = mybir.ActivationFunctionType
ALU = mybir.AluOpType
AX = mybir.AxisListType

NCORES = 8
D = 2048
KC = 16
SEQ = 8192
CTX = 256
GW = 64
DFF = 5632
NIN1 = 4352
EPS = 1e-6
NEG = -30000.0


class Buf:
    __slots__ = ("w", "r", "excl")

    def __init__(self, excl=False):
        self.w = None
        self.r = {}
        self.excl = excl


class Sync:
    ENG = ("pe", "act", "dve", "pool", "sp")

    def __init__(self, nc, n_dma_sems=16):
        self.nc = nc
        self.q = {e: [] for e in self.ENG}
        self.sem = {e: nc.alloc_semaphore(name="S_" + e) for e in self.ENG}
        self.cnt = {e: 0 for e in self.ENG}
        self.waited = {e: {} for e in self.ENG}
        self.dsem = {}
        self.dcnt = {}
        self.drr = {}
        for e in ("sp", "pool", "act"):
            self.dsem[e] = [nc.alloc_semaphore(name="D_%s%d" % (e, i)) for i in range(n_dma_sems)]
            self.drr[e] = 0
            for s in self.dsem[e]:
                self.dcnt[s] = 0

    def _deps(self, reads, writes):
        deps = {}

        def add(s, v):
            if deps.get(s, 0) < v:
                deps[s] = v
        for b in reads:
            if b.w is not None:
                add(*b.w)
        for b in writes:
            if b.w is not None:
                add(*b.w)
            for s, v in b.r.items():
                add(s, v)
        return deps

    def _emit_waits(self, eng, deps):
        q = self.q[eng]
        for s, v in deps.items():
            if eng == "pe" and s is self.sem["pe"]:
                continue
            if self.waited[eng].get(s, 0) >= v:
                continue
            self.waited[eng][s] = v
            q.append(("w", s, v))

    def _record(self, me, reads, writes):
        s, v = me
        for b in reads:
            if b.r.get(s, 0) < v:
                b.r[s] = v
        for b in writes:
            b.w = me
            b.r = {}

    def op(self, eng, fn, reads=(), writes=(), **kw):
        if isinstance(fn, str):
            name = fn
            fn = lambda h, name=name, kw=kw: getattr(h, name)(**kw)
        ex = [b for b in reads if b.excl]
        if ex:
            reads = [b for b in reads if not b.excl]
            writes = list(writes) + ex
        deps = self._deps(reads, writes)
        self._emit_waits(eng, deps)
        self.cnt[eng] += 1
        me = (self.sem[eng], self.cnt[eng])
        self.q[eng].append(("i", fn, self.sem[eng], 1))
        self._record(me, reads, writes)

    def dma(self, eng, out, in_, reads=(), writes=(), **kw):
        sems = self.dsem[eng]
        s = sems[self.drr[eng] % len(sems)]
        self.drr[eng] += 1
        deps = self._deps(reads, writes)
        if self.dcnt[s] > 0 and deps.get(s, 0) < self.dcnt[s]:
            deps[s] = self.dcnt[s]
        self._emit_waits(eng, deps)
        self.dcnt[s] += 16
        me = (s, self.dcnt[s])
        self.q[eng].append(("i", lambda e, o=out, i=in_, k=kw: e.dma_start(out=o, in_=i, **k), s, 16))
        self._record(me, reads, writes)

    def finish(self):
        q = self.q["sp"]
        for s, v in self.dcnt.items():
            if v > 0:
                q.append(("w", s, v))
        for e in ("pe", "act", "dve", "pool"):
            if self.cnt[e] > 0:
                q.append(("w", self.sem[e], self.cnt[e]))
        qs = self.q

        def replay(h, items):
            for it in items:
                if it[0] == "w":
                    h.wait_ge(it[1], it[2])
                else:
                    it[1](h).then_inc(it[2], it[3])

        with self.nc.Block() as block:
            @block.sync
            def _(e):
                replay(e, qs["sp"])

            @block.tensor
            def _(e):
                replay(e, qs["pe"])

            @block.scalar
            def _(e):
                replay(e, qs["act"])

            @block.vector
            def _(e):
                replay(e, qs["dve"])

            @block.gpsimd
            def _(e):
                replay(e, qs["pool"])


class Prog:
    def __init__(self):
        self.nc = bass.Bass("TRN2", target_bir_lowering=False)
        self.S = Sync(self.nc)
        self._n = 0
        self.done = False
        self.psr = 0

    def inp(self, name, shape, dt=F32):
        return self.nc.dram_tensor(name, list(shape), dt, kind="ExternalInput").ap(), Buf()

    def out(self, name, shape, dt=F32):
        return self.nc.dram_tensor(name, list(shape), dt, kind="ExternalOutput").ap(), Buf()

    def sb(self, shape, dt=F32):
        self._n += 1
        return self.nc.alloc_sbuf_tensor("t%d" % self._n, list(shape), dt), Buf()

    def ps(self, shape=(128, 512), dt=F32):
        self._n += 1
        return self.nc.alloc_psum_tensor("p%d" % self._n, list(shape), dt), Buf(excl=True)

    def load(self, dram, shape, dt=F32, eng="sp", view=None):
        t, b = self.sb(shape, dt)
        kw = {"max_dma_last_dim": 4096} if eng == "pool" else {}
        self.S.dma(eng, t[:] if view is None else view(t), dram[0], reads=[dram[1]], writes=[b], **kw)
        return t, b

    def run(self, in_maps):
        global N_LAUNCH
        if not self.done:
            self.S.finish()
            self.done = True
        res = run_bass_kernel_spmd(self.nc, in_maps, core_ids=list(range(NCORES)))
        return res.results


def fm(a, p=128):
    r, t = a.shape
    return np.ascontiguousarray(a.reshape(r // p, p, t).transpose(1, 0, 2))


def unfm(a):
    p, c, t = a.shape
    return a.transpose(1, 0, 2).reshape(c * p, t)


def vec_fm(v, p=128):
    return np.ascontiguousarray(v.reshape(-1, p).T)


def emit_norm_mod(P, xT, bx, hT, bh, blocks, g_in, sc_in, sh_in, ones, bones, psums):
    S = P.S
    g, bg = P.load(g_in, [128, KC])
    sc, bsc = P.load(sc_in, [128, KC, 2])
    sh, bsh = P.load(sh_in, [128, KC, 2])
    gs, bgs = P.sb([128, KC, 2])
    S.op("dve", lambda e: e.tensor_scalar(out=gs[:], in0=sc[:], scalar1=1.0, scalar2=None, op0=ALU.add), reads=[bsc], writes=[bgs])
    for j in range(2):
        S.op("dve", lambda e, j=j: e.tensor_tensor(out=gs[:, :, j], in0=gs[:, :, j], in1=g[:], op=ALU.mult), reads=[bgs, bg], writes=[bgs])
    sq, bsq = P.sb([128, 2, 512])
    bsqs = [bsq, Buf()]
    rstd, brs = P.sb([128, 512])
    tmp, btmp = P.sb([128, 2, 512])
    btmps = [btmp, Buf()]
    for (c0, n, kind) in blocks:
        ps, bps = psums[0]
        for c in range(KC):
            S.op("act", "activation", reads=[bx], writes=[bsqs[c % 2]], out=sq[:, c % 2, :n], in_=xT[:, c, c0:c0 + n], func=AF.Square)
            S.op("pe", "matmul", reads=[bones, bsqs[c % 2]], writes=[bps], out=ps[:, :n], lhsT=ones[:], rhs=sq[:, c % 2, :n], start=(c == 0), stop=(c == KC - 1))
        S.op("act", "activation", reads=[bps, epsb[1]], writes=[brs], out=rstd[:, :n], in_=ps[:, :n], func=AF.Sqrt, bias=epsb[0][:], scale=1.0 / D)
        S.op("dve", "reciprocal", reads=[brs], writes=[brs], out=rstd[:, :n], in_=rstd[:, :n])
        for c in range(KC):
            S.op("dve", "tensor_tensor", reads=[bx, brs], writes=[btmps[c % 2]], out=tmp[:, c % 2, :n], in0=xT[:, c, c0:c0 + n], in1=rstd[:, :n], op=ALU.mult)
            S.op("act", "activation", reads=[btmps[c % 2], bgs, bsh], writes=[bh], out=hT[:, c, c0:c0 + n], in_=tmp[:, c % 2, :n], func=AF.Identity,
                 bias=sh[:, c, kind:kind + 1], scale=gs[:, c, kind:kind + 1])


epsb = [None, None]


def make_consts(P):
    S = P.S
    ones, bones = P.sb([128, 128])
    S.op("dve", lambda e: e.memset(ones[:], 1.0), writes=[bones])
    ep, bep = P.sb([128, 1])
    S.op("dve", lambda e: e.memset(ep[:], EPS), writes=[bep])
    epsb[0], epsb[1] = ep, bep
    return ones, bones


def emit_proj(P, w_in, ncols, col0, hT, bh, kc_n, blocks, evac, psums, wtile=512):
    S = P.S
    wt = [P.sb([128, kc_n, wtile], BF16) for _ in range(2)]
    wv = w_in[0].rearrange("(kc p) f -> p kc f", p=128)
    nt = (ncols + wtile - 1) // wtile
    pi = 0
    for t in range(nt):
        w0 = t * wtile
        wn = min(wtile, ncols - w0)
        wtt, bwt = wt[t % 2]
        S.dma("pool", wtt[:, :, :wn], wv[:, :, col0 + w0:col0 + w0 + wn], reads=[w_in[1]], writes=[bwt])
        for mm in range(wn // 128):
            m = (w0 // 128) + mm
            for bi, (c0, n, kind) in enumerate(blocks):
                ps, bps = psums[pi % len(psums)]
                pi += 1
                for k in range(kc_n):
                    S.op("pe", lambda e, ps=ps, wtt=wtt, k=k, mm=mm, c0=c0, n=n: e.matmul(ps[:, :n], wtt[:, k, mm * 128:(mm + 1) * 128], hT[:, k, c0:c0 + n], start=(k == 0), stop=(k == kc_n - 1)), reads=[bwt, bh], writes=[bps])
                evac(m, bi, ps, bps, c0, n, kind)


def build_k0():
    P = Prog()
    S = P.S
    cc_in = P.inp("cc", [128, KC, 2])
    w_in = P.inp("w", [D, 3072])
    b_in = P.inp("b", [2, 3072])
    o = P.out("mod", [2, 3072])
    cc, bcc = P.load(cc_in, [128, KC, 2])
    sc, bsc = P.sb([128, KC, 2])
    S.op("act", lambda e: e.activation(out=sc[:], in_=cc[:], func=AF.Silu), reads=[bcc], writes=[bsc])
    b2, bb2 = P.load(b_in, [2, 3072])
    res, bres = P.sb([2, 3072])
    wt = [P.sb([128, KC, 512]) for _ in range(2)]
    pss = [P.ps() for _ in range(2)]
    wv = w_in[0].rearrange("(kc p) f -> p kc f", p=128)
    for nb in range(6):
        wtt, bwt = wt[nb % 2]
        S.dma("sp", wtt[:], wv[:, :, nb * 512:(nb + 1) * 512], reads=[w_in[1]], writes=[bwt])
        ps, bps = pss[nb % 2]
        for k in range(KC):
            S.op("pe", lambda e, ps=ps, wtt=wtt, k=k: e.matmul(ps[0:2, :], sc[:, k, :], wtt[:, k, :], start=(k == 0), stop=(k == KC - 1)), reads=[bsc, bwt], writes=[bps])
        S.op("dve", lambda e, ps=ps, nb=nb: e.tensor_tensor(out=res[:, nb * 512:(nb + 1) * 512], in0=ps[0:2, :], in1=b2[:, nb * 512:(nb + 1) * 512], op=ALU.add), reads=[bps, bb2], writes=[bres])
    S.dma("sp", o[0], res[:], reads=[bres])
    return P


def run_k0(P, c, c_ctx, w_ada, b_ada):
    cc = np.stack([vec_fm(c.reshape(-1)), vec_fm(c_ctx.reshape(-1))], axis=-1).astype(np.float32)
    maps = []
    for i in range(NCORES):
        l, c0 = i // 4, (i % 4) * 3072
        maps.append({"cc": cc, "w": np.ascontiguousarray(w_ada[l][:, c0:c0 + 3072]),
                     "b": np.ascontiguousarray(np.broadcast_to(b_ada[l][c0:c0 + 3072], (2, 3072)))})
    res = P.run(maps)
    mod = np.zeros((2, 2, 12288), np.float32)
    for i in range(NCORES):
        l, c0 = i // 4, (i % 4) * 3072
        mod[l][:, c0:c0 + 3072] = res[i]["mod"]
    return mod


def mod_fm(mod_l, j):
    a = mod_l[:, j * D:(j + 1) * D]
    return np.ascontiguousarray(np.stack([vec_fm(a[0]), vec_fm(a[1])], axis=-1))


TK = 1056
BLK1 = [(0, 512, 0), (512, 512, 0), (1024, 32, 1)]


def build_k1():
    P = Prog()
    S = P.S
    x_in = P.inp("xT", [128, KC, TK])
    g_in = P.inp("g", [128, KC])
    sc_in = P.inp("sc", [128, KC, 2])
    sh_in = P.inp("sh", [128, KC, 2])
    w_in = P.inp("w", [D, NIN1])
    z_out = P.out("zT", [NIN1, TK])
    ones, bones = make_consts(P)
    xT, bx = P.load(x_in, [128, KC, TK])
    hT, bh = P.sb([128, KC, TK], BF16)
    psums = [P.ps() for _ in range(4)]
    emit_norm_mod(P, xT, bx, hT, bh, BLK1, g_in, sc_in, sh_in, ones, bones, psums)
    stg = [P.sb([128, 512]) for _ in range(3)]
    cnt = [0]

    def evac(m, bi, ps, bps, c0, n, kind):
        st, bst = stg[cnt[0] % 3]
        cnt[0] += 1
        if cnt[0] % 2:
            S.op("act", lambda e: e.copy(out=st[:, :n], in_=ps[:, :n]), reads=[bps], writes=[bst])
        else:
            S.op("dve", lambda e: e.tensor_copy(out=st[:, :n], in_=ps[:, :n]), reads=[bps], writes=[bst])
        S.dma("sp", z_out[0][m * 128:(m + 1) * 128, c0:c0 + n], st[:, :n], reads=[bst])

    emit_proj(P, w_in, NIN1, 0, hT, bh, KC, BLK1, evac, psums)
    return P


def shard_tokens_T(x_lat, x_ctx):
    outs = []
    for i in range(NCORES):
        a = np.concatenate([x_lat[i * 1024:(i + 1) * 1024], x_ctx[i * 32:(i + 1) * 32]], axis=0)
        outs.append(np.ascontiguousarray(a.T))
    return outs


def unshard_tokens_T(parts):
    lat = np.concatenate([p[:, :1024].T for p in parts], axis=0)
    ctx = np.concatenate([p[:, 1024:].T for p in parts], axis=0)
    return np.ascontiguousarray(lat), np.ascontiguousarray(ctx)


def run_k1(P, x_lat, x_ctx, g, mod_l, w_in_l):
    xs = shard_tokens_T(x_lat, x_ctx)
    gf = vec_fm(g)
    sc, sh = mod_fm(mod_l, 1), mod_fm(mod_l, 0)
    w = np.ascontiguousarray(w_in_l[:, :NIN1])
    maps = [{"xT": fm(xs[i]), "g": gf, "sc": sc, "sh": sh, "w": w} for i in range(NCORES)]
    res = P.run(maps)
    return unshard_tokens_T([r["zT"] for r in res])


NA_TILES = 66
NA_R0 = [0, 2, 60, 124, 126]


def natten_bias(rpb_h):
    out = np.full((5, 128, 576), NEG, np.float32)
    for vi, r0 in enumerate(NA_R0):
        base = min(max(r0 - 4, 0), 119)
        for qr in range(2):
            r = r0 + qr
            rs = min(max(r - 4, 0), 120)
            for qc in range(64):
                cs = min(max(qc - 8, 0), 48)
                kcs = np.arange(cs, cs + 16)
                for kr in range(9):
                    ar = base + kr
                    if rs <= ar < rs + 8:
                        out[vi, qr * 64 + qc, kr * 64 + kcs] = rpb_h[ar - r + 7, kcs - qc + 15]
    return np.ascontiguousarray(out.transpose(1, 0, 2))


def build_k2a():
    P = Prog()
    S = P.S
    q_in = P.inp("qT", [64, 8448])
    k_in = P.inp("kT", [64, 8448])
    val_in = P.inp("val", [128, 64, 64])
    vsh_in = P.inp("vsh", [128, 5, 64])
    vc_in = P.inp("vc", [128, 2, 64])
    bias_in = P.inp("bias", [128, 5, 576])
    id_in = P.inp("ident", [128, 128])
    o_out = P.out("o", [128, NA_TILES, 64])
    qb, bqb = P.load(q_in, [64, 8448], BF16, eng="pool")
    kb, bkb = P.load(k_in, [64, 8448], BF16, eng="pool")
    val, bval = P.load(val_in, [128, 64, 64], BF16, eng="pool")
    vsh, bvsh = P.load(vsh_in, [128, 5, 64], BF16, eng="pool")
    vc, bvc = P.load(vc_in, [128, 2, 64], BF16, eng="pool")
    ident, bid = P.load(id_in, [128, 128], BF16, eng="pool")
    bias, bbias = P.load(bias_in, [128, 5, 576])
    o_all, bo = P.sb([128, NA_TILES, 64])
    psA = [P.ps() for _ in range(2)]
    psB = [P.ps() for _ in range(2)]
    psT = [P.ps([128, 7, 128], BF16) for _ in range(2)]
    psO = [P.ps() for _ in range(2)]
    sbs = [P.sb([128, 832]) for _ in range(2)]
    pbf = [P.sb([128, 832], BF16) for _ in range(2)]
    pTs = [P.sb([128, 7, 128], BF16) for _ in range(2)]
    small = [P.sb([128, 4]) for _ in range(2)]
    for t in range(NA_TILES):
        i = t % 2
        (pa, bpa), (pb, bpb), (pt, bpt), (po, bpo) = psA[i], psB[i], psT[i], psO[i]
        (ss, bss), (pf, bpf), (pT, bpT), (sm, bsm) = sbs[i], pbf[i], pTs[i], small[i]
        local = t < 64
        qs = slice(t * 128, (t + 1) * 128)
        if local:
            r0 = 2 * t
            base = min(max(r0 - 4, 0), 119)
            k0 = base * 64
            var = {0: 0, 2: 1, 124: 3, 126: 4}.get(r0, 2)
            S.op("pe", lambda e, pa=pa, qs=qs, k0=k0: e.matmul(pa[:, 0:512], qb[:, qs], kb[:, k0:k0 + 512], start=True, stop=True), reads=[bqb, bkb], writes=[bpa])
            S.op("pe", lambda e, pb=pb, qs=qs, k0=k0: e.matmul(pb[:, 0:64], qb[:, qs], kb[:, k0 + 512:k0 + 576], start=True, stop=True), reads=[bqb, bkb], writes=[bpb])
        S.op("pe", lambda e, pb=pb, qs=qs: e.matmul(pb[:, 64:320], qb[:, qs], kb[:, 8192:8448], start=True, stop=True), reads=[bqb, bkb], writes=[bpb])
        lo = 0 if local else 576
        if local:
            S.op("dve", lambda e, ss=ss, pa=pa, var=var: e.scalar_tensor_tensor(out=ss[:, 0:512], in0=pa[:, 0:512], scalar=0.125, in1=bias[:, var, 0:512], op0=ALU.mult, op1=ALU.add), reads=[bpa, bbias], writes=[bss])
            S.op("dve", lambda e, ss=ss, pb=pb, var=var: e.scalar_tensor_tensor(out=ss[:, 512:576], in0=pb[:, 0:64], scalar=0.125, in1=bias[:, var, 512:576], op0=ALU.mult, op1=ALU.add), reads=[bpb, bbias], writes=[bss])
        S.op("act", lambda e, ss=ss, pb=pb: e.activation(out=ss[:, 576:832], in_=pb[:, 64:320], func=AF.Identity, scale=0.125), reads=[bpb], writes=[bss])
        S.op("dve", lambda e, ss=ss, sm=sm, lo=lo: e.tensor_reduce(out=sm[:, 0:1], in_=ss[:, lo:832], axis=AX.X, op=ALU.max), reads=[bss], writes=[bsm])
        S.op("dve", lambda e, sm=sm: e.tensor_scalar(out=sm[:, 1:2], in0=sm[:, 0:1], scalar1=-1.0, scalar2=None, op0=ALU.mult), reads=[bsm], writes=[bsm])
        S.op("act", lambda e, ss=ss, pf=pf, sm=sm, lo=lo: e.activation(out=pf[:, lo:832], in_=ss[:, lo:832], func=AF.Exp, bias=sm[:, 1:2], scale=1.0, accum_out=sm[:, 2:3]), reads=[bss, bsm], writes=[bpf, bsm])
        chunks = ([(0, 128), (128, 128), (256, 128), (384, 128), (512, 64)] if local else []) + [(576, 128), (704, 128)]
        jl = [0, 1, 2, 3, 4, 5, 6] if local else [5, 6]
        for j, (c0, w) in zip(jl, chunks):
            S.op("pe", lambda e, pt=pt, pf=pf, j=j, c0=c0, w=w: e.transpose(out=pt[0:w, j, :], in_=pf[:, c0:c0 + w], identity=ident[:]), reads=[bpf, bid], writes=[bpt])
        if local:
            S.op("dve", lambda e, pT=pT, pt=pt: e.tensor_copy(out=pT[:, 0:4, :], in_=pt[:, 0:4, :]), reads=[bpt], writes=[bpT])
            S.op("act", lambda e, pT=pT, pt=pt: e.copy(out=pT[0:64, 4, :], in_=pt[0:64, 4, :]), reads=[bpt], writes=[bpT])
        S.op("act", lambda e, pT=pT, pt=pt: e.copy(out=pT[:, 5:7, :], in_=pt[:, 5:7, :]), reads=[bpt], writes=[bpT])
        mm = []
        if local:
            even = (base % 2 == 0)
            for j in range(4):
                rhs = val[:, k0 // 128 + j, :] if even else vsh[:, j, :]
                mm.append((pT[:, j, :], rhs))
            rhs = val[0:64, k0 // 128 + 4, :] if even else vsh[0:64, 4, :]
            mm.append((pT[0:64, 4, :], rhs))
        mm.append((pT[:, 5, :], vc[:, 0, :]))
        mm.append((pT[:, 6, :], vc[:, 1, :]))
        for n, (l, r) in enumerate(mm):
            S.op("pe", lambda e, po=po, l=l, r=r, n=n, last=len(mm) - 1: e.matmul(po[:, 0:64], l, r, start=(n == 0), stop=(n == last)), reads=[bpT, bval, bvsh, bvc], writes=[bpo])
        S.op("dve", lambda e, sm=sm: e.reciprocal(out=sm[:, 3:4], in_=sm[:, 2:3]), reads=[bsm], writes=[bsm])
        S.op("dve", lambda e, po=po, sm=sm, t=t: e.tensor_scalar(out=o_all[:, t, :], in0=po[:, 0:64], scalar1=sm[:, 3:4], scalar2=None, op0=ALU.mult), reads=[bpo, bsm], writes=[bo])
    S.dma("sp", o_out[0], o_all[:], reads=[bo])
    return P


def tok_chunks(a, p=128):
    t, f = a.shape
    return np.ascontiguousarray(a.reshape(t // p, p, f).transpose(1, 0, 2))


def run_k2a(P, zl, zc, rpb):
    ident = np.eye(128, dtype=np.float32)
    maps = []
    for h in range(NCORES):
        hs = slice(h * 64, (h + 1) * 64)
        q = np.concatenate([zl[:, 0:512][:, hs], zc[:, 0:512][:, hs]], 0)
        k = np.concatenate([zl[:, 512:1024][:, hs], zc[:, 512:1024][:, hs]], 0)
        v = zl[:, 1024:1536][:, hs]
        vcx = zc[:, 1024:1536][:, hs]
        vs = np.zeros((5 * 128, 64), np.float32)
        vs[:576] = v[119 * 64:119 * 64 + 576]
        maps.append({"qT": np.ascontiguousarray(q.T), "kT": np.ascontiguousarray(k.T), "val": tok_chunks(v),
                     "vsh": tok_chunks(vs), "vc": tok_chunks(vcx), "bias": natten_bias(rpb[h]), "ident": ident})
    res = P.run(maps)
    a_lat = np.zeros((SEQ, 512), np.float32)
    a_ctx = np.zeros((CTX, 512), np.float32)
    for h in range(NCORES):
        o = res[h]["o"]
        o = o.transpose(1, 0, 2).reshape(66 * 128, 64)
        a_lat[:, h * 64:(h + 1) * 64] = o[:SEQ]
        a_ctx[:, h * 64:(h + 1) * 64] = o[SEQ:]
    return a_lat, a_ctx


QCOLS = 8448
NCH = 17


def rope_tables():
    inv = (np.float32(10000.0) ** (-np.arange(16, dtype=np.float32) / np.float32(16))).astype(np.float32)
    pos = np.arange(SEQ)
    pr = (pos // GW).astype(np.float32)
    pc = (pos % GW).astype(np.float32)
    cos = np.zeros((64, SEQ), np.float32)
    sin = np.zeros((64, SEQ), np.float32)
    for m in range(64):
        p = pr if m < 32 else pc
        ang = (p * inv[m % 16]).astype(np.float32)
        cos[m] = np.cos(ang).astype(np.float32)
        sin[m] = np.sin(ang).astype(np.float32)
    return cos, sin


def rot_matrix():
    r = np.zeros((64, 64), np.float32)
    for m in range(64):
        if (m % 32) < 16:
            r[m + 16, m] = -1.0
        else:
            r[m - 16, m] = 1.0
    return r


def build_k2d():
    P = Prog()
    S = P.S
    q_in = P.inp("q", [64, QCOLS])
    k_in = P.inp("k", [64, QCOLS])
    cq_in = P.inp("cosq", [64, QCOLS])
    sq_in = P.inp("sinq", [64, QCOLS])
    ck_in = P.inp("cosk", [64, QCOLS])
    sk_in = P.inp("sink", [64, QCOLS])
    v_in = P.inp("vaug", [128, 66, 65])
    gq_in = P.inp("gq", [64, 1])
    gk_in = P.inp("gk", [64, 1])
    grow_in = P.inp("grow", [1, 2, 64])
    rot_in = P.inp("rot", [64, 64])
    o_out = P.out("o", [128, 17, 4, 64])
    gq, bgq = P.load(gq_in, [64, 1])
    gk, bgk = P.load(gk_in, [64, 1])
    grow, bgrow = P.load(grow_in, [1, 2, 64])
    rot, brot = P.load(rot_in, [64, 64])
    vaug, bv = P.load(v_in, [128, 66, 65], BF16, eng="pool")
    ones, bones = P.sb([64, 128])
    S.op("dve", lambda e: e.memset(ones[:], 1.0 / 64.0), writes=[bones])
    one1, bone1 = P.sb([1, 128])
    S.op("dve", lambda e: e.memset(one1[:], 1.0), writes=[bone1])
    ep, bep = P.sb([128, 1])
    S.op("dve", lambda e: e.memset(ep[:], EPS), writes=[bep])
    mm_, bmm = P.sb([1, 4])
    S.op("dve", lambda e: e.tensor_reduce(out=mm_[:, 0:2], in_=grow[:], axis=AX.X, op=ALU.max, apply_absolute_value=True), reads=[bgrow], writes=[bmm])
    S.op("dve", lambda e: e.tensor_tensor(out=mm_[:, 2:3], in0=mm_[:, 0:1], in1=mm_[:, 1:2], op=ALU.mult), reads=[bmm], writes=[bmm])
    S.op("dve", lambda e: e.tensor_scalar(out=mm_[:, 3:4], in0=mm_[:, 2:3], scalar1=-8.0, scalar2=None, op0=ALU.mult), reads=[bmm], writes=[bmm])
    psn = [P.ps() for _ in range(2)]
    negM, bnegM = P.sb([128, 1])
    S.op("pe", lambda e: e.matmul(psn[0][0][:, 0:1], one1[:], mm_[:, 3:4], start=True, stop=True), reads=[bone1, bmm], writes=[psn[0][1]])
    S.op("dve", lambda e: e.tensor_copy(out=negM[:], in_=psn[0][0][:, 0:1]), reads=[psn[0][1]], writes=[bnegM])

    qb, bqb = P.sb([64, QCOLS], BF16)
    kb, bkb = P.sb([64, QCOLS], BF16)
    xin = [P.sb([64, 3, 512]) for _ in range(2)]
    w1 = [P.sb([64, 4, 512]) for _ in range(2)]
    it = 0
    for (src, cs, sn, g, bg, dst, bdst) in ((q_in, cq_in, sq_in, gq, bgq, qb, bqb), (k_in, ck_in, sk_in, gk, bgk, kb, bkb)):
        for ch in range(NCH):
            c0 = ch * 512
            n = min(512, QCOLS - c0)
            (xi, bxi), (w, bw) = xin[it % 2], w1[it % 2]
            (pm, bpm), (pr, bpr) = psn[0], psn[1]
            it += 1
            S.dma("sp", xi[:, 0, :n], src[0][:, c0:c0 + n], reads=[src[1]], writes=[bxi])
            S.dma("sp", xi[:, 1, :n], cs[0][:, c0:c0 + n], reads=[cs[1]], writes=[bxi])
            S.dma("sp", xi[:, 2, :n], sn[0][:, c0:c0 + n], reads=[sn[1]], writes=[bxi])
            S.op("act", lambda e, w=w, xi=xi, n=n: e.activation(out=w[:, 0, :n], in_=xi[:, 0, :n], func=AF.Square), reads=[bxi], writes=[bw])
            S.op("pe", lambda e, pm=pm, w=w, n=n: e.matmul(pm[0:64, :n], ones[:, 0:64], w[:, 0, :n], start=True, stop=True), reads=[bones, bw], writes=[bpm])
            S.op("act", lambda e, pm=pm, w=w, n=n: e.activation(out=w[:, 1, :n], in_=pm[0:64, :n], func=AF.Sqrt, bias=ep[0:64, :], scale=1.0), reads=[bpm, bep], writes=[bw])
            S.op("dve", lambda e, w=w, n=n: e.reciprocal(out=w[:, 1, :n], in_=w[:, 1, :n]), reads=[bw], writes=[bw])
            S.op("dve", lambda e, w=w, xi=xi, n=n, g=g: e.scalar_tensor_tensor(out=w[:, 2, :n], in0=xi[:, 0, :n], scalar=g[:, 0:1], in1=w[:, 1, :n], op0=ALU.mult, op1=ALU.mult), reads=[bxi, bw, bg], writes=[bw])
            S.op("pe", lambda e, pr=pr, w=w, n=n: e.matmul(pr[0:64, :n], rot[:], w[:, 2, :n], start=True, stop=True), reads=[brot, bw], writes=[bpr])
            S.op("dve", lambda e, w=w, xi=xi, n=n: e.tensor_tensor(out=w[:, 3, :n], in0=w[:, 2, :n], in1=xi[:, 1, :n], op=ALU.mult), reads=[bw, bxi], writes=[bw])
            S.op("dve", lambda e, w=w, xi=xi, pr=pr, n=n: e.tensor_tensor(out=w[:, 0, :n], in0=pr[0:64, :n], in1=xi[:, 2, :n], op=ALU.mult), reads=[bpr, bxi], writes=[bw])
            S.op("dve", lambda e, w=w, dst=dst, c0=c0, n=n: e.tensor_tensor(out=dst[:, c0:c0 + n], in0=w[:, 3, :n], in1=w[:, 0, :n], op=ALU.add), reads=[bw], writes=[bdst])

    o_all, bo = P.sb([128, 17, 4, 64])
    S.op("pool", lambda e: e.memset(o_all[:], 0.0), writes=[bo])
    psS = [P.ps() for _ in range(2)]
    psO = [P.ps() for _ in range(4)]
    pTs = [P.sb([128, 512], BF16) for _ in range(3)]
    rv, brv = P.sb([128, 4])
    it = 0
    for t in range(17):
        ntok = 128 if t < 16 else 64
        ncol = 4 * ntok
        c0 = t * 512
        chunks = list(range(66)) if t < 16 else [64, 65]
        for ci, kc in enumerate(chunks):
            (pS, bpS), (pT, bpT) = psS[it % 2], pTs[it % 3]
            it += 1
            S.op("pe", lambda e, pS=pS, kc=kc, c0=c0, ncol=ncol: e.matmul(pS[:, :ncol], kb[:, kc * 128:(kc + 1) * 128], qb[:, c0:c0 + ncol], start=True, stop=True), reads=[bkb, bqb], writes=[bpS])
            S.op("act", lambda e, pS=pS, pT=pT, ncol=ncol: e.activation(out=pT[:, :ncol], in_=pS[:, :ncol], func=AF.Exp, bias=negM[:], scale=0.125), reads=[bpS, bnegM], writes=[bpT])
            for h in range(4):
                po, bpo = psO[h]
                S.op("pe", lambda e, po=po, pT=pT, h=h, ntok=ntok, kc=kc, ci=ci, last=len(chunks) - 1: e.matmul(po[0:ntok, 0:65], pT[:, h * ntok:(h + 1) * ntok], vaug[:, kc, :], start=(ci == 0), stop=(ci == last)), reads=[bpT, bv], writes=[bpo])
        for h in range(4):
            po, bpo = psO[h]
            S.op("dve", lambda e, po=po, h=h, ntok=ntok: e.reciprocal(out=rv[0:ntok, h:h + 1], in_=po[0:ntok, 64:65]), reads=[bpo], writes=[brv])
            S.op("dve", lambda e, po=po, h=h, ntok=ntok, t=t: e.tensor_scalar(out=o_all[0:ntok, t, h, :], in0=po[0:ntok, 0:64], scalar1=rv[0:ntok, h:h + 1], scalar2=None, op0=ALU.mult), reads=[bpo, brv], writes=[bo])
    S.dma("sp", o_out[0], o_all[:], reads=[bo])
    return P


def run_k2d(P, zl, zc, gq, gk):
    cos, sin = rope_tables()
    rot = rot_matrix()
    maps = []
    for core in range(NCORES):
        g, qt = core // 4, core % 4
        ql = zl[qt * 2048:(qt + 1) * 2048, 3584:4096].reshape(16, 128, 8, 64)[:, :, 4 * g:4 * g + 4, :]
        qlT = ql.transpose(3, 0, 2, 1).reshape(64, 16 * 4 * 128)
        qc = zc[qt * 64:(qt + 1) * 64, 3584:4096].reshape(64, 8, 64)[:, 4 * g:4 * g + 4, :]
        qcT = qc.transpose(2, 1, 0).reshape(64, 256)
        q = np.ascontiguousarray(np.concatenate([qlT, qcT], 1))
        cl = cos[:, qt * 2048:(qt + 1) * 2048].reshape(64, 16, 1, 128)
        sl = sin[:, qt * 2048:(qt + 1) * 2048].reshape(64, 16, 1, 128)
        cosq = np.concatenate([np.broadcast_to(cl, (64, 16, 4, 128)).reshape(64, 8192), np.ones((64, 256), np.float32)], 1)
        sinq = np.concatenate([np.broadcast_to(sl, (64, 16, 4, 128)).reshape(64, 8192), np.zeros((64, 256), np.float32)], 1)
        k = np.concatenate([zl[:, 4096:4224][:, g * 64:(g + 1) * 64], zc[:, 4096:4224][:, g * 64:(g + 1) * 64]], 0)
        cosk = np.concatenate([cos, np.ones((64, 256), np.float32)], 1)
        sink = np.concatenate([sin, np.zeros((64, 256), np.float32)], 1)
        v = np.concatenate([zl[:, 4224:4352][:, g * 64:(g + 1) * 64], zc[:, 4224:4352][:, g * 64:(g + 1) * 64]], 0)
        vaug = np.concatenate([v, np.ones((8448, 1), np.float32)], 1)
        maps.append({"q": q, "k": np.ascontiguousarray(k.T), "cosq": np.ascontiguousarray(cosq), "sinq": np.ascontiguousarray(sinq),
                     "cosk": np.ascontiguousarray(cosk), "sink": np.ascontiguousarray(sink), "vaug": tok_chunks(vaug),
                     "gq": np.ascontiguousarray(gq.reshape(64, 1)), "gk": np.ascontiguousarray(gk.reshape(64, 1)),
                     "grow": np.ascontiguousarray(np.stack([gq, gk])[None]), "rot": rot})
    res = P.run(maps)
    d_lat = np.zeros((SEQ, 512), np.float32)
    d_ctx = np.zeros((CTX, 512), np.float32)
    for core in range(NCORES):
        g, qt = core // 4, core % 4
        o = res[core]["o"]
        lat = o[:, :16].transpose(1, 0, 2, 3).reshape(2048, 256)
        d_lat[qt * 2048:(qt + 1) * 2048, g * 256:(g + 1) * 256] = lat
        d_ctx[qt * 64:(qt + 1) * 64, g * 256:(g + 1) * 256] = o[:64, 16].reshape(64, 256)
    return d_lat, d_ctx


TALL = CTX + SEQ
PADW = TALL + 6


def rev(ap):
    return ap[:, ::-1]


def build_k2b():
    P = Prog()
    S = P.S
    x_in = P.inp("x3", [64, TALL])
    z_in = P.inp("z4", [64, TALL])
    tap_in = P.inp("taps", [64, 4])
    w_in = P.inp("wrg", [64, 4, 64])
    b_in = P.inp("brg", [64, 4])
    lam_in = P.inp("lam", [64, 2])
    o_out = P.out("o", [64, TALL])
    taps, btaps = P.load(tap_in, [64, 4])
    wrg, bwrg = P.load(w_in, [64, 4, 64], BF16, eng="pool")
    brg, bbrg = P.load(b_in, [64, 4])
    lam, blam = P.load(lam_in, [64, 2])
    T1, bT1 = P.sb([64, PADW])
    XC, bXC = P.sb([64, TALL])
    A, bA = P.sb([64, TALL])
    H = [P.sb([64, TALL]) for _ in range(2)]
    S.op("pool", "memset", writes=[bT1], ap=T1[:], constant=0.0)
    S.dma("sp", T1[:, 2:258], x_in[0][:, 0:256], reads=[x_in[1]], writes=[bT1])
    S.dma("sp", T1[:, 261:261 + SEQ], x_in[0][:, 256:TALL], reads=[x_in[1]], writes=[bT1])
    cd, bcd = P.sb([64, 2])
    S.op("act", "activation", reads=[blam], writes=[bcd], out=cd[:], in_=lam[:], func=AF.Exp, scale=-1.0)
    S.op("act", "activation", reads=[bcd], writes=[bcd], out=cd[:], in_=cd[:], func=AF.Ln, bias=1.0, scale=1.0)
    S.op("dve", "tensor_scalar", reads=[bcd], writes=[bcd], out=cd[:], in0=cd[:], scalar1=-8.0, scalar2=None, op0=ALU.mult)
    for (o0, n, s0) in ((0, CTX, 0), (CTX, SEQ, 259)):
        S.op("dve", "tensor_scalar", reads=[bT1, btaps], writes=[bXC], out=XC[:, o0:o0 + n], in0=T1[:, s0:s0 + n], scalar1=taps[:, 0:1], scalar2=None, op0=ALU.mult)
        for j in range(1, 4):
            S.op("dve", "scalar_tensor_tensor", reads=[bT1, btaps, bXC], writes=[bXC], out=XC[:, o0:o0 + n], in0=T1[:, s0 + j:s0 + j + n],
                 scalar=taps[:, j:j + 1], in1=XC[:, o0:o0 + n], op0=ALU.mult, op1=ALU.add)
    psr = [P.ps() for _ in range(2)]
    psi = [P.ps() for _ in range(2)]
    xb = [P.sb([64, 512], BF16) for _ in range(2)]
    wk = [P.sb([64, 4, 512]) for _ in range(2)]
    it = 0
    for d in range(2):
        Hd, bH = H[d]
        for ch in range(17):
            c0 = ch * 512
            n = min(512, TALL - c0)
            (xbb, bxb), (w, bw), (pr, bpr), (pi_, bpi) = xb[it % 2], wk[it % 2], psr[it % 2], psi[it % 2]
            it += 1
            S.op("act", "copy", reads=[bXC], writes=[bxb], out=xbb[:, :n], in_=XC[:, c0:c0 + n])
            S.op("pe", "matmul", reads=[bwrg, bxb], writes=[bpr], out=pr[0:64, :n], lhsT=wrg[:, 2 * d, :], rhs=xbb[:, :n], start=True, stop=True)
            S.op("pe", "matmul", reads=[bwrg, bxb], writes=[bpi], out=pi_[0:64, :n], lhsT=wrg[:, 2 * d + 1, :], rhs=xbb[:, :n], start=True, stop=True)
            S.op("act", "activation", reads=[bpr, bbrg], writes=[bw], out=w[:, 0, :n], in_=pr[0:64, :n], func=AF.Sigmoid, bias=brg[:, 2 * d:2 * d + 1], scale=1.0)
            S.op("act", "activation", reads=[bpi, bbrg], writes=[bw], out=w[:, 1, :n], in_=pi_[0:64, :n], func=AF.Sigmoid, bias=brg[:, 2 * d + 1:2 * d + 2], scale=1.0)
            S.op("act", "activation", reads=[bw, bcd], writes=[bA], out=A[:, c0:c0 + n], in_=w[:, 0, :n], func=AF.Exp, scale=cd[:, d:d + 1])
            S.op("dve", "tensor_tensor", reads=[bA], writes=[bw], out=w[:, 2, :n], in0=A[:, c0:c0 + n], in1=A[:, c0:c0 + n], op=ALU.mult)
            S.op("dve", "tensor_scalar", reads=[bw], writes=[bw], out=w[:, 2, :n], in0=w[:, 2, :n], scalar1=-1.0, scalar2=1.0, op0=ALU.mult, op1=ALU.add)
            S.op("act", "activation", reads=[bw], writes=[bw], out=w[:, 3, :n], in_=w[:, 2, :n], func=AF.Sqrt)
            S.op("dve", "tensor_tensor", reads=[bw], writes=[bw], out=w[:, 3, :n], in0=w[:, 3, :n], in1=w[:, 1, :n], op=ALU.mult)
            S.op("dve", "tensor_tensor", reads=[bw, bXC], writes=[bH], out=Hd[:, c0:c0 + n], in0=w[:, 3, :n], in1=XC[:, c0:c0 + n], op=ALU.mult)
        if d == 0:
            S.op("dve", "tensor_tensor_scan", reads=[bA, bH], writes=[bH], out=Hd[:, 0:CTX], data0=A[:, 0:CTX], data1=Hd[:, 0:CTX], initial=0.0, op0=ALU.mult, op1=ALU.add)
            for k in range(4):
                a0 = CTX + k * 2048
                S.op("dve", "tensor_tensor_scan", reads=[bA, bH], writes=[bH], out=Hd[:, a0:a0 + 2048], data0=A[:, a0:a0 + 2048], data1=Hd[:, a0:a0 + 2048],
                     initial=Hd[:, a0 - 1:a0], op0=ALU.mult, op1=ALU.add)
        else:
            S.op("dve", "tensor_tensor_scan", reads=[bA, bH], writes=[bH], out=rev(Hd[:, 0:CTX]), data0=rev(A[:, 0:CTX]), data1=rev(Hd[:, 0:CTX]), initial=0.0, op0=ALU.mult, op1=ALU.add)
            for k in range(4):
                a0 = TALL - (k + 1) * 2048
                init = Hd[:, 0:1] if k == 0 else Hd[:, a0 + 2048:a0 + 2049]
                S.op("dve", "tensor_tensor_scan", reads=[bA, bH], writes=[bH], out=rev(Hd[:, a0:a0 + 2048]), data0=rev(A[:, a0:a0 + 2048]), data1=rev(Hd[:, a0:a0 + 2048]),
                     initial=init, op0=ALU.mult, op1=ALU.add)
    (Hf, bHf), (Hb, bHb) = H
    S.op("dve", "tensor_tensor", reads=[bHf, bHb], writes=[bHf], out=Hf[:], in0=Hf[:], in1=Hb[:], op=ALU.add)
    Z = T1
    S.dma("sp", Z[:, 0:TALL], z_in[0], reads=[z_in[1]], writes=[bT1])
    for c0 in range(0, TALL, 2112):
        n = 2112
        sl = slice(c0, c0 + n)
        S.op("dve", "tensor_tensor", reads=[bT1], writes=[bA], out=A[:, sl], in0=Z[:, sl], in1=Z[:, sl], op=ALU.mult)
        S.op("dve", "tensor_scalar", reads=[bA], writes=[bA], out=A[:, sl], in0=A[:, sl], scalar1=0.044715, scalar2=1.0, op0=ALU.mult, op1=ALU.add)
        S.op("dve", "tensor_tensor", reads=[bA, bT1], writes=[bA], out=A[:, sl], in0=A[:, sl], in1=Z[:, sl], op=ALU.mult)
        S.op("act", "activation", reads=[bA], writes=[bA], out=A[:, sl], in_=A[:, sl], func=AF.Sigmoid, scale=1.5957691216057308)
        S.op("dve", "tensor_tensor", reads=[bA, bT1], writes=[bA], out=A[:, sl], in0=A[:, sl], in1=Z[:, sl], op=ALU.mult)
        S.op("dve", "tensor_tensor", reads=[bA, bHf], writes=[bHb], out=Hb[:, sl], in0=A[:, sl], in1=Hf[:, sl], op=ALU.mult)
    S.dma("sp", o_out[0], Hb[:], reads=[bHb])
    return P


def run_k2b(P, zl, zc, rg_conv, w_rg, b_rg, rg_lambda):
    maps = []
    for n in range(NCORES):
        cs = slice(n * 64, (n + 1) * 64)
        x3 = np.concatenate([zc[:, 1536:2048][:, cs], zl[:, 1536:2048][:, cs]], 0).T
        z4 = np.concatenate([zc[:, 2048:2560][:, cs], zl[:, 2048:2560][:, cs]], 0).T
        wr = w_rg[:, :, n].reshape(4, 64, 64).transpose(1, 0, 2)
        maps.append({"x3": np.ascontiguousarray(x3), "z4": np.ascontiguousarray(z4), "taps": np.ascontiguousarray(rg_conv[:, cs].T),
                     "wrg": np.ascontiguousarray(wr), "brg": np.ascontiguousarray(b_rg.reshape(4, 512)[:, cs].T),
                     "lam": np.ascontiguousarray(rg_lambda[:, cs].T)})
    res = P.run(maps)
    o = np.concatenate([r["o"] for r in res], 0).T
    return np.ascontiguousarray(o[CTX:]), np.ascontiguousarray(o[:CTX])


CW = 1116
CWO = CW - 30


def build_k2c():
    P = Prog()
    S = P.S
    v_in = P.inp("val", [128, 4, CW])
    g_in = P.inp("gt", [128, 4, CW])
    cw_in = P.inp("cw", [128, 4, 31])
    lg_in = P.inp("lng", [128, 4])
    lb_in = P.inp("lnb", [128, 4])
    o_out = P.out("o", [128, 4, CWO])
    val, bval = P.load(v_in, [128, 4, CW])
    gt, bgt = P.load(g_in, [128, 4, CW])
    cw, bcw = P.load(cw_in, [128, 4, 31])
    lng, blg = P.load(lg_in, [128, 4])
    lnb, blb = P.load(lb_in, [128, 4])
    ones, bones = P.sb([128, 128])
    S.op("dve", "memset", writes=[bones], ap=ones[:], constant=1.0 / 512.0)
    ep, bep = P.sb([128, 1])
    S.op("dve", "memset", writes=[bep], ap=ep[:], constant=EPS)
    S.op("act", "activation", reads=[bgt], writes=[bgt], out=gt[:], in_=gt[:], func=AF.Sigmoid)
    Y = [P.sb([128, CW]) for _ in range(4)]
    O = [P.sb([128, CWO]) for _ in range(4)]
    for c in range(4):
        eng = "dve"
        S.op(eng, "tensor_tensor", reads=[bval, bgt], writes=[Y[c][1]], out=Y[c][0][:], in0=val[:, c, :], in1=gt[:, c, :], op=ALU.mult)
        S.op(eng, "tensor_scalar", reads=[Y[c][1], bcw], writes=[O[c][1]], out=O[c][0][:], in0=Y[c][0][:, 0:CWO], scalar1=cw[:, c, 0:1], scalar2=None, op0=ALU.mult)
        for k in range(1, 31):
            S.op(eng, "scalar_tensor_tensor", reads=[Y[c][1], bcw, O[c][1]], writes=[O[c][1]], out=O[c][0][:], in0=Y[c][0][:, k:k + CWO],
                 scalar=cw[:, c, k:k + 1], in1=O[c][0][:], op0=ALU.mult, op1=ALU.add)
    psm, bpsm = P.ps()
    psq, bpsq = P.ps()
    sq = [P.sb([128, 512]) for _ in range(2)]
    st, bst = P.sb([128, 4, 512])
    res, bres = P.sb([128, 4, CWO])
    for (c0, n) in ((0, 512), (512, 512), (1024, CWO - 1024)):
        for c in range(4):
            S.op("pe", "matmul", reads=[bones, O[c][1]], writes=[bpsm], out=psm[:, :n], lhsT=ones[:], rhs=O[c][0][:, c0:c0 + n], start=(c == 0), stop=(c == 3))
        for c in range(4):
            s_, bs_ = sq[c % 2]
            S.op("act", "activation", reads=[O[c][1]], writes=[bs_], out=s_[:, :n], in_=O[c][0][:, c0:c0 + n], func=AF.Square)
            S.op("pe", "matmul", reads=[bones, bs_], writes=[bpsq], out=psq[:, :n], lhsT=ones[:], rhs=s_[:, :n], start=(c == 0), stop=(c == 3))
        S.op("act", "copy", reads=[bpsm], writes=[bst], out=st[:, 0, :n], in_=psm[:, :n])
        S.op("act", "activation", reads=[bst], writes=[bst], out=st[:, 1, :n], in_=st[:, 0, :n], func=AF.Square)
        S.op("dve", "tensor_tensor", reads=[bpsq, bst], writes=[bst], out=st[:, 2, :n], in0=psq[:, :n], in1=st[:, 1, :n], op=ALU.subtract)
        S.op("act", "activation", reads=[bst, bep], writes=[bst], out=st[:, 2, :n], in_=st[:, 2, :n], func=AF.Sqrt, bias=ep[:], scale=1.0)
        S.op("dve", "reciprocal", reads=[bst], writes=[bst], out=st[:, 2, :n], in_=st[:, 2, :n])
        for c in range(4):
            s_, bs_ = sq[c % 2]
            S.op("dve", "tensor_tensor", reads=[O[c][1], bst], writes=[bs_], out=s_[:, :n], in0=O[c][0][:, c0:c0 + n], in1=st[:, 0, :n], op=ALU.subtract)
            S.op("dve", "tensor_tensor", reads=[bs_, bst], writes=[bs_], out=s_[:, :n], in0=s_[:, :n], in1=st[:, 2, :n], op=ALU.mult)
            S.op("act", "activation", reads=[bs_, blg, blb], writes=[bres], out=res[:, c, c0:c0 + n], in_=s_[:, :n], func=AF.Silu, bias=lnb[:, c:c + 1], scale=lng[:, c:c + 1])
    S.dma("sp", o_out[0], res[:], reads=[bres])
    return P


def halo_rows(a, s, n, h):
    out = np.zeros((n + 2 * h, a.shape[1]), a.dtype)
    lo, hi = max(s - h, 0), min(s + n + h, a.shape[0])
    out[lo - (s - h):hi - (s - h)] = a[lo:hi]
    return out


def run_k2c(P, zl, zc, conf_dw, ln_g, ln_b):
    maps = []
    cwf = np.ascontiguousarray(conf_dw.T.reshape(4, 128, 31).transpose(1, 0, 2))
    for i in range(NCORES):
        z = np.concatenate([halo_rows(zl[:, 2560:3584], i * 1024, 1024, 15), halo_rows(zc[:, 2560:3584], i * 32, 32, 15)], 0)
        maps.append({"val": fm(np.ascontiguousarray(z[:, :512].T)), "gt": fm(np.ascontiguousarray(z[:, 512:].T)), "cw": cwf,
                     "lng": vec_fm(ln_g), "lnb": vec_fm(ln_b)})
    res = P.run(maps)
    lat, ctx = [], []
    for i in range(NCORES):
        o = unfm(res[i]["o"]).T
        lat.append(o[:1024])
        ctx.append(o[1054:1086])
    return np.concatenate(lat, 0), np.concatenate(ctx, 0)


T3 = 528
BLK3 = [(0, 512, 0), (512, 16, 1)]


def build_k3():
    P = Prog()
    S = P.S
    x_in = P.inp("xT", [128, KC, T3])
    g_in = P.inp("g", [128, KC])
    sc_in = P.inp("sc", [128, KC, 2])
    sh_in = P.inp("sh", [128, KC, 2])
    ga_in = P.inp("gate", [128, KC, 2])
    br_in = P.inp("brT", [128, 4, 4, T3])
    wg_in = P.inp("wg", [D, 4 * D])
    wb_in = P.inp("wb", [4 * 512, D])
    wo_in = P.inp("wo", [D, D])
    o_out = P.out("xmT", [D, T3])
    ones, bones = make_consts(P)
    xT, bx = P.load(x_in, [128, KC, T3])
    ga, bga = P.load(ga_in, [128, KC, 2])
    brT, bbr = P.load(br_in, [128, 4, 4, T3], BF16, eng="pool")
    hT, bh = P.sb([128, KC, T3], BF16)
    psums = [P.ps() for _ in range(4)]
    emit_norm_mod(P, xT, bx, hT, bh, BLK3, g_in, sc_in, sh_in, ones, bones, psums)
    yT, byT = P.sb([128, KC, T3], BF16)
    acc = [P.sb([128, T3]) for _ in range(4)]
    wgt = [P.sb([128, KC, 512], BF16) for _ in range(2)]
    wbt = [P.sb([128, 4, 512], BF16) for _ in range(2)]
    sg = [P.sb([128, 512]) for _ in range(2)]
    psb = [P.ps() for _ in range(2)]
    wgv = wg_in[0].rearrange("(kc p) f -> p kc f", p=128)
    wbv = wb_in[0].rearrange("(n kc p) f -> p n kc f", p=128, kc=4)
    it = 0
    pi = 0
    for fg in range(4):
        for n in range(4):
            (wgg, bwg), (wbb, bwb) = wgt[it % 2], wbt[it % 2]
            it += 1
            S.dma("pool", wgg[:], wgv[:, :, n * D + fg * 512:n * D + (fg + 1) * 512], reads=[wg_in[1]], writes=[bwg])
            S.dma("pool", wbb[:], wbv[:, n, :, fg * 512:(fg + 1) * 512], reads=[wb_in[1]], writes=[bwb])
            for (c0, nn, kind) in BLK3:
                for f in range(4):
                    (pg, bpg), (pb, bpb), (sgg, bsg) = psums[pi % 4], psb[pi % 2], sg[pi % 2]
                    pi += 1
                    for k in range(KC):
                        S.op("pe", "matmul", reads=[bwg, bh], writes=[bpg], out=pg[:, :nn], lhsT=wgg[:, k, f * 128:(f + 1) * 128], rhs=hT[:, k, c0:c0 + nn], start=(k == 0), stop=(k == KC - 1))
                    for k in range(4):
                        S.op("pe", "matmul", reads=[bwb, bbr], writes=[bpb], out=pb[:, :nn], lhsT=wbb[:, k, f * 128:(f + 1) * 128], rhs=brT[:, n, k, c0:c0 + nn], start=(k == 0), stop=(k == 3))
                    S.op("act", "activation", reads=[bpg], writes=[bsg], out=sgg[:, :nn], in_=pg[:, :nn], func=AF.Sigmoid)
                    a_, ba_ = acc[f]
                    if n == 0:
                        S.op("dve", "tensor_tensor", reads=[bsg, bpb], writes=[ba_], out=a_[:, c0:c0 + nn], in0=sgg[:, :nn], in1=pb[:, :nn], op=ALU.mult)
                    else:
                        S.op("dve", "tensor_tensor", reads=[bsg, bpb], writes=[bsg], out=sgg[:, :nn], in0=sgg[:, :nn], in1=pb[:, :nn], op=ALU.mult)
                        S.op("pool", "tensor_tensor", reads=[bsg, ba_], writes=[ba_], out=a_[:, c0:c0 + nn], in0=a_[:, c0:c0 + nn], in1=sgg[:, :nn], op=ALU.add)
        for f in range(4):
            S.op("act", "copy", reads=[acc[f][1]], writes=[byT], out=yT[:, fg * 4 + f, :], in_=acc[f][0][:])
    stg = [P.sb([128, 512]) for _ in range(3)]
    cnt = [0]

    def evac(m, bi, ps, bps, c0, n, kind):
        st, bst = stg[cnt[0] % 3]
        cnt[0] += 1
        S.op("dve", "scalar_tensor_tensor", reads=[bps, bga, bx], writes=[bst], out=st[:, :n], in0=ps[:, :n], scalar=ga[:, m, kind:kind + 1], in1=xT[:, m, c0:c0 + n], op0=ALU.mult, op1=ALU.add)
        S.dma("sp", o_out[0][m * 128:(m + 1) * 128, c0:c0 + n], st[:, :n], reads=[bst])

    emit_proj(P, wo_in, D, 0, yT, byT, KC, BLK3, evac, psums)
    return P


def run_k3(P, x_lat, x_ctx, br_lat, br_ctx, g, mod_l, w_in_l, w_branch_l, w_out_l):
    gf = vec_fm(g)
    sc, sh, ga = mod_fm(mod_l, 1), mod_fm(mod_l, 0), mod_fm(mod_l, 2)
    wg = np.ascontiguousarray(w_in_l[:, NIN1:])
    wb = np.ascontiguousarray(w_branch_l.reshape(4 * 512, D))
    lat_out = np.zeros((SEQ, D), np.float32)
    ctx_out = np.zeros((CTX, D), np.float32)
    for hf in range(2):
        maps = []
        for i in range(NCORES):
            ls = slice(i * 1024 + hf * 512, i * 1024 + hf * 512 + 512)
            cs = slice(i * 32 + hf * 16, i * 32 + hf * 16 + 16)
            xs = np.concatenate([x_lat[ls], x_ctx[cs]], 0).T
            br = np.stack([np.concatenate([br_lat[n][ls], br_ctx[n][cs]], 0).T for n in range(4)], 0)
            brf = np.ascontiguousarray(br.reshape(4, 4, 128, T3).transpose(2, 0, 1, 3))
            maps.append({"xT": fm(np.ascontiguousarray(xs)), "g": gf, "sc": sc, "sh": sh, "gate": ga, "brT": brf, "wg": wg, "wb": wb, "wo": w_out_l})
        res = P.run(maps)
        for i in range(NCORES):
            o = res[i]["xmT"].T
            lat_out[i * 1024 + hf * 512:i * 1024 + hf * 512 + 512] = o[:512]
            ctx_out[i * 32 + hf * 16:i * 32 + hf * 16 + 16] = o[512:]
    return lat_out, ctx_out


PIECES = [(0, 342, 0), (342, 341, 0), (683, 341, 0), (0, 32, 1)]
PW = [n + 2 for (_, n, _) in PIECES]
PC0 = [sum(PW[:i]) for i in range(len(PW))]
T4 = sum(PW)
WMAX = max(PW)


def build_k4():
    P = Prog()
    S = P.S
    x_in = P.inp("xT", [128, KC, T4])
    g_in = P.inp("g", [128, KC])
    sc_in = P.inp("sc", [128, KC, 2])
    sh_in = P.inp("sh", [128, KC, 2])
    ga_in = P.inp("gate", [128, KC, 2])
    tap_in = P.inp("taps", [128, 88, 3])
    mask_in = P.inp("mask", [128, T4])
    wu_in = P.inp("wu", [D, 2 * DFF])
    wd_in = P.inp("wd", [DFF, D])
    o_out = P.out("xoT", [D, T4])
    ones, bones = make_consts(P)
    xT, bx = P.load(x_in, [128, KC, T4])
    ga, bga = P.load(ga_in, [128, KC, 2])
    taps, btaps = P.load(tap_in, [128, 88, 3])
    hT, bh = P.sb([128, KC, T4], BF16)
    psums = [P.ps() for _ in range(4)]
    blocks = [(PC0[i], PW[i], PIECES[i][2]) for i in range(len(PIECES))]
    emit_norm_mod(P, xT, bx, hT, bh, blocks, g_in, sc_in, sh_in, ones, bones, psums)
    mask, bmask = P.load(mask_in, [128, T4])
    for c in range(KC):
        S.op("dve", "tensor_tensor", reads=[bh, bmask], writes=[bh], out=hT[:, c, :], in0=hT[:, c, :], in1=mask[:], op=ALU.mult)
    actT, bact = P.sb([128, 44, WMAX], BF16)
    wut = [P.sb([128, KC, 256], BF16) for _ in range(4)]
    wdt = [P.sb([128, 11, 256], BF16) for _ in range(2)]
    psf = [P.ps() for _ in range(4)]
    uv = [P.sb([128, WMAX]) for _ in range(2)]
    cg = [P.sb([128, WMAX]) for _ in range(2)]
    cv = [P.sb([128, WMAX]) for _ in range(2)]
    stg = [P.sb([128, WMAX]) for _ in range(3)]
    tvs = [P.sb([128, WMAX]) for _ in range(2)]
    wuv = wu_in[0].rearrange("(kc p) f -> p kc f", p=128)
    wdv = wd_in[0].rearrange("(m p) f -> p m f", p=128)
    it = 0
    pi = 0
    sc_ = 0
    for (c0, W, kind) in blocks:
        S.op("pool", "memset", writes=[bact], ap=actT[:], constant=0.0)
        n = W - 2
        for mg in range(22):
            (wg_, bwg), (wv_, bwv) = wut[(it % 2) * 2], wut[(it % 2) * 2 + 1]
            it += 1
            S.dma("pool", wg_[:], wuv[:, :, mg * 256:(mg + 1) * 256], reads=[wu_in[1]], writes=[bwg])
            S.dma("pool", wv_[:], wuv[:, :, DFF + mg * 256:DFF + (mg + 1) * 256], reads=[wu_in[1]], writes=[bwv])
            for mm in range(2):
                m = mg * 2 + mm
                (pg, bpg), (pv, bpv) = psums[(pi % 2) * 2], psums[(pi % 2) * 2 + 1]
                (u_, bu), (g_, bg_), (v_, bv_), (tv_, btv) = uv[pi % 2], cg[pi % 2], cv[pi % 2], tvs[pi % 2]
                pi += 1
                for k in range(KC):
                    S.op("pe", "matmul", reads=[bwg, bh], writes=[bpg], out=pg[:, :W], lhsT=wg_[:, k, mm * 128:(mm + 1) * 128], rhs=hT[:, k, c0:c0 + W], start=(k == 0), stop=(k == KC - 1))
                for k in range(KC):
                    S.op("pe", "matmul", reads=[bwv, bh], writes=[bpv], out=pv[:, :W], lhsT=wv_[:, k, mm * 128:(mm + 1) * 128], rhs=hT[:, k, c0:c0 + W], start=(k == 0), stop=(k == KC - 1))
                S.op("act", "copy", reads=[bpv], writes=[bu], out=u_[:, :W], in_=pv[:, :W])
                S.op("dve", "tensor_scalar", reads=[bpg, btaps], writes=[bg_], out=g_[:, :n], in0=pg[:, 0:n], scalar1=taps[:, m, 0:1], scalar2=None, op0=ALU.mult)
                for t in (1, 2):
                    S.op("dve", "scalar_tensor_tensor", reads=[bpg, btaps, bg_], writes=[bg_], out=g_[:, :n], in0=pg[:, t:t + n], scalar=taps[:, m, t:t + 1], in1=g_[:, :n], op0=ALU.mult, op1=ALU.add)
                S.op("pool", "tensor_scalar", reads=[bu, btaps], writes=[bv_], out=v_[:, :n], in0=u_[:, 0:n], scalar1=taps[:, 44 + m, 0:1], scalar2=None, op0=ALU.mult)
                for t in (1, 2):
                    S.op("pool", "tensor_scalar", reads=[bu, btaps], writes=[btv], out=tv_[:, :n], in0=u_[:, t:t + n], scalar1=taps[:, 44 + m, t:t + 1], scalar2=None, op0=ALU.mult)
                    S.op("pool", "tensor_tensor", reads=[btv, bv_], writes=[bv_], out=v_[:, :n], in0=v_[:, :n], in1=tv_[:, :n], op=ALU.add)
                S.op("act", "activation", reads=[bg_], writes=[bg_], out=g_[:, :n], in_=g_[:, :n], func=AF.Silu)
                S.op("dve", "tensor_tensor", reads=[bg_, bv_], writes=[bact], out=actT[:, m, 1:1 + n], in0=g_[:, :n], in1=v_[:, :n], op=ALU.mult)
        for fg in range(8):
            for kg in range(4):
                wd_, bwd = wdt[it % 2]
                it += 1
                S.dma("pool", wd_[:], wdv[:, kg * 11:(kg + 1) * 11, fg * 256:(fg + 1) * 256], reads=[wd_in[1]], writes=[bwd])
                for f in range(2):
                    pf, bpf = psf[f]
                    for mi in range(11):
                        S.op("pe", "matmul", reads=[bwd, bact], writes=[bpf], out=pf[:, :W], lhsT=wd_[:, mi, f * 128:(f + 1) * 128], rhs=actT[:, kg * 11 + mi, 0:W],
                             start=(kg == 0 and mi == 0), stop=(kg == 3 and mi == 10))
            for f in range(2):
                pf, bpf = psf[f]
                st, bst = stg[sc_ % 3]
                sc_ += 1
                mfe = fg * 2 + f
                S.op("dve", "scalar_tensor_tensor", reads=[bpf, bga, bx], writes=[bst], out=st[:, :W], in0=pf[:, :W], scalar=ga[:, mfe, kind:kind + 1], in1=xT[:, mfe, c0:c0 + W], op0=ALU.mult, op1=ALU.add)
                S.dma("sp", o_out[0][mfe * 128:(mfe + 1) * 128, c0:c0 + W], st[:, :W], reads=[bst])
    return P


def run_k4(P, x_lat, x_ctx, g, mod_l, w_up_l, ffn_dw_l, w_down_l):
    gf = vec_fm(g)
    sc, sh, ga = mod_fm(mod_l, 4), mod_fm(mod_l, 3), mod_fm(mod_l, 5)
    taps = np.ascontiguousarray(ffn_dw_l.T.reshape(88, 128, 3).transpose(1, 0, 2))
    maps = []
    for i in range(NCORES):
        cols, mk = [], []
        for (s, n, kind) in PIECES:
            src, t0 = (x_lat, i * 1024 + s) if kind == 0 else (x_ctx, i * 32 + s)
            cols.append(halo_rows(src, t0, n, 1))
            pos = np.arange(t0 - 1, t0 + n + 1)
            mk.append(((pos >= 0) & (pos < src.shape[0])).astype(np.float32))
        xs = np.concatenate(cols, 0).T
        mask = np.ascontiguousarray(np.broadcast_to(np.concatenate(mk)[None, :], (128, T4)))
        maps.append({"xT": fm(np.ascontiguousarray(xs)), "g": gf, "sc": sc, "sh": sh, "gate": ga, "taps": taps, "mask": mask, "wu": w_up_l, "wd": w_down_l})
    res = P.run(maps)
    lat = np.zeros((SEQ, D), np.float32)
    ctx = np.zeros((CTX, D), np.float32)
    for i in range(NCORES):
        o = res[i]["xoT"].T
        for pi_, (s, n, kind) in enumerate(PIECES):
            seg = o[PC0[pi_] + 1:PC0[pi_] + 1 + n]
            if kind == 0:
                lat[i * 1024 + s:i * 1024 + s + n] = seg
            else:
                ctx[i * 32 + s:i * 32 + s + n] = seg
    return lat, ctx


def build_k5():
    P = Prog()
    S = P.S
    x_in = P.inp("xT", [128, KC, 1024])
    g_in = P.inp("g", [128, KC])
    o_out = P.out("oT", [128, KC, 1024])
    ones, bones = make_consts(P)
    xT, bx = P.load(x_in, [128, KC, 1024])
    g, bg = P.load(g_in, [128, KC])
    ps, bps = P.ps()
    sq = [P.sb([128, 512]) for _ in range(2)]
    rstd, brs = P.sb([128, 512])
    res, bres = P.sb([128, KC, 1024])
    for c0 in (0, 512):
        for c in range(KC):
            s_, bs_ = sq[c % 2]
            S.op("act", "activation", reads=[bx], writes=[bs_], out=s_[:], in_=xT[:, c, c0:c0 + 512], func=AF.Square)
            S.op("pe", "matmul", reads=[bones, bs_], writes=[bps], out=ps[:], lhsT=ones[:], rhs=s_[:], start=(c == 0), stop=(c == KC - 1))
        S.op("act", "activation", reads=[bps, epsb[1]], writes=[brs], out=rstd[:], in_=ps[:], func=AF.Sqrt, bias=epsb[0][:], scale=1.0 / D)
        S.op("dve", "reciprocal", reads=[brs], writes=[brs], out=rstd[:], in_=rstd[:])
        for c in range(KC):
            S.op("dve", "scalar_tensor_tensor", reads=[bx, bg, brs], writes=[bres], out=res[:, c, c0:c0 + 512], in0=xT[:, c, c0:c0 + 512], scalar=g[:, c:c + 1], in1=rstd[:], op0=ALU.mult, op1=ALU.mult)
    S.dma("sp", o_out[0], res[:], reads=[bres])
    return P


def run_k5(P, x_lat, g):
    gf = vec_fm(g)
    maps = [{"xT": fm(np.ascontiguousarray(x_lat[i * 1024:(i + 1) * 1024].T)), "g": gf} for i in range(NCORES)]
    res = P.run(maps)
    return np.concatenate([unfm(r["oT"]).T for r in res], 0)


_PROGS = {}


def _prog(name, builder):
    if name not in _PROGS:
        _PROGS[name] = builder()
    return _PROGS[name]


def kernel(x, c, ctx, c_ctx, w_ada, b_ada, g_mix, w_in, na_rpb, rg_conv, w_rg, b_rg, rg_lambda,
           conf_dw, conf_ln_g, conf_ln_b, q_norm_g, k_norm_g, w_branch, w_out, g_ffn, w_up,
           ffn_dw, w_down, g_final):
    f = lambda a: np.ascontiguousarray(np.asarray(a, dtype=np.float32))
    x, c, ctx, c_ctx, w_ada, b_ada, g_mix, w_in, na_rpb, rg_conv, w_rg, b_rg, rg_lambda = map(f, (x, c, ctx, c_ctx, w_ada, b_ada, g_mix, w_in, na_rpb, rg_conv, w_rg, b_rg, rg_lambda))
    conf_dw, conf_ln_g, conf_ln_b, q_norm_g, k_norm_g, w_branch, w_out, g_ffn, w_up, ffn_dw, w_down, g_final = map(
        f, (conf_dw, conf_ln_g, conf_ln_b, q_norm_g, k_norm_g, w_branch, w_out, g_ffn, w_up, ffn_dw, w_down, g_final))
    xl, xc = x[0], ctx[0]
    mod = run_k0(_prog("k0", build_k0), c, c_ctx, w_ada, b_ada)
    depth = w_in.shape[0]
    for l in range(depth):
        zl, zc = run_k1(_prog("k1", build_k1), xl, xc, g_mix[l], mod[l], w_in[l])
        al, ac = run_k2a(_prog("k2a", build_k2a), zl, zc, na_rpb[l])
        bl, bc = run_k2b(_prog("k2b", build_k2b), zl, zc, rg_conv[l], w_rg[l], b_rg[l], rg_lambda[l])
        cl, cc = run_k2c(_prog("k2c", build_k2c), zl, zc, conf_dw[l], conf_ln_g[l], conf_ln_b[l])
        dl, dc = run_k2d(_prog("k2d", build_k2d), zl, zc, q_norm_g[l], k_norm_g[l])
        xm, cm = run_k3(_prog("k3", build_k3), xl, xc, [al, bl, cl, dl], [ac, bc, cc, dc], g_mix[l], mod[l], w_in[l], w_branch[l], w_out[l])
        xl, xc2 = run_k4(_prog("k4", build_k4), xm, cm, g_ffn[l], mod[l], w_up[l], ffn_dw[l], w_down[l])
        if l < depth - 1:
            xc = xc2
    out = run_k5(_prog("k5", build_k5), xl, g_final)
    return np.ascontiguousarray(out[None].astype(np.float32))
```

```python
import numpy as np
import concourse.bass as bass
import concourse.mybir as mybir
from concourse.bass_utils import run_bass_kernel_spmd

F32 = mybir.dt.float32
BF16 = mybir.dt.bfloat16
AF = mybir.ActivationFunctionType
ALU = mybir.AluOpType
AX = mybir.AxisListType

NCORES = 8
D = 2048
KC = 16
SEQ = 8192
CTX = 256
GW = 64
DFF = 5632
NIN1 = 4352
EPS = 1e-6
NEG = -30000.0
TRACE = False
TRACE_TAG = [""]


class Buf:
    __slots__ = ("w", "r", "excl")

    def __init__(self, excl=False):
        self.w = None
        self.r = {}
        self.excl = excl


class Sync:
    ENG = ("pe", "act", "dve", "pool", "sp")

    def __init__(self, nc, n_dma_sems=16):
        self.nc = nc
        self.q = {e: [] for e in self.ENG}
        self.sem = {e: nc.alloc_semaphore(name="S_" + e) for e in self.ENG}
        self.cnt = {e: 0 for e in self.ENG}
        self.waited = {e: {} for e in self.ENG}
        self.dsem = {}
        self.dcnt = {}
        self.drr = {}
        for e in ("sp", "pool", "act"):
            self.dsem[e] = [nc.alloc_semaphore(name="D_%s%d" % (e, i)) for i in range(n_dma_sems)]
            self.drr[e] = 0
            for s in self.dsem[e]:
                self.dcnt[s] = 0

    def _deps(self, reads, writes):
        deps = {}

        def add(s, v):
            if deps.get(s, 0) < v:
                deps[s] = v
        for b in reads:
            if b.w is not None:
                add(*b.w)
        for b in writes:
            if b.w is not None:
                add(*b.w)
            for s, v in b.r.items():
                add(s, v)
        return deps

    def _emit_waits(self, eng, deps):
        q = self.q[eng]
        for s, v in deps.items():
            if eng == "pe" and s is self.sem["pe"]:
                continue
            if self.waited[eng].get(s, 0) >= v:
                continue
            self.waited[eng][s] = v
            q.append(("w", s, v))

    def _record(self, me, reads, writes):
        s, v = me
        for b in reads:
            if b.r.get(s, 0) < v:
                b.r[s] = v
        for b in writes:
            b.w = me
            b.r = {}

    def op(self, eng, fn, reads=(), writes=(), **kw):
        if isinstance(fn, str):
            name = fn
            fn = lambda h, name=name, kw=kw: getattr(h, name)(**kw)
        ex = [b for b in reads if b.excl]
        if ex:
            reads = [b for b in reads if not b.excl]
            writes = list(writes) + ex
        deps = self._deps(reads, writes)
        self._emit_waits(eng, deps)
        self.cnt[eng] += 1
        me = (self.sem[eng], self.cnt[eng])
        self.q[eng].append(("i", fn, self.sem[eng], 1))
        self._record(me, reads, writes)

    def dma(self, eng, out, in_, reads=(), writes=(), **kw):
        sems = self.dsem[eng]
        s = sems[self.drr[eng] % len(sems)]
        self.drr[eng] += 1
        deps = self._deps(reads, writes)
        if self.dcnt[s] > 0 and deps.get(s, 0) < self.dcnt[s]:
            deps[s] = self.dcnt[s]
        self._emit_waits(eng, deps)
        self.dcnt[s] += 16
        me = (s, self.dcnt[s])
        self.q[eng].append(("i", lambda e, o=out, i=in_, k=kw: e.dma_start(out=o, in_=i, **k), s, 16))
        self._record(me, reads, writes)

    def finish(self):
        q = self.q["sp"]
        for s, v in self.dcnt.items():
            if v > 0:
                q.append(("w", s, v))
        for e in ("pe", "act", "dve", "pool"):
            if self.cnt[e] > 0:
                q.append(("w", self.sem[e], self.cnt[e]))
        qs = self.q

        def replay(h, items):
            for it in items:
                if it[0] == "w":
                    h.wait_ge(it[1], it[2])
                else:
                    it[1](h).then_inc(it[2], it[3])

        with self.nc.Block() as block:
            @block.sync
            def _(e):
                replay(e, qs["sp"])

            @block.tensor
            def _(e):
                replay(e, qs["pe"])

            @block.scalar
            def _(e):
                replay(e, qs["act"])

            @block.vector
            def _(e):
                replay(e, qs["dve"])

            @block.gpsimd
            def _(e):
                replay(e, qs["pool"])


class Prog:
    def __init__(self):
        self.nc = bass.Bass("TRN2", target_bir_lowering=False)
        self.S = Sync(self.nc)
        self._n = 0
        self.done = False
        self.psr = 0

    def inp(self, name, shape, dt=F32):
        return self.nc.dram_tensor(name, list(shape), dt, kind="ExternalInput").ap(), Buf()

    def out(self, name, shape, dt=F32):
        return self.nc.dram_tensor(name, list(shape), dt, kind="ExternalOutput").ap(), Buf()

    def sb(self, shape, dt=F32):
        self._n += 1
        return self.nc.alloc_sbuf_tensor("t%d" % self._n, list(shape), dt), Buf()

    def ps(self, shape=(128, 512), dt=F32):
        self._n += 1
        return self.nc.alloc_psum_tensor("p%d" % self._n, list(shape), dt), Buf(excl=True)

    def load(self, dram, shape, dt=F32, eng="sp", view=None):
        t, b = self.sb(shape, dt)
        kw = {"max_dma_last_dim": 4096} if eng == "pool" else {}
        self.S.dma(eng, t[:] if view is None else view(t), dram[0], reads=[dram[1]], writes=[b], **kw)
        return t, b

    def run(self, in_maps):
        global N_LAUNCH
        if not self.done:
            self.S.finish()
            self.done = True
        if TRACE:
            res = run_bass_kernel_spmd(self.nc, in_maps, core_ids=list(range(NCORES)), trace=True)
            print("EXEC_NS", TRACE_TAG[0], res.exec_time_ns, flush=True)
        else:
            res = run_bass_kernel_spmd(self.nc, in_maps, core_ids=list(range(NCORES)))
        return res.results


def fm(a, p=128):
    r, t = a.shape
    return np.ascontiguousarray(a.reshape(r // p, p, t).transpose(1, 0, 2))


def unfm(a):
    p, c, t = a.shape
    return a.transpose(1, 0, 2).reshape(c * p, t)


def vec_fm(v, p=128):
    return np.ascontiguousarray(v.reshape(-1, p).T)


def norm_setup(P, g_in, sc_in, sh_in):
    S = P.S
    g, bg = P.load(g_in, [128, KC])
    sc, bsc = P.load(sc_in, [128, KC, 2])
    sh, bsh = P.load(sh_in, [128, KC, 2])
    gs, bgs = P.sb([128, KC, 2])
    S.op("dve", "tensor_scalar", reads=[bsc], writes=[bgs], out=gs[:], in0=sc[:], scalar1=1.0, scalar2=None, op0=ALU.add)
    for j in range(2):
        S.op("dve", "tensor_tensor", reads=[bgs, bg], writes=[bgs], out=gs[:, :, j], in0=gs[:, :, j], in1=g[:], op=ALU.mult)
    sq, bsq = P.sb([128, 2, 512])
    rstd, brs = P.sb([128, 512])
    tmp, btmp = P.sb([128, 2, 512])
    return dict(gs=gs, bgs=bgs, sh=sh, bsh=bsh, sq=sq, bsqs=[bsq, Buf()], rstd=rstd, brs=brs, tmp=tmp, btmps=[btmp, Buf()])


def norm_run(P, C, xT, bx, hT, bh, blocks, ones, bones, psums):
    S = P.S
    gs, bgs, sh, bsh, sq, bsqs, rstd, brs, tmp, btmps = (C[k] for k in ("gs", "bgs", "sh", "bsh", "sq", "bsqs", "rstd", "brs", "tmp", "btmps"))
    for (c0, n, kind) in blocks:
        ps, bps = psums[0]
        for c in range(KC):
            S.op("act", "activation", reads=[bx], writes=[bsqs[c % 2]], out=sq[:, c % 2, :n], in_=xT[:, c, c0:c0 + n], func=AF.Square)
            S.op("pe", "matmul", reads=[bones, bsqs[c % 2]], writes=[bps], out=ps[:, :n], lhsT=ones[:], rhs=sq[:, c % 2, :n], start=(c == 0), stop=(c == KC - 1))
        S.op("act", "activation", reads=[bps, epsb[1]], writes=[brs], out=rstd[:, :n], in_=ps[:, :n], func=AF.Sqrt, bias=epsb[0][:], scale=1.0 / D)
        S.op("dve", "reciprocal", reads=[brs], writes=[brs], out=rstd[:, :n], in_=rstd[:, :n])
        for c in range(KC):
            S.op("dve", "tensor_tensor", reads=[bx, brs], writes=[btmps[c % 2]], out=tmp[:, c % 2, :n], in0=xT[:, c, c0:c0 + n], in1=rstd[:, :n], op=ALU.mult)
            S.op("act", "activation", reads=[btmps[c % 2], bgs, bsh], writes=[bh], out=hT[:, c, c0:c0 + n], in_=tmp[:, c % 2, :n], func=AF.Identity,
                 bias=sh[:, c, kind:kind + 1], scale=gs[:, c, kind:kind + 1])


def emit_norm_mod(P, xT, bx, hT, bh, blocks, g_in, sc_in, sh_in, ones, bones, psums):
    C = norm_setup(P, g_in, sc_in, sh_in)
    norm_run(P, C, xT, bx, hT, bh, blocks, ones, bones, psums)


epsb = [None, None]


def make_consts(P):
    S = P.S
    ones, bones = P.sb([128, 128])
    S.op("dve", lambda e: e.memset(ones[:], 1.0), writes=[bones])
    ep, bep = P.sb([128, 1])
    S.op("dve", lambda e: e.memset(ep[:], EPS), writes=[bep])
    epsb[0], epsb[1] = ep, bep
    return ones, bones


def tile_w(w, tw):
    k, f = w.shape
    return np.ascontiguousarray(w.reshape(k // 128, 128, f // tw, tw).transpose(2, 1, 0, 3))


def emit_proj(P, w_in, ncols, col0, hT, bh, kc_n, blocks, evac, psums, wtile=512):
    S = P.S
    wt = [P.sb([128, kc_n, wtile], BF16) for _ in range(2)]
    assert ncols % wtile == 0 and col0 % wtile == 0
    nt = ncols // wtile
    pi = 0
    for t in range(nt):
        w0 = t * wtile
        wtt, bwt = wt[t % 2]
        S.dma("sp", wtt[:], w_in[0][col0 // wtile + t], reads=[w_in[1]], writes=[bwt])
        for mm in range(wtile // 128):
            m = (w0 // 128) + mm
            for bi, (c0, n, kind) in enumerate(blocks):
                ps, bps = psums[pi % len(psums)]
                pi += 1
                for k in range(kc_n):
                    S.op("pe", "matmul", reads=[bwt, bh], writes=[bps], out=ps[:, :n], lhsT=wtt[:, k, mm * 128:(mm + 1) * 128], rhs=hT[:, k, c0:c0 + n], start=(k == 0), stop=(k == kc_n - 1))
                evac(m, bi, ps, bps, c0, n, kind)


def build_k0():
    P = Prog()
    S = P.S
    cc_in = P.inp("cc", [128, KC, 2])
    w_in = P.inp("w", [D, 3072])
    b_in = P.inp("b", [2, 3072])
    o = P.out("mod", [2, 3072])
    cc, bcc = P.load(cc_in, [128, KC, 2])
    sc, bsc = P.sb([128, KC, 2])
    S.op("act", lambda e: e.activation(out=sc[:], in_=cc[:], func=AF.Silu), reads=[bcc], writes=[bsc])
    b2, bb2 = P.load(b_in, [2, 3072])
    res, bres = P.sb([2, 3072])
    wt = [P.sb([128, KC, 512]) for _ in range(2)]
    pss = [P.ps() for _ in range(2)]
    wv = w_in[0].rearrange("(kc p) f -> p kc f", p=128)
    for nb in range(6):
        wtt, bwt = wt[nb % 2]
        S.dma("sp", wtt[:], wv[:, :, nb * 512:(nb + 1) * 512], reads=[w_in[1]], writes=[bwt])
        ps, bps = pss[nb % 2]
        for k in range(KC):
            S.op("pe", lambda e, ps=ps, wtt=wtt, k=k: e.matmul(ps[0:2, :], sc[:, k, :], wtt[:, k, :], start=(k == 0), stop=(k == KC - 1)), reads=[bsc, bwt], writes=[bps])
        S.op("dve", lambda e, ps=ps, nb=nb: e.tensor_tensor(out=res[:, nb * 512:(nb + 1) * 512], in0=ps[0:2, :], in1=b2[:, nb * 512:(nb + 1) * 512], op=ALU.add), reads=[bps, bb2], writes=[bres])
    S.dma("sp", o[0], res[:], reads=[bres])
    return P


def run_k0(P, c, c_ctx, w_ada, b_ada):
    cc = np.stack([vec_fm(c.reshape(-1)), vec_fm(c_ctx.reshape(-1))], axis=-1).astype(np.float32)
    maps = []
    for i in range(NCORES):
        l, c0 = i // 4, (i % 4) * 3072
        maps.append({"cc": cc, "w": np.ascontiguousarray(w_ada[l][:, c0:c0 + 3072]),
                     "b": np.ascontiguousarray(np.broadcast_to(b_ada[l][c0:c0 + 3072], (2, 3072)))})
    res = P.run(maps)
    mod = np.zeros((2, 2, 12288), np.float32)
    for i in range(NCORES):
        l, c0 = i // 4, (i % 4) * 3072
        mod[l][:, c0:c0 + 3072] = res[i]["mod"]
    return mod


def mod_fm(mod_l, j):
    a = mod_l[:, j * D:(j + 1) * D]
    return np.ascontiguousarray(np.stack([vec_fm(a[0]), vec_fm(a[1])], axis=-1))


TK = 1056
BLK1 = [(0, 512, 0), (512, 512, 0), (1024, 32, 1)]


def build_k1():
    P = Prog()
    S = P.S
    x_in = P.inp("xT", [128, KC, TK])
    g_in = P.inp("g", [128, KC])
    sc_in = P.inp("sc", [128, KC, 2])
    sh_in = P.inp("sh", [128, KC, 2])
    w_in = P.inp("w", [NIN1 // 256, 128, KC, 256], BF16)
    z_out = P.out("zT", [NIN1, TK])
    ones, bones = make_consts(P)
    xT, bx = P.load(x_in, [128, KC, TK])
    hT, bh = P.sb([128, KC, TK], BF16)
    psums = [P.ps() for _ in range(4)]
    emit_norm_mod(P, xT, bx, hT, bh, BLK1, g_in, sc_in, sh_in, ones, bones, psums)
    stg = [P.sb([128, 512]) for _ in range(3)]
    cnt = [0]

    def evac(m, bi, ps, bps, c0, n, kind):
        st, bst = stg[cnt[0] % 3]
        cnt[0] += 1
        if cnt[0] % 2:
            S.op("act", lambda e: e.copy(out=st[:, :n], in_=ps[:, :n]), reads=[bps], writes=[bst])
        else:
            S.op("dve", lambda e: e.tensor_copy(out=st[:, :n], in_=ps[:, :n]), reads=[bps], writes=[bst])
        S.dma("pool", z_out[0][m * 128:(m + 1) * 128, c0:c0 + n], st[:, :n], reads=[bst])

    emit_proj(P, w_in, NIN1, 0, hT, bh, KC, BLK1, evac, psums, wtile=256)
    return P


def shard_tokens_T(x_lat, x_ctx):
    outs = []
    for i in range(NCORES):
        a = np.concatenate([x_lat[i * 1024:(i + 1) * 1024], x_ctx[i * 32:(i + 1) * 32]], axis=0)
        outs.append(np.ascontiguousarray(a.T))
    return outs


def unshard_tokens_T(parts):
    lat = np.concatenate([p[:, :1024].T for p in parts], axis=0)
    ctx = np.concatenate([p[:, 1024:].T for p in parts], axis=0)
    return np.ascontiguousarray(lat), np.ascontiguousarray(ctx)


def run_k1(P, x_lat, x_ctx, g, mod_l, w_in_b):
    xs = shard_tokens_T(x_lat, x_ctx)
    gf = vec_fm(g)
    sc, sh = mod_fm(mod_l, 1), mod_fm(mod_l, 0)
    w = tile_w(w_in_b[:, :NIN1], 256)
    maps = [{"xT": fm(xs[i]), "g": gf, "sc": sc, "sh": sh, "w": w} for i in range(NCORES)]
    res = P.run(maps)
    return unshard_tokens_T([r["zT"] for r in res])


NA_TILES = 66
NA_R0 = [0, 2, 60, 124, 126]


def natten_bias(rpb_h):
    out = np.full((5, 128, 576), NEG, np.float32)
    for vi, r0 in enumerate(NA_R0):
        base = min(max(r0 - 4, 0), 119)
        for qr in range(2):
            r = r0 + qr
            rs = min(max(r - 4, 0), 120)
            for qc in range(64):
                cs = min(max(qc - 8, 0), 48)
                kcs = np.arange(cs, cs + 16)
                for kr in range(9):
                    ar = base + kr
                    if rs <= ar < rs + 8:
                        out[vi, qr * 64 + qc, kr * 64 + kcs] = rpb_h[ar - r + 7, kcs - qc + 15]
    return np.ascontiguousarray(out.transpose(1, 0, 2))


def build_k2a():
    P = Prog()
    S = P.S
    q_in = P.inp("qT", [64, 8448])
    k_in = P.inp("kT", [64, 8448])
    val_in = P.inp("val", [128, 64, 64])
    vsh_in = P.inp("vsh", [128, 5, 64])
    vc_in = P.inp("vc", [128, 2, 64])
    bias_in = P.inp("bias", [128, 5, 576])
    id_in = P.inp("ident", [128, 128])
    o_out = P.out("o", [128, NA_TILES, 64])
    qb, bqb = P.load(q_in, [64, 8448], BF16, eng="pool")
    kb, bkb = P.load(k_in, [64, 8448], BF16, eng="pool")
    val, bval = P.load(val_in, [128, 64, 64], BF16, eng="pool")
    vsh, bvsh = P.load(vsh_in, [128, 5, 64], BF16, eng="pool")
    vc, bvc = P.load(vc_in, [128, 2, 64], BF16, eng="pool")
    ident, bid = P.load(id_in, [128, 128], BF16, eng="pool")
    bias, bbias = P.load(bias_in, [128, 5, 576])
    o_all, bo = P.sb([128, NA_TILES, 64])
    psA = [P.ps() for _ in range(2)]
    psB = [P.ps() for _ in range(2)]
    psT = [P.ps([128, 7, 128], BF16) for _ in range(2)]
    psO = [P.ps() for _ in range(2)]
    sbs = [P.sb([128, 832]) for _ in range(2)]
    pbf = [P.sb([128, 832], BF16) for _ in range(2)]
    pTs = [P.sb([128, 7, 128], BF16) for _ in range(2)]
    small = [P.sb([128, 4]) for _ in range(2)]
    for t in range(NA_TILES):
        i = t % 2
        (pa, bpa), (pb, bpb), (pt, bpt), (po, bpo) = psA[i], psB[i], psT[i], psO[i]
        (ss, bss), (pf, bpf), (pT, bpT), (sm, bsm) = sbs[i], pbf[i], pTs[i], small[i]
        local = t < 64
        qs = slice(t * 128, (t + 1) * 128)
        if local:
            r0 = 2 * t
            base = min(max(r0 - 4, 0), 119)
            k0 = base * 64
            var = {0: 0, 2: 1, 124: 3, 126: 4}.get(r0, 2)
            S.op("pe", lambda e, pa=pa, qs=qs, k0=k0: e.matmul(pa[:, 0:512], qb[:, qs], kb[:, k0:k0 + 512], start=True, stop=True), reads=[bqb, bkb], writes=[bpa])
            S.op("pe", lambda e, pb=pb, qs=qs, k0=k0: e.matmul(pb[:, 0:64], qb[:, qs], kb[:, k0 + 512:k0 + 576], start=True, stop=True), reads=[bqb, bkb], writes=[bpb])
        S.op("pe", lambda e, pb=pb, qs=qs: e.matmul(pb[:, 64:320], qb[:, qs], kb[:, 8192:8448], start=True, stop=True), reads=[bqb, bkb], writes=[bpb])
        lo = 0 if local else 576
        if local:
            S.op("dve", lambda e, ss=ss, pa=pa, var=var: e.scalar_tensor_tensor(out=ss[:, 0:512], in0=pa[:, 0:512], scalar=0.125, in1=bias[:, var, 0:512], op0=ALU.mult, op1=ALU.add), reads=[bpa, bbias], writes=[bss])
            S.op("dve", lambda e, ss=ss, pb=pb, var=var: e.scalar_tensor_tensor(out=ss[:, 512:576], in0=pb[:, 0:64], scalar=0.125, in1=bias[:, var, 512:576], op0=ALU.mult, op1=ALU.add), reads=[bpb, bbias], writes=[bss])
        S.op("act", lambda e, ss=ss, pb=pb: e.activation(out=ss[:, 576:832], in_=pb[:, 64:320], func=AF.Identity, scale=0.125), reads=[bpb], writes=[bss])
        S.op("dve", lambda e, ss=ss, sm=sm, lo=lo: e.tensor_reduce(out=sm[:, 0:1], in_=ss[:, lo:832], axis=AX.X, op=ALU.max), reads=[bss], writes=[bsm])
        S.op("dve", lambda e, sm=sm: e.tensor_scalar(out=sm[:, 1:2], in0=sm[:, 0:1], scalar1=-1.0, scalar2=None, op0=ALU.mult), reads=[bsm], writes=[bsm])
        S.op("act", lambda e, ss=ss, pf=pf, sm=sm, lo=lo: e.activation(out=pf[:, lo:832], in_=ss[:, lo:832], func=AF.Exp, bias=sm[:, 1:2], scale=1.0, accum_out=sm[:, 2:3]), reads=[bss, bsm], writes=[bpf, bsm])
        chunks = ([(0, 128), (128, 128), (256, 128), (384, 128), (512, 64)] if local else []) + [(576, 128), (704, 128)]
        jl = [0, 1, 2, 3, 4, 5, 6] if local else [5, 6]
        for j, (c0, w) in zip(jl, chunks):
            S.op("pe", lambda e, pt=pt, pf=pf, j=j, c0=c0, w=w: e.transpose(out=pt[0:w, j, :], in_=pf[:, c0:c0 + w], identity=ident[:]), reads=[bpf, bid], writes=[bpt])
        if local:
            S.op("dve", lambda e, pT=pT, pt=pt: e.tensor_copy(out=pT[:, 0:4, :], in_=pt[:, 0:4, :]), reads=[bpt], writes=[bpT])
            S.op("act", lambda e, pT=pT, pt=pt: e.copy(out=pT[0:64, 4, :], in_=pt[0:64, 4, :]), reads=[bpt], writes=[bpT])
        S.op("act", lambda e, pT=pT, pt=pt: e.copy(out=pT[:, 5:7, :], in_=pt[:, 5:7, :]), reads=[bpt], writes=[bpT])
        mm = []
        if local:
            even = (base % 2 == 0)
            for j in range(4):
                rhs = val[:, k0 // 128 + j, :] if even else vsh[:, j, :]
                mm.append((pT[:, j, :], rhs))
            rhs = val[0:64, k0 // 128 + 4, :] if even else vsh[0:64, 4, :]
            mm.append((pT[0:64, 4, :], rhs))
        mm.append((pT[:, 5, :], vc[:, 0, :]))
        mm.append((pT[:, 6, :], vc[:, 1, :]))
        for n, (l, r) in enumerate(mm):
            S.op("pe", lambda e, po=po, l=l, r=r, n=n, last=len(mm) - 1: e.matmul(po[:, 0:64], l, r, start=(n == 0), stop=(n == last)), reads=[bpT, bval, bvsh, bvc], writes=[bpo])
        S.op("dve", lambda e, sm=sm: e.reciprocal(out=sm[:, 3:4], in_=sm[:, 2:3]), reads=[bsm], writes=[bsm])
        S.op("dve", lambda e, po=po, sm=sm, t=t: e.tensor_scalar(out=o_all[:, t, :], in0=po[:, 0:64], scalar1=sm[:, 3:4], scalar2=None, op0=ALU.mult), reads=[bpo, bsm], writes=[bo])
    S.dma("sp", o_out[0], o_all[:], reads=[bo])
    return P


def tok_chunks(a, p=128):
    t, f = a.shape
    return np.ascontiguousarray(a.reshape(t // p, p, f).transpose(1, 0, 2))


def run_k2a(P, zl, zc, rpb):
    ident = np.eye(128, dtype=np.float32)
    maps = []
    for h in range(NCORES):
        hs = slice(h * 64, (h + 1) * 64)
        q = np.concatenate([zl[:, 0:512][:, hs], zc[:, 0:512][:, hs]], 0)
        k = np.concatenate([zl[:, 512:1024][:, hs], zc[:, 512:1024][:, hs]], 0)
        v = zl[:, 1024:1536][:, hs]
        vcx = zc[:, 1024:1536][:, hs]
        vs = np.zeros((5 * 128, 64), np.float32)
        vs[:576] = v[119 * 64:119 * 64 + 576]
        maps.append({"qT": np.ascontiguousarray(q.T), "kT": np.ascontiguousarray(k.T), "val": tok_chunks(v),
                     "vsh": tok_chunks(vs), "vc": tok_chunks(vcx), "bias": natten_bias(rpb[h]), "ident": ident})
    res = P.run(maps)
    a_lat = np.zeros((SEQ, 512), np.float32)
    a_ctx = np.zeros((CTX, 512), np.float32)
    for h in range(NCORES):
        o = res[h]["o"]
        o = o.transpose(1, 0, 2).reshape(66 * 128, 64)
        a_lat[:, h * 64:(h + 1) * 64] = o[:SEQ]
        a_ctx[:, h * 64:(h + 1) * 64] = o[SEQ:]
    return a_lat, a_ctx


QCOLS = 8448
NCH = 17


def rope_tables():
    inv = (np.float32(10000.0) ** (-np.arange(16, dtype=np.float32) / np.float32(16))).astype(np.float32)
    pos = np.arange(SEQ)
    pr = (pos // GW).astype(np.float32)
    pc = (pos % GW).astype(np.float32)
    cos = np.zeros((64, SEQ), np.float32)
    sin = np.zeros((64, SEQ), np.float32)
    for m in range(64):
        p = pr if m < 32 else pc
        ang = (p * inv[m % 16]).astype(np.float32)
        cos[m] = np.cos(ang).astype(np.float32)
        sin[m] = np.sin(ang).astype(np.float32)
    return cos, sin


def rot_matrix():
    r = np.zeros((64, 64), np.float32)
    for m in range(64):
        if (m % 32) < 16:
            r[m + 16, m] = -1.0
        else:
            r[m - 16, m] = 1.0
    return r


def build_k2d():
    P = Prog()
    S = P.S
    q_in = P.inp("q", [64, QCOLS])
    k_in = P.inp("k", [64, QCOLS])
    cq_in = P.inp("cosq", [64, QCOLS])
    sq_in = P.inp("sinq", [64, QCOLS])
    ck_in = P.inp("cosk", [64, QCOLS])
    sk_in = P.inp("sink", [64, QCOLS])
    v_in = P.inp("vaug", [128, 66, 65])
    gq_in = P.inp("gq", [64, 1])
    gk_in = P.inp("gk", [64, 1])
    grow_in = P.inp("grow", [1, 2, 64])
    rot_in = P.inp("rot", [64, 64])
    o_out = P.out("o", [64, 17, 512])
    gq, bgq = P.load(gq_in, [64, 1])
    gk, bgk = P.load(gk_in, [64, 1])
    grow, bgrow = P.load(grow_in, [1, 2, 64])
    rot, brot = P.load(rot_in, [64, 64])
    vaug, bv = P.load(v_in, [128, 66, 65], BF16, eng="pool")
    ones, bones = P.sb([64, 128])
    S.op("dve", lambda e: e.memset(ones[:], 1.0 / 64.0), writes=[bones])
    one1, bone1 = P.sb([1, 128])
    S.op("dve", lambda e: e.memset(one1[:], 1.0), writes=[bone1])
    ep, bep = P.sb([128, 1])
    S.op("dve", lambda e: e.memset(ep[:], EPS), writes=[bep])
    mm_, bmm = P.sb([1, 4])
    S.op("dve", lambda e: e.tensor_reduce(out=mm_[:, 0:2], in_=grow[:], axis=AX.X, op=ALU.max, apply_absolute_value=True), reads=[bgrow], writes=[bmm])
    S.op("dve", lambda e: e.tensor_tensor(out=mm_[:, 2:3], in0=mm_[:, 0:1], in1=mm_[:, 1:2], op=ALU.mult), reads=[bmm], writes=[bmm])
    S.op("dve", lambda e: e.tensor_scalar(out=mm_[:, 3:4], in0=mm_[:, 2:3], scalar1=-8.0, scalar2=None, op0=ALU.mult), reads=[bmm], writes=[bmm])
    psn = [P.ps() for _ in range(2)]
    negM, bnegM = P.sb([128, 1])
    S.op("pe", lambda e: e.matmul(psn[0][0][:, 0:1], one1[:], mm_[:, 3:4], start=True, stop=True), reads=[bone1, bmm], writes=[psn[0][1]])
    S.op("dve", lambda e: e.tensor_copy(out=negM[:], in_=psn[0][0][:, 0:1]), reads=[psn[0][1]], writes=[bnegM])

    qb, bqb = P.sb([64, QCOLS], BF16)
    kb, bkb = P.sb([64, QCOLS], BF16)
    xin = [P.sb([64, 3, 512]) for _ in range(2)]
    w1 = [P.sb([64, 4, 512]) for _ in range(2)]
    it = 0
    for (src, cs, sn, g, bg, dst, bdst) in ((q_in, cq_in, sq_in, gq, bgq, qb, bqb), (k_in, ck_in, sk_in, gk, bgk, kb, bkb)):
        for ch in range(NCH):
            c0 = ch * 512
            n = min(512, QCOLS - c0)
            (xi, bxi), (w, bw) = xin[it % 2], w1[it % 2]
            (pm, bpm), (pr, bpr) = psn[0], psn[1]
            it += 1
            S.dma("sp", xi[:, 0, :n], src[0][:, c0:c0 + n], reads=[src[1]], writes=[bxi])
            S.dma("sp", xi[:, 1, :n], cs[0][:, c0:c0 + n], reads=[cs[1]], writes=[bxi])
            S.dma("sp", xi[:, 2, :n], sn[0][:, c0:c0 + n], reads=[sn[1]], writes=[bxi])
            S.op("act", lambda e, w=w, xi=xi, n=n: e.activation(out=w[:, 0, :n], in_=xi[:, 0, :n], func=AF.Square), reads=[bxi], writes=[bw])
            S.op("pe", lambda e, pm=pm, w=w, n=n: e.matmul(pm[0:64, :n], ones[:, 0:64], w[:, 0, :n], start=True, stop=True), reads=[bones, bw], writes=[bpm])
            S.op("act", lambda e, pm=pm, w=w, n=n: e.activation(out=w[:, 1, :n], in_=pm[0:64, :n], func=AF.Sqrt, bias=ep[0:64, :], scale=1.0), reads=[bpm, bep], writes=[bw])
            S.op("dve", lambda e, w=w, n=n: e.reciprocal(out=w[:, 1, :n], in_=w[:, 1, :n]), reads=[bw], writes=[bw])
            S.op("dve", lambda e, w=w, xi=xi, n=n, g=g: e.scalar_tensor_tensor(out=w[:, 2, :n], in0=xi[:, 0, :n], scalar=g[:, 0:1], in1=w[:, 1, :n], op0=ALU.mult, op1=ALU.mult), reads=[bxi, bw, bg], writes=[bw])
            S.op("pe", lambda e, pr=pr, w=w, n=n: e.matmul(pr[0:64, :n], rot[:], w[:, 2, :n], start=True, stop=True), reads=[brot, bw], writes=[bpr])
            S.op("dve", lambda e, w=w, xi=xi, n=n: e.tensor_tensor(out=w[:, 3, :n], in0=w[:, 2, :n], in1=xi[:, 1, :n], op=ALU.mult), reads=[bw, bxi], writes=[bw])
            S.op("dve", lambda e, w=w, xi=xi, pr=pr, n=n: e.tensor_tensor(out=w[:, 0, :n], in0=pr[0:64, :n], in1=xi[:, 2, :n], op=ALU.mult), reads=[bpr, bxi], writes=[bw])
            S.op("dve", lambda e, w=w, dst=dst, c0=c0, n=n: e.tensor_tensor(out=dst[:, c0:c0 + n], in0=w[:, 3, :n], in1=w[:, 0, :n], op=ALU.add), reads=[bw], writes=[bdst])

    oT_all, bo = P.sb([64, 17, 512])
    S.op("pool", "memset", writes=[bo], ap=oT_all[:], constant=0.0)
    sel, bsel = P.sb([65, 64])
    S.op("dve", "memset", writes=[bsel], ap=sel[:], constant=0.0)
    S.op("dve", "memset", writes=[bsel], ap=sel[64:65, :], constant=1.0)
    psS = [P.ps() for _ in range(2)]
    psO = [P.ps() for _ in range(2)]
    psD, bpsD = P.ps()
    pTs = [P.sb([128, 512], BF16) for _ in range(3)]
    osbs = [P.sb([65, 512]) for _ in range(2)]
    rinv, brinv = P.sb([64, 512])
    work = []
    for t in range(17):
        ntok = 128 if t < 16 else 64
        chunks = list(range(66)) if t < 16 else [64, 65]
        for ci, kc in enumerate(chunks):
            work.append((t, 4 * ntok, t * 512, ci, kc, len(chunks)))

    def emit_S(i):
        t, ncol, c0, ci, kc, nch = work[i]
        pS, bpS = psS[i % 2]
        S.op("pe", "matmul", reads=[bkb, bqb], writes=[bpS], out=pS[:, :ncol], lhsT=kb[:, kc * 128:(kc + 1) * 128], rhs=qb[:, c0:c0 + ncol], start=True, stop=True)

    emit_S(0)
    for i in range(len(work)):
        t, ncol, c0, ci, kc, nch = work[i]
        if i + 1 < len(work):
            emit_S(i + 1)
        (pS, bpS), (pT, bpT) = psS[i % 2], pTs[i % 3]
        po, bpo = psO[t % 2]
        osb, bosb = osbs[t % 2]
        S.op("act", "activation", reads=[bpS, bnegM], writes=[bpT], out=pT[:, :ncol], in_=pS[:, :ncol], func=AF.Exp, bias=negM[:], scale=0.125)
        S.op("pe", "matmul", reads=[bpT, bv], writes=[bpo], out=po[0:65, :ncol], lhsT=vaug[:, kc, :], rhs=pT[:, :ncol], start=(ci == 0), stop=(ci == nch - 1))
        if ci == nch - 1:
            S.op("act", "copy", reads=[bpo], writes=[bosb], out=osb[:, :ncol], in_=po[0:65, :ncol])
            S.op("pe", "matmul", reads=[bsel, bosb], writes=[bpsD], out=psD[0:64, :ncol], lhsT=sel[:], rhs=osb[:, :ncol], start=True, stop=True)
            S.op("dve", "reciprocal", reads=[bpsD], writes=[brinv], out=rinv[:, :ncol], in_=psD[0:64, :ncol])
            S.op("dve", "tensor_tensor", reads=[bosb, brinv], writes=[bo], out=oT_all[:, t, :ncol], in0=osb[0:64, :ncol], in1=rinv[:, :ncol], op=ALU.mult)
    S.dma("sp", o_out[0], oT_all[:], reads=[bo])
    return P


def run_k2d(P, zl, zc, gq, gk):
    cos, sin = rope_tables()
    rot = rot_matrix()
    maps = []
    for core in range(NCORES):
        g, qt = core // 4, core % 4
        ql = zl[qt * 2048:(qt + 1) * 2048, 3584:4096].reshape(16, 128, 8, 64)[:, :, 4 * g:4 * g + 4, :]
        qlT = ql.transpose(3, 0, 2, 1).reshape(64, 16 * 4 * 128)
        qc = zc[qt * 64:(qt + 1) * 64, 3584:4096].reshape(64, 8, 64)[:, 4 * g:4 * g + 4, :]
        qcT = qc.transpose(2, 1, 0).reshape(64, 256)
        q = np.ascontiguousarray(np.concatenate([qlT, qcT], 1))
        cl = cos[:, qt * 2048:(qt + 1) * 2048].reshape(64, 16, 1, 128)
        sl = sin[:, qt * 2048:(qt + 1) * 2048].reshape(64, 16, 1, 128)
        cosq = np.concatenate([np.broadcast_to(cl, (64, 16, 4, 128)).reshape(64, 8192), np.ones((64, 256), np.float32)], 1)
        sinq = np.concatenate([np.broadcast_to(sl, (64, 16, 4, 128)).reshape(64, 8192), np.zeros((64, 256), np.float32)], 1)
        k = np.concatenate([zl[:, 4096:4224][:, g * 64:(g + 1) * 64], zc[:, 4096:4224][:, g * 64:(g + 1) * 64]], 0)
        cosk = np.concatenate([cos, np.ones((64, 256), np.float32)], 1)
        sink = np.concatenate([sin, np.zeros((64, 256), np.float32)], 1)
        v = np.concatenate([zl[:, 4224:4352][:, g * 64:(g + 1) * 64], zc[:, 4224:4352][:, g * 64:(g + 1) * 64]], 0)
        vaug = np.concatenate([v, np.ones((8448, 1), np.float32)], 1)
        maps.append({"q": q, "k": np.ascontiguousarray(k.T), "cosq": np.ascontiguousarray(cosq), "sinq": np.ascontiguousarray(sinq),
                     "cosk": np.ascontiguousarray(cosk), "sink": np.ascontiguousarray(sink), "vaug": tok_chunks(vaug),
                     "gq": np.ascontiguousarray(gq.reshape(64, 1)), "gk": np.ascontiguousarray(gk.reshape(64, 1)),
                     "grow": np.ascontiguousarray(np.stack([gq, gk])[None]), "rot": rot})
    res = P.run(maps)
    d_lat = np.zeros((SEQ, 512), np.float32)
    d_ctx = np.zeros((CTX, 512), np.float32)
    for core in range(NCORES):
        g, qt = core // 4, core % 4
        o = res[core]["o"]
        lat = o[:, :16, :].reshape(64, 16, 4, 128).transpose(1, 3, 2, 0).reshape(2048, 256)
        d_lat[qt * 2048:(qt + 1) * 2048, g * 256:(g + 1) * 256] = lat
        d_ctx[qt * 64:(qt + 1) * 64, g * 256:(g + 1) * 256] = o[:, 16, :256].reshape(64, 4, 64).transpose(2, 1, 0).reshape(64, 256)
    return d_lat, d_ctx


TALL = CTX + SEQ
PADW = TALL + 6


def rev(ap):
    return ap[:, ::-1]


def build_k2b():
    P = Prog()
    S = P.S
    x_in = P.inp("x3", [64, TALL])
    z_in = P.inp("z4", [64, TALL])
    tap_in = P.inp("taps", [64, 4])
    w_in = P.inp("wrg", [64, 4, 64])
    b_in = P.inp("brg", [64, 4])
    lam_in = P.inp("lam", [64, 2])
    o_out = P.out("o", [64, TALL])
    taps, btaps = P.load(tap_in, [64, 4])
    wrg, bwrg = P.load(w_in, [64, 4, 64], BF16, eng="pool")
    brg, bbrg = P.load(b_in, [64, 4])
    lam, blam = P.load(lam_in, [64, 2])
    T1, bT1 = P.sb([64, PADW])
    XC, bXC = P.sb([64, TALL])
    A, bA = P.sb([64, TALL])
    H = [P.sb([64, TALL]) for _ in range(2)]
    S.op("pool", "memset", writes=[bT1], ap=T1[:], constant=0.0)
    S.dma("sp", T1[:, 2:258], x_in[0][:, 0:256], reads=[x_in[1]], writes=[bT1])
    S.dma("sp", T1[:, 261:261 + SEQ], x_in[0][:, 256:TALL], reads=[x_in[1]], writes=[bT1])
    cd, bcd = P.sb([64, 2])
    S.op("act", "activation", reads=[blam], writes=[bcd], out=cd[:], in_=lam[:], func=AF.Exp, scale=-1.0)
    S.op("act", "activation", reads=[bcd], writes=[bcd], out=cd[:], in_=cd[:], func=AF.Ln, bias=1.0, scale=1.0)
    S.op("dve", "tensor_scalar", reads=[bcd], writes=[bcd], out=cd[:], in0=cd[:], scalar1=-8.0, scalar2=None, op0=ALU.mult)
    for (o0, n, s0) in ((0, CTX, 0), (CTX, SEQ, 259)):
        S.op("dve", "tensor_scalar", reads=[bT1, btaps], writes=[bXC], out=XC[:, o0:o0 + n], in0=T1[:, s0:s0 + n], scalar1=taps[:, 0:1], scalar2=None, op0=ALU.mult)
        for j in range(1, 4):
            S.op("dve", "scalar_tensor_tensor", reads=[bT1, btaps, bXC], writes=[bXC], out=XC[:, o0:o0 + n], in0=T1[:, s0 + j:s0 + j + n],
                 scalar=taps[:, j:j + 1], in1=XC[:, o0:o0 + n], op0=ALU.mult, op1=ALU.add)
    psr = [P.ps() for _ in range(2)]
    psi = [P.ps() for _ in range(2)]
    xb = [P.sb([64, 512], BF16) for _ in range(2)]
    wk = [P.sb([64, 4, 512]) for _ in range(2)]
    it = 0
    for d in range(2):
        Hd, bH = H[d]
        for ch in range(17):
            c0 = ch * 512
            n = min(512, TALL - c0)
            (xbb, bxb), (w, bw), (pr, bpr), (pi_, bpi) = xb[it % 2], wk[it % 2], psr[it % 2], psi[it % 2]
            it += 1
            S.op("act", "copy", reads=[bXC], writes=[bxb], out=xbb[:, :n], in_=XC[:, c0:c0 + n])
            S.op("pe", "matmul", reads=[bwrg, bxb], writes=[bpr], out=pr[0:64, :n], lhsT=wrg[:, 2 * d, :], rhs=xbb[:, :n], start=True, stop=True)
            S.op("pe", "matmul", reads=[bwrg, bxb], writes=[bpi], out=pi_[0:64, :n], lhsT=wrg[:, 2 * d + 1, :], rhs=xbb[:, :n], start=True, stop=True)
            S.op("act", "activation", reads=[bpr, bbrg], writes=[bw], out=w[:, 0, :n], in_=pr[0:64, :n], func=AF.Sigmoid, bias=brg[:, 2 * d:2 * d + 1], scale=1.0)
            S.op("act", "activation", reads=[bpi, bbrg], writes=[bw], out=w[:, 1, :n], in_=pi_[0:64, :n], func=AF.Sigmoid, bias=brg[:, 2 * d + 1:2 * d + 2], scale=1.0)
            S.op("act", "activation", reads=[bw, bcd], writes=[bA], out=A[:, c0:c0 + n], in_=w[:, 0, :n], func=AF.Exp, scale=cd[:, d:d + 1])
            S.op("dve", "tensor_tensor", reads=[bA], writes=[bw], out=w[:, 2, :n], in0=A[:, c0:c0 + n], in1=A[:, c0:c0 + n], op=ALU.mult)
            S.op("dve", "tensor_scalar", reads=[bw], writes=[bw], out=w[:, 2, :n], in0=w[:, 2, :n], scalar1=-1.0, scalar2=1.0, op0=ALU.mult, op1=ALU.add)
            S.op("act", "activation", reads=[bw], writes=[bw], out=w[:, 3, :n], in_=w[:, 2, :n], func=AF.Sqrt)
            S.op("dve", "tensor_tensor", reads=[bw], writes=[bw], out=w[:, 3, :n], in0=w[:, 3, :n], in1=w[:, 1, :n], op=ALU.mult)
            S.op("dve", "tensor_tensor", reads=[bw, bXC], writes=[bH], out=Hd[:, c0:c0 + n], in0=w[:, 3, :n], in1=XC[:, c0:c0 + n], op=ALU.mult)
        if d == 0:
            S.op("dve", "tensor_tensor_scan", reads=[bA, bH], writes=[bH], out=Hd[:, 0:CTX], data0=A[:, 0:CTX], data1=Hd[:, 0:CTX], initial=0.0, op0=ALU.mult, op1=ALU.add)
            for k in range(4):
                a0 = CTX + k * 2048
                S.op("dve", "tensor_tensor_scan", reads=[bA, bH], writes=[bH], out=Hd[:, a0:a0 + 2048], data0=A[:, a0:a0 + 2048], data1=Hd[:, a0:a0 + 2048],
                     initial=Hd[:, a0 - 1:a0], op0=ALU.mult, op1=ALU.add)
        else:
            S.op("dve", "tensor_tensor_scan", reads=[bA, bH], writes=[bH], out=rev(Hd[:, 0:CTX]), data0=rev(A[:, 0:CTX]), data1=rev(Hd[:, 0:CTX]), initial=0.0, op0=ALU.mult, op1=ALU.add)
            for k in range(4):
                a0 = TALL - (k + 1) * 2048
                init = Hd[:, 0:1] if k == 0 else Hd[:, a0 + 2048:a0 + 2049]
                S.op("dve", "tensor_tensor_scan", reads=[bA, bH], writes=[bH], out=rev(Hd[:, a0:a0 + 2048]), data0=rev(A[:, a0:a0 + 2048]), data1=rev(Hd[:, a0:a0 + 2048]),
                     initial=init, op0=ALU.mult, op1=ALU.add)
    (Hf, bHf), (Hb, bHb) = H
    S.op("dve", "tensor_tensor", reads=[bHf, bHb], writes=[bHf], out=Hf[:], in0=Hf[:], in1=Hb[:], op=ALU.add)
    Z = T1
    S.dma("sp", Z[:, 0:TALL], z_in[0], reads=[z_in[1]], writes=[bT1])
    for c0 in range(0, TALL, 2112):
        n = 2112
        sl = slice(c0, c0 + n)
        S.op("dve", "tensor_tensor", reads=[bT1], writes=[bA], out=A[:, sl], in0=Z[:, sl], in1=Z[:, sl], op=ALU.mult)
        S.op("dve", "tensor_scalar", reads=[bA], writes=[bA], out=A[:, sl], in0=A[:, sl], scalar1=0.044715, scalar2=1.0, op0=ALU.mult, op1=ALU.add)
        S.op("dve", "tensor_tensor", reads=[bA, bT1], writes=[bA], out=A[:, sl], in0=A[:, sl], in1=Z[:, sl], op=ALU.mult)
        S.op("act", "activation", reads=[bA], writes=[bA], out=A[:, sl], in_=A[:, sl], func=AF.Sigmoid, scale=1.5957691216057308)
        S.op("dve", "tensor_tensor", reads=[bA, bT1], writes=[bA], out=A[:, sl], in0=A[:, sl], in1=Z[:, sl], op=ALU.mult)
        S.op("dve", "tensor_tensor", reads=[bA, bHf], writes=[bHb], out=Hb[:, sl], in0=A[:, sl], in1=Hf[:, sl], op=ALU.mult)
    S.dma("sp", o_out[0], Hb[:], reads=[bHb])
    return P


def run_k2b(P, zl, zc, rg_conv, w_rg, b_rg, rg_lambda):
    maps = []
    for n in range(NCORES):
        cs = slice(n * 64, (n + 1) * 64)
        x3 = np.concatenate([zc[:, 1536:2048][:, cs], zl[:, 1536:2048][:, cs]], 0).T
        z4 = np.concatenate([zc[:, 2048:2560][:, cs], zl[:, 2048:2560][:, cs]], 0).T
        wr = w_rg[:, :, n].reshape(4, 64, 64).transpose(1, 0, 2)
        maps.append({"x3": np.ascontiguousarray(x3), "z4": np.ascontiguousarray(z4), "taps": np.ascontiguousarray(rg_conv[:, cs].T),
                     "wrg": np.ascontiguousarray(wr), "brg": np.ascontiguousarray(b_rg.reshape(4, 512)[:, cs].T),
                     "lam": np.ascontiguousarray(rg_lambda[:, cs].T)})
    res = P.run(maps)
    o = np.concatenate([r["o"] for r in res], 0).T
    return np.ascontiguousarray(o[CTX:]), np.ascontiguousarray(o[:CTX])


CW = 1116
CWO = CW - 30


def build_k2c():
    P = Prog()
    S = P.S
    v_in = P.inp("val", [128, 4, CW])
    g_in = P.inp("gt", [128, 4, CW])
    cw_in = P.inp("cw", [128, 4, 31])
    lg_in = P.inp("lng", [128, 4])
    lb_in = P.inp("lnb", [128, 4])
    o_out = P.out("o", [128, 4, CWO])
    val, bval = P.load(v_in, [128, 4, CW])
    gt, bgt = P.load(g_in, [128, 4, CW])
    cw, bcw = P.load(cw_in, [128, 4, 31])
    lng, blg = P.load(lg_in, [128, 4])
    lnb, blb = P.load(lb_in, [128, 4])
    ones, bones = P.sb([128, 128])
    S.op("dve", "memset", writes=[bones], ap=ones[:], constant=1.0 / 512.0)
    ep, bep = P.sb([128, 1])
    S.op("dve", "memset", writes=[bep], ap=ep[:], constant=EPS)
    S.op("act", "activation", reads=[bgt], writes=[bgt], out=gt[:], in_=gt[:], func=AF.Sigmoid)
    Y = [P.sb([128, CW]) for _ in range(4)]
    O = [P.sb([128, CWO]) for _ in range(4)]
    for c in range(4):
        eng = "dve"
        S.op(eng, "tensor_tensor", reads=[bval, bgt], writes=[Y[c][1]], out=Y[c][0][:], in0=val[:, c, :], in1=gt[:, c, :], op=ALU.mult)
        S.op(eng, "tensor_scalar", reads=[Y[c][1], bcw], writes=[O[c][1]], out=O[c][0][:], in0=Y[c][0][:, 0:CWO], scalar1=cw[:, c, 0:1], scalar2=None, op0=ALU.mult)
        for k in range(1, 31):
            S.op(eng, "scalar_tensor_tensor", reads=[Y[c][1], bcw, O[c][1]], writes=[O[c][1]], out=O[c][0][:], in0=Y[c][0][:, k:k + CWO],
                 scalar=cw[:, c, k:k + 1], in1=O[c][0][:], op0=ALU.mult, op1=ALU.add)
    psm, bpsm = P.ps()
    psq, bpsq = P.ps()
    sq = [P.sb([128, 512]) for _ in range(2)]
    st, bst = P.sb([128, 4, 512])
    res, bres = P.sb([128, 4, CWO])
    for (c0, n) in ((0, 512), (512, 512), (1024, CWO - 1024)):
        for c in range(4):
            S.op("pe", "matmul", reads=[bones, O[c][1]], writes=[bpsm], out=psm[:, :n], lhsT=ones[:], rhs=O[c][0][:, c0:c0 + n], start=(c == 0), stop=(c == 3))
        for c in range(4):
            s_, bs_ = sq[c % 2]
            S.op("act", "activation", reads=[O[c][1]], writes=[bs_], out=s_[:, :n], in_=O[c][0][:, c0:c0 + n], func=AF.Square)
            S.op("pe", "matmul", reads=[bones, bs_], writes=[bpsq], out=psq[:, :n], lhsT=ones[:], rhs=s_[:, :n], start=(c == 0), stop=(c == 3))
        S.op("act", "copy", reads=[bpsm], writes=[bst], out=st[:, 0, :n], in_=psm[:, :n])
        S.op("act", "activation", reads=[bst], writes=[bst], out=st[:, 1, :n], in_=st[:, 0, :n], func=AF.Square)
        S.op("dve", "tensor_tensor", reads=[bpsq, bst], writes=[bst], out=st[:, 2, :n], in0=psq[:, :n], in1=st[:, 1, :n], op=ALU.subtract)
        S.op("act", "activation", reads=[bst, bep], writes=[bst], out=st[:, 2, :n], in_=st[:, 2, :n], func=AF.Sqrt, bias=ep[:], scale=1.0)
        S.op("dve", "reciprocal", reads=[bst], writes=[bst], out=st[:, 2, :n], in_=st[:, 2, :n])
        for c in range(4):
            s_, bs_ = sq[c % 2]
            S.op("dve", "tensor_tensor", reads=[O[c][1], bst], writes=[bs_], out=s_[:, :n], in0=O[c][0][:, c0:c0 + n], in1=st[:, 0, :n], op=ALU.subtract)
            S.op("dve", "tensor_tensor", reads=[bs_, bst], writes=[bs_], out=s_[:, :n], in0=s_[:, :n], in1=st[:, 2, :n], op=ALU.mult)
            S.op("act", "activation", reads=[bs_, blg, blb], writes=[bres], out=res[:, c, c0:c0 + n], in_=s_[:, :n], func=AF.Silu, bias=lnb[:, c:c + 1], scale=lng[:, c:c + 1])
    S.dma("sp", o_out[0], res[:], reads=[bres])
    return P


def halo_rows(a, s, n, h):
    out = np.zeros((n + 2 * h, a.shape[1]), a.dtype)
    lo, hi = max(s - h, 0), min(s + n + h, a.shape[0])
    out[lo - (s - h):hi - (s - h)] = a[lo:hi]
    return out


def run_k2c(P, zl, zc, conf_dw, ln_g, ln_b):
    maps = []
    cwf = np.ascontiguousarray(conf_dw.T.reshape(4, 128, 31).transpose(1, 0, 2))
    for i in range(NCORES):
        z = np.concatenate([halo_rows(zl[:, 2560:3584], i * 1024, 1024, 15), halo_rows(zc[:, 2560:3584], i * 32, 32, 15)], 0)
        maps.append({"val": fm(np.ascontiguousarray(z[:, :512].T)), "gt": fm(np.ascontiguousarray(z[:, 512:].T)), "cw": cwf,
                     "lng": vec_fm(ln_g), "lnb": vec_fm(ln_b)})
    res = P.run(maps)
    lat, ctx = [], []
    for i in range(NCORES):
        o = unfm(res[i]["o"]).T
        lat.append(o[:1024])
        ctx.append(o[1054:1086])
    return np.concatenate(lat, 0), np.concatenate(ctx, 0)


T3 = 528
BLK3 = [(0, 512, 0), (512, 16, 1)]


def build_k3():
    P = Prog()
    S = P.S
    x_in = P.inp("xT", [128, KC, T3])
    g_in = P.inp("g", [128, KC])
    sc_in = P.inp("sc", [128, KC, 2])
    sh_in = P.inp("sh", [128, KC, 2])
    ga_in = P.inp("gate", [128, KC, 2])
    br_in = P.inp("brT", [128, 4, 4, T3])
    wg_in = P.inp("wg", [16, 128, KC, 512], BF16)
    wb_in = P.inp("wb", [16, 128, 4, 512], BF16)
    wo_in = P.inp("wo", [4, 128, KC, 512], BF16)
    o_out = P.out("xmT", [D, T3])
    ones, bones = make_consts(P)
    xT, bx = P.load(x_in, [128, KC, T3])
    ga, bga = P.load(ga_in, [128, KC, 2])
    brT, bbr = P.load(br_in, [128, 4, 4, T3], BF16, eng="pool")
    hT, bh = P.sb([128, KC, T3], BF16)
    psums = [P.ps() for _ in range(4)]
    emit_norm_mod(P, xT, bx, hT, bh, BLK3, g_in, sc_in, sh_in, ones, bones, psums)
    yT, byT = P.sb([128, KC, T3], BF16)
    acc = [P.sb([128, T3]) for _ in range(4)]
    wgt = [P.sb([128, KC, 512], BF16) for _ in range(2)]
    wbt = [P.sb([128, 4, 512], BF16) for _ in range(2)]
    sg = [P.sb([128, 512]) for _ in range(2)]
    psb = [P.ps() for _ in range(2)]
    it = 0
    pi = 0
    for fg in range(4):
        for n in range(4):
            (wgg, bwg), (wbb, bwb) = wgt[it % 2], wbt[it % 2]
            it += 1
            S.dma("sp", wgg[:], wg_in[0][n * 4 + fg], reads=[wg_in[1]], writes=[bwg])
            S.dma("sp", wbb[:], wb_in[0][n * 4 + fg], reads=[wb_in[1]], writes=[bwb])
            for (c0, nn, kind) in BLK3:
                for f in range(4):
                    (pg, bpg), (pb, bpb), (sgg, bsg) = psums[pi % 4], psb[pi % 2], sg[pi % 2]
                    pi += 1
                    for k in range(KC):
                        S.op("pe", "matmul", reads=[bwg, bh], writes=[bpg], out=pg[:, :nn], lhsT=wgg[:, k, f * 128:(f + 1) * 128], rhs=hT[:, k, c0:c0 + nn], start=(k == 0), stop=(k == KC - 1))
                    for k in range(4):
                        S.op("pe", "matmul", reads=[bwb, bbr], writes=[bpb], out=pb[:, :nn], lhsT=wbb[:, k, f * 128:(f + 1) * 128], rhs=brT[:, n, k, c0:c0 + nn], start=(k == 0), stop=(k == 3))
                    S.op("act", "activation", reads=[bpg], writes=[bsg], out=sgg[:, :nn], in_=pg[:, :nn], func=AF.Sigmoid)
                    a_, ba_ = acc[f]
                    if n == 0:
                        S.op("dve", "tensor_tensor", reads=[bsg, bpb], writes=[ba_], out=a_[:, c0:c0 + nn], in0=sgg[:, :nn], in1=pb[:, :nn], op=ALU.mult)
                    else:
                        S.op("dve", "tensor_tensor", reads=[bsg, bpb], writes=[bsg], out=sgg[:, :nn], in0=sgg[:, :nn], in1=pb[:, :nn], op=ALU.mult)
                        S.op("pool", "tensor_tensor", reads=[bsg, ba_], writes=[ba_], out=a_[:, c0:c0 + nn], in0=a_[:, c0:c0 + nn], in1=sgg[:, :nn], op=ALU.add)
        for f in range(4):
            S.op("act", "copy", reads=[acc[f][1]], writes=[byT], out=yT[:, fg * 4 + f, :], in_=acc[f][0][:])
    stg = [P.sb([128, 512]) for _ in range(3)]
    cnt = [0]

    def evac(m, bi, ps, bps, c0, n, kind):
        st, bst = stg[cnt[0] % 3]
        cnt[0] += 1
        S.op("dve", "scalar_tensor_tensor", reads=[bps, bga, bx], writes=[bst], out=st[:, :n], in0=ps[:, :n], scalar=ga[:, m, kind:kind + 1], in1=xT[:, m, c0:c0 + n], op0=ALU.mult, op1=ALU.add)
        S.dma("pool", o_out[0][m * 128:(m + 1) * 128, c0:c0 + n], st[:, :n], reads=[bst])

    emit_proj(P, wo_in, D, 0, yT, byT, KC, BLK3, evac, psums)
    return P


def run_k3(P, x_lat, x_ctx, br_lat, br_ctx, g, mod_l, w_in_b, w_branch_b, w_out_b):
    gf = vec_fm(g)
    sc, sh, ga = mod_fm(mod_l, 1), mod_fm(mod_l, 0), mod_fm(mod_l, 2)
    wg = tile_w(w_in_b[:, NIN1:], 512)
    wb = np.ascontiguousarray(np.concatenate([tile_w(w_branch_b[n], 512) for n in range(4)], 0))
    wo = tile_w(w_out_b, 512)
    lat_out = np.zeros((SEQ, D), np.float32)
    ctx_out = np.zeros((CTX, D), np.float32)
    for hf in range(2):
        maps = []
        for i in range(NCORES):
            ls = slice(i * 1024 + hf * 512, i * 1024 + hf * 512 + 512)
            cs = slice(i * 32 + hf * 16, i * 32 + hf * 16 + 16)
            xs = np.concatenate([x_lat[ls], x_ctx[cs]], 0).T
            br = np.stack([np.concatenate([br_lat[n][ls], br_ctx[n][cs]], 0).T for n in range(4)], 0)
            brf = np.ascontiguousarray(br.reshape(4, 4, 128, T3).transpose(2, 0, 1, 3))
            maps.append({"xT": fm(np.ascontiguousarray(xs)), "g": gf, "sc": sc, "sh": sh, "gate": ga, "brT": brf, "wg": wg, "wb": wb, "wo": wo})
        res = P.run(maps)
        for i in range(NCORES):
            o = res[i]["xmT"].T
            lat_out[i * 1024 + hf * 512:i * 1024 + hf * 512 + 512] = o[:512]
            ctx_out[i * 32 + hf * 16:i * 32 + hf * 16 + 16] = o[512:]
    return lat_out, ctx_out


SEGS = [(0, 512, 0), (512, 512, 0), (0, 32, 1)]
SEG_C0 = [0, 514, 1028]
T4 = 1062
PASSES = [dict(c0=0, W=514, blocks=[(0, 257, 0), (257, 257, 0)]),
          dict(c0=514, W=548, blocks=[(0, 257, 0), (257, 257, 0), (514, 34, 1)])]
WMAX = 548


def build_k4():
    P = Prog()
    S = P.S
    x_in = P.inp("xT", [128, KC, T4])
    g_in = P.inp("g", [128, KC])
    sc_in = P.inp("sc", [128, KC, 2])
    sh_in = P.inp("sh", [128, KC, 2])
    ga_in = P.inp("gate", [128, KC, 2])
    tap_in = P.inp("taps", [128, 88, 3])
    mask_in = P.inp("mask", [128, T4])
    wu_in = P.inp("wu", [44, 128, KC, 256], BF16)
    wd_in = P.inp("wd", [8, 4, 128, 11, 256], BF16)
    o_out = P.out("xoT", [D, T4])
    ones, bones = make_consts(P)
    ga, bga = P.load(ga_in, [128, KC, 2])
    taps, btaps = P.load(tap_in, [128, 88, 3])
    mask, bmask = P.load(mask_in, [128, T4])
    C = norm_setup(P, g_in, sc_in, sh_in)
    xT, bx = P.sb([128, KC, WMAX])
    hT, bh = P.sb([128, KC, WMAX], BF16)
    actT, bact = P.sb([128, 44, WMAX], BF16)
    psums = [P.ps() for _ in range(8)]
    wut = [P.sb([128, KC, 256], BF16) for _ in range(4)]
    wdt = [P.sb([128, 11, 256], BF16) for _ in range(2)]
    ug = [P.sb([128, WMAX]) for _ in range(2)]
    uv = [P.sb([128, WMAX]) for _ in range(2)]
    cg = [P.sb([128, WMAX]) for _ in range(2)]
    cv = [P.sb([128, WMAX]) for _ in range(2)]
    stg = [P.sb([128, 257]) for _ in range(3)]
    it = 0
    pi = 0
    mi_ = 0
    sc_ = 0
    for ps_ in PASSES:
        p0, W, blocks = ps_["c0"], ps_["W"], ps_["blocks"]
        n = W - 2
        S.dma("sp", xT[:, :, :W], x_in[0][:, :, p0:p0 + W], reads=[x_in[1]], writes=[bx])
        norm_run(P, C, xT, bx, hT, bh, blocks, ones, bones, psums)
        for c in range(KC):
            S.op("dve", "tensor_tensor", reads=[bh, bmask], writes=[bh], out=hT[:, c, :W], in0=hT[:, c, :W], in1=mask[:, p0:p0 + W], op=ALU.mult)
        S.op("pool", "memset", writes=[bact], ap=actT[:], constant=0.0)
        for mg in range(22):
            (wg_, bwg), (wv_, bwv) = wut[(it % 2) * 2], wut[(it % 2) * 2 + 1]
            it += 1
            S.dma("sp", wg_[:], wu_in[0][mg], reads=[wu_in[1]], writes=[bwg])
            S.dma("sp", wv_[:], wu_in[0][22 + mg], reads=[wu_in[1]], writes=[bwv])
            for mm in range(2):
                m = mg * 2 + mm
                (ug_, bug), (uv_, buv), (g_, bg_), (v_, bv_) = ug[mi_ % 2], uv[mi_ % 2], cg[mi_ % 2], cv[mi_ % 2]
                mi_ += 1
                for (c0, nn, kind) in blocks:
                    (pg, bpg), (pv, bpv) = psums[(pi % 4) * 2], psums[(pi % 4) * 2 + 1]
                    pi += 1
                    for k in range(KC):
                        S.op("pe", "matmul", reads=[bwg, bh], writes=[bpg], out=pg[:, :nn], lhsT=wg_[:, k, mm * 128:(mm + 1) * 128], rhs=hT[:, k, c0:c0 + nn], start=(k == 0), stop=(k == KC - 1))
                    for k in range(KC):
                        S.op("pe", "matmul", reads=[bwv, bh], writes=[bpv], out=pv[:, :nn], lhsT=wv_[:, k, mm * 128:(mm + 1) * 128], rhs=hT[:, k, c0:c0 + nn], start=(k == 0), stop=(k == KC - 1))
                    S.op("act", "copy", reads=[bpg], writes=[bug], out=ug_[:, c0:c0 + nn], in_=pg[:, :nn])
                    S.op("act", "copy", reads=[bpv], writes=[buv], out=uv_[:, c0:c0 + nn], in_=pv[:, :nn])
                for (u_, bu, o_, bo_, tm) in ((ug_, bug, g_, bg_, m), (uv_, buv, v_, bv_, 44 + m)):
                    S.op("dve", "tensor_scalar", reads=[bu, btaps], writes=[bo_], out=o_[:, :n], in0=u_[:, 0:n], scalar1=taps[:, tm, 0:1], scalar2=None, op0=ALU.mult)
                    for t in (1, 2):
                        S.op("dve", "scalar_tensor_tensor", reads=[bu, btaps, bo_], writes=[bo_], out=o_[:, :n], in0=u_[:, t:t + n], scalar=taps[:, tm, t:t + 1], in1=o_[:, :n], op0=ALU.mult, op1=ALU.add)
                S.op("act", "activation", reads=[bg_], writes=[bg_], out=g_[:, :n], in_=g_[:, :n], func=AF.Silu)
                S.op("dve", "tensor_tensor", reads=[bg_, bv_], writes=[bact], out=actT[:, m, 1:1 + n], in0=g_[:, :n], in1=v_[:, :n], op=ALU.mult)
        for fg in range(8):
            for kg in range(4):
                wd_, bwd = wdt[it % 2]
                it += 1
                S.dma("sp", wd_[:], wd_in[0][fg, kg], reads=[wd_in[1]], writes=[bwd])
                for f in range(2):
                    for bi, (c0, nn, kind) in enumerate(blocks):
                        pf, bpf = psums[f * 3 + bi]
                        for mi in range(11):
                            S.op("pe", "matmul", reads=[bwd, bact], writes=[bpf], out=pf[:, :nn], lhsT=wd_[:, mi, f * 128:(f + 1) * 128], rhs=actT[:, kg * 11 + mi, c0:c0 + nn],
                                 start=(kg == 0 and mi == 0), stop=(kg == 3 and mi == 10))
            for f in range(2):
                mfe = fg * 2 + f
                for bi, (c0, nn, kind) in enumerate(blocks):
                    pf, bpf = psums[f * 3 + bi]
                    st, bst = stg[sc_ % 3]
                    sc_ += 1
                    S.op("dve", "scalar_tensor_tensor", reads=[bpf, bga, bx], writes=[bst], out=st[:, :nn], in0=pf[:, :nn], scalar=ga[:, mfe, kind:kind + 1], in1=xT[:, mfe, c0:c0 + nn], op0=ALU.mult, op1=ALU.add)
                    S.dma("pool", o_out[0][mfe * 128:(mfe + 1) * 128, p0 + c0:p0 + c0 + nn], st[:, :nn], reads=[bst])
    return P


def run_k4(P, x_lat, x_ctx, g, mod_l, w_up_b, ffn_dw_l, w_down_b):
    gf = vec_fm(g)
    sc, sh, ga = mod_fm(mod_l, 4), mod_fm(mod_l, 3), mod_fm(mod_l, 5)
    taps = np.ascontiguousarray(ffn_dw_l.T.reshape(88, 128, 3).transpose(1, 0, 2))
    wu = tile_w(w_up_b, 256)
    wd = np.ascontiguousarray(w_down_b.reshape(4, 11, 128, 8, 256).transpose(3, 0, 2, 1, 4))
    maps = []
    for i in range(NCORES):
        cols, mk = [], []
        for (s, n, kind) in SEGS:
            src, t0 = (x_lat, i * 1024 + s) if kind == 0 else (x_ctx, i * 32 + s)
            cols.append(halo_rows(src, t0, n, 1))
            pos = np.arange(t0 - 1, t0 + n + 1)
            mk.append(((pos >= 0) & (pos < src.shape[0])).astype(np.float32))
        xs = np.concatenate(cols, 0).T
        mask = np.ascontiguousarray(np.broadcast_to(np.concatenate(mk)[None, :], (128, T4)))
        maps.append({"xT": fm(np.ascontiguousarray(xs)), "g": gf, "sc": sc, "sh": sh, "gate": ga, "taps": taps, "mask": mask, "wu": wu, "wd": wd})
    res = P.run(maps)
    lat = np.zeros((SEQ, D), np.float32)
    ctx = np.zeros((CTX, D), np.float32)
    for i in range(NCORES):
        o = res[i]["xoT"].T
        for si, (s, n, kind) in enumerate(SEGS):
            seg = o[SEG_C0[si] + 1:SEG_C0[si] + 1 + n]
            if kind == 0:
                lat[i * 1024 + s:i * 1024 + s + n] = seg
            else:
                ctx[i * 32 + s:i * 32 + s + n] = seg
    return lat, ctx


def build_k5():
    P = Prog()
    S = P.S
    x_in = P.inp("xT", [128, KC, 1024])
    g_in = P.inp("g", [128, KC])
    o_out = P.out("oT", [128, KC, 1024])
    ones, bones = make_consts(P)
    xT, bx = P.load(x_in, [128, KC, 1024])
    g, bg = P.load(g_in, [128, KC])
    ps, bps = P.ps()
    sq = [P.sb([128, 512]) for _ in range(2)]
    rstd, brs = P.sb([128, 512])
    res, bres = P.sb([128, KC, 1024])
    for c0 in (0, 512):
        for c in range(KC):
            s_, bs_ = sq[c % 2]
            S.op("act", "activation", reads=[bx], writes=[bs_], out=s_[:], in_=xT[:, c, c0:c0 + 512], func=AF.Square)
            S.op("pe", "matmul", reads=[bones, bs_], writes=[bps], out=ps[:], lhsT=ones[:], rhs=s_[:], start=(c == 0), stop=(c == KC - 1))
        S.op("act", "activation", reads=[bps, epsb[1]], writes=[brs], out=rstd[:], in_=ps[:], func=AF.Sqrt, bias=epsb[0][:], scale=1.0 / D)
        S.op("dve", "reciprocal", reads=[brs], writes=[brs], out=rstd[:], in_=rstd[:])
        for c in range(KC):
            S.op("dve", "scalar_tensor_tensor", reads=[bx, bg, brs], writes=[bres], out=res[:, c, c0:c0 + 512], in0=xT[:, c, c0:c0 + 512], scalar=g[:, c:c + 1], in1=rstd[:], op0=ALU.mult, op1=ALU.mult)
    S.dma("sp", o_out[0], res[:], reads=[bres])
    return P


def run_k5(P, x_lat, g):
    gf = vec_fm(g)
    maps = [{"xT": fm(np.ascontiguousarray(x_lat[i * 1024:(i + 1) * 1024].T)), "g": gf} for i in range(NCORES)]
    res = P.run(maps)
    return np.concatenate([unfm(r["oT"]).T for r in res], 0)


KW_ROWS = 8384


def build_kw():
    P = Prog()
    S = P.S
    w_in = P.inp("w", [KW_ROWS, 2048])
    o = P.out("wb", [KW_ROWS, 2048], BF16)
    for r0 in range(0, KW_ROWS, 512):
        r1 = min(r0 + 512, KW_ROWS)
        S.dma("pool", o[0][r0:r1], w_in[0][r0:r1], reads=[w_in[1]], max_dma_last_dim=4096)
    return P


def run_kw(P, mats):
    flat = np.concatenate([m.reshape(-1) for m in mats]).reshape(-1, 2048)
    assert flat.shape[0] == KW_ROWS * NCORES
    res = P.run([{"w": flat[i * KW_ROWS:(i + 1) * KW_ROWS]} for i in range(NCORES)])
    fb = np.concatenate([r["wb"] for r in res], 0).reshape(-1)
    out, o0 = [], 0
    for m in mats:
        out.append(fb[o0:o0 + m.size].reshape(m.shape))
        o0 += m.size
    return out


_PROGS = {}


def _prog(name, builder):
    if name not in _PROGS:
        _PROGS[name] = builder()
    return _PROGS[name]


def kernel(x, c, ctx, c_ctx, w_ada, b_ada, g_mix, w_in, na_rpb, rg_conv, w_rg, b_rg, rg_lambda,
           conf_dw, conf_ln_g, conf_ln_b, q_norm_g, k_norm_g, w_branch, w_out, g_ffn, w_up,
           ffn_dw, w_down, g_final):
    f = lambda a: np.ascontiguousarray(np.asarray(a, dtype=np.float32))
    x, c, ctx, c_ctx, w_ada, b_ada, g_mix, w_in, na_rpb, rg_conv, w_rg, b_rg, rg_lambda = map(f, (x, c, ctx, c_ctx, w_ada, b_ada, g_mix, w_in, na_rpb, rg_conv, w_rg, b_rg, rg_lambda))
    conf_dw, conf_ln_g, conf_ln_b, q_norm_g, k_norm_g, w_branch, w_out, g_ffn, w_up, ffn_dw, w_down, g_final = map(
        f, (conf_dw, conf_ln_g, conf_ln_b, q_norm_g, k_norm_g, w_branch, w_out, g_ffn, w_up, ffn_dw, w_down, g_final))
    xl, xc = x[0], ctx[0]
    depth = 2
    mats = []
    for l in range(depth):
        mats += [w_in[l], w_branch[l], w_out[l], w_up[l], w_down[l]]
    wb = run_kw(_prog("kw", build_kw), mats)
    mod = run_k0(_prog("k0", build_k0), c, c_ctx, w_ada, b_ada)
    for l in range(depth):
        w_in_b, w_br_b, w_out_b, w_up_b, w_dn_b = wb[5 * l:5 * l + 5]
        zl, zc = run_k1(_prog("k1", build_k1), xl, xc, g_mix[l], mod[l], w_in_b)
        al, ac = run_k2a(_prog("k2a", build_k2a), zl, zc, na_rpb[l])
        bl, bc = run_k2b(_prog("k2b", build_k2b), zl, zc, rg_conv[l], w_rg[l], b_rg[l], rg_lambda[l])
        cl, cc = run_k2c(_prog("k2c", build_k2c), zl, zc, conf_dw[l], conf_ln_g[l], conf_ln_b[l])
        dl, dc = run_k2d(_prog("k2d", build_k2d), zl, zc, q_norm_g[l], k_norm_g[l])
        xm, cm = run_k3(_prog("k3", build_k3), xl, xc, [al, bl, cl, dl], [ac, bc, cc, dc], g_mix[l], mod[l], w_in_b, w_br_b, w_out_b)
        xl, xc2 = run_k4(_prog("k4", build_k4), xm, cm, g_ffn[l], mod[l], w_up_b, ffn_dw[l], w_dn_b)
        if l < depth - 1:
            xc = xc2
    out = run_k5(_prog("k5", build_k5), xl, g_final)
    return np.ascontiguousarray(out[None].astype(np.float32))
```

```python
import numpy as np
import concourse.bass as bass
import concourse.mybir as mybir
from concourse.bass_utils import run_bass_kernel_spmd

F32 = mybir.dt.float32
BF16 = mybir.dt.bfloat16
AF = mybir.ActivationFunctionType
ALU = mybir.AluOpType
AX = mybir.AxisListType

NCORES = 8
D = 2048
KC = 16
SEQ = 8192
CTX = 256
GW = 64
DFF = 5632
NIN1 = 4352
EPS = 1e-6
NEG = -30000.0
TRACE = False
TRACE_TAG = [""]


class Buf:
    __slots__ = ("w", "r", "excl")

    def __init__(self, excl=False):
        self.w = None
        self.r = {}
        self.excl = excl


class Sync:
    ENG = ("pe", "act", "dve", "pool", "sp")

    def __init__(self, nc, n_dma_sems=16):
        self.nc = nc
        self.q = {e: [] for e in self.ENG}
        self.sem = {e: nc.alloc_semaphore(name="S_" + e) for e in self.ENG}
        self.cnt = {e: 0 for e in self.ENG}
        self.waited = {e: {} for e in self.ENG}
        self.dsem = {}
        self.dcnt = {}
        self.drr = {}
        for e in ("sp", "pool", "act"):
            self.dsem[e] = [nc.alloc_semaphore(name="D_%s%d" % (e, i)) for i in range(n_dma_sems)]
            self.drr[e] = 0
            for s in self.dsem[e]:
                self.dcnt[s] = 0

    def _deps(self, reads, writes):
        deps = {}

        def add(s, v):
            if deps.get(s, 0) < v:
                deps[s] = v
        for b in reads:
            if b.w is not None:
                add(*b.w)
        for b in writes:
            if b.w is not None:
                add(*b.w)
            for s, v in b.r.items():
                add(s, v)
        return deps

    def _emit_waits(self, eng, deps):
        q = self.q[eng]
        for s, v in deps.items():
            if eng == "pe" and s is self.sem["pe"]:
                continue
            if self.waited[eng].get(s, 0) >= v:
                continue
            self.waited[eng][s] = v
            q.append(("w", s, v))

    def _record(self, me, reads, writes):
        s, v = me
        for b in reads:
            if b.r.get(s, 0) < v:
                b.r[s] = v
        for b in writes:
            b.w = me
            b.r = {}

    def op(self, eng, fn, reads=(), writes=(), **kw):
        if isinstance(fn, str):
            name = fn
            fn = lambda h, name=name, kw=kw: getattr(h, name)(**kw)
        ex = [b for b in reads if b.excl]
        if ex:
            reads = [b for b in reads if not b.excl]
            writes = list(writes) + ex
        deps = self._deps(reads, writes)
        self._emit_waits(eng, deps)
        self.cnt[eng] += 1
        me = (self.sem[eng], self.cnt[eng])
        self.q[eng].append(("i", fn, self.sem[eng], 1))
        self._record(me, reads, writes)

    def dma(self, eng, out, in_, reads=(), writes=(), **kw):
        sems = self.dsem[eng]
        s = sems[self.drr[eng] % len(sems)]
        self.drr[eng] += 1
        deps = self._deps(reads, writes)
        if self.dcnt[s] > 0 and deps.get(s, 0) < self.dcnt[s]:
            deps[s] = self.dcnt[s]
        self._emit_waits(eng, deps)
        self.dcnt[s] += 16
        me = (s, self.dcnt[s])
        self.q[eng].append(("i", lambda e, o=out, i=in_, k=kw: e.dma_start(out=o, in_=i, **k), s, 16))
        self._record(me, reads, writes)

    def finish(self):
        q = self.q["sp"]
        for s, v in self.dcnt.items():
            if v > 0:
                q.append(("w", s, v))
        for e in ("pe", "act", "dve", "pool"):
            if self.cnt[e] > 0:
                q.append(("w", self.sem[e], self.cnt[e]))
        qs = self.q

        def replay(h, items):
            for it in items:
                if it[0] == "w":
                    h.wait_ge(it[1], it[2])
                else:
                    it[1](h).then_inc(it[2], it[3])

        with self.nc.Block() as block:
            @block.sync
            def _(e):
                replay(e, qs["sp"])

            @block.tensor
            def _(e):
                replay(e, qs["pe"])

            @block.scalar
            def _(e):
                replay(e, qs["act"])

            @block.vector
            def _(e):
                replay(e, qs["dve"])

            @block.gpsimd
            def _(e):
                replay(e, qs["pool"])


class Prog:
    def __init__(self):
        self.nc = bass.Bass("TRN2", target_bir_lowering=False)
        self.S = Sync(self.nc)
        self._n = 0
        self.done = False
        self.psr = 0

    def inp(self, name, shape, dt=F32):
        return self.nc.dram_tensor(name, list(shape), dt, kind="ExternalInput").ap(), Buf()

    def out(self, name, shape, dt=F32):
        return self.nc.dram_tensor(name, list(shape), dt, kind="ExternalOutput").ap(), Buf()

    def sb(self, shape, dt=F32):
        self._n += 1
        return self.nc.alloc_sbuf_tensor("t%d" % self._n, list(shape), dt), Buf()

    def ps(self, shape=(128, 512), dt=F32):
        self._n += 1
        return self.nc.alloc_psum_tensor("p%d" % self._n, list(shape), dt), Buf(excl=True)

    def load(self, dram, shape, dt=F32, eng="sp", view=None):
        t, b = self.sb(shape, dt)
        kw = {"max_dma_last_dim": 4096} if eng == "pool" else {}
        self.S.dma(eng, t[:] if view is None else view(t), dram[0], reads=[dram[1]], writes=[b], **kw)
        return t, b

    def run(self, in_maps):
        global N_LAUNCH
        if not self.done:
            self.S.finish()
            self.done = True
        if TRACE:
            res = run_bass_kernel_spmd(self.nc, in_maps, core_ids=list(range(NCORES)), trace=True)
            print("EXEC_NS", TRACE_TAG[0], res.exec_time_ns, flush=True)
        else:
            res = run_bass_kernel_spmd(self.nc, in_maps, core_ids=list(range(NCORES)))
        return res.results


def fm(a, p=128):
    r, t = a.shape
    return np.ascontiguousarray(a.reshape(r // p, p, t).transpose(1, 0, 2))


def unfm(a):
    p, c, t = a.shape
    return a.transpose(1, 0, 2).reshape(c * p, t)


def vec_fm(v, p=128):
    return np.ascontiguousarray(v.reshape(-1, p).T)


def norm_setup(P, g_in, sc_in, sh_in):
    S = P.S
    g, bg = P.load(g_in, [128, KC])
    sc, bsc = P.load(sc_in, [128, KC, 2])
    sh, bsh = P.load(sh_in, [128, KC, 2])
    gs, bgs = P.sb([128, KC, 2])
    S.op("dve", "tensor_scalar", reads=[bsc], writes=[bgs], out=gs[:], in0=sc[:], scalar1=1.0, scalar2=None, op0=ALU.add)
    for j in range(2):
        S.op("dve", "tensor_tensor", reads=[bgs, bg], writes=[bgs], out=gs[:, :, j], in0=gs[:, :, j], in1=g[:], op=ALU.mult)
    sq, bsq = P.sb([128, 2, 512])
    rstd, brs = P.sb([128, 512])
    tmp, btmp = P.sb([128, 2, 512])
    return dict(gs=gs, bgs=bgs, sh=sh, bsh=bsh, sq=sq, bsqs=[bsq, Buf()], rstd=rstd, brs=brs, tmp=tmp, btmps=[btmp, Buf()])


def norm_run(P, C, xT, bx, hT, bh, blocks, ones, bones, psums):
    S = P.S
    gs, bgs, sh, bsh, sq, bsqs, rstd, brs, tmp, btmps = (C[k] for k in ("gs", "bgs", "sh", "bsh", "sq", "bsqs", "rstd", "brs", "tmp", "btmps"))
    for (c0, n, kind) in blocks:
        ps, bps = psums[0]
        for c in range(KC):
            S.op("act", "activation", reads=[bx], writes=[bsqs[c % 2]], out=sq[:, c % 2, :n], in_=xT[:, c, c0:c0 + n], func=AF.Square)
            S.op("pe", "matmul", reads=[bones, bsqs[c % 2]], writes=[bps], out=ps[:, :n], lhsT=ones[:], rhs=sq[:, c % 2, :n], start=(c == 0), stop=(c == KC - 1))
        S.op("act", "activation", reads=[bps, epsb[1]], writes=[brs], out=rstd[:, :n], in_=ps[:, :n], func=AF.Sqrt, bias=epsb[0][:], scale=1.0 / D)
        S.op("dve", "reciprocal", reads=[brs], writes=[brs], out=rstd[:, :n], in_=rstd[:, :n])
        for c in range(KC):
            S.op("dve", "tensor_tensor", reads=[bx, brs], writes=[btmps[c % 2]], out=tmp[:, c % 2, :n], in0=xT[:, c, c0:c0 + n], in1=rstd[:, :n], op=ALU.mult)
            S.op("act", "activation", reads=[btmps[c % 2], bgs, bsh], writes=[bh], out=hT[:, c, c0:c0 + n], in_=tmp[:, c % 2, :n], func=AF.Identity,
                 bias=sh[:, c, kind:kind + 1], scale=gs[:, c, kind:kind + 1])


def emit_norm_mod(P, xT, bx, hT, bh, blocks, g_in, sc_in, sh_in, ones, bones, psums):
    C = norm_setup(P, g_in, sc_in, sh_in)
    norm_run(P, C, xT, bx, hT, bh, blocks, ones, bones, psums)


epsb = [None, None]


def make_consts(P):
    S = P.S
    ones, bones = P.sb([128, 128])
    S.op("dve", lambda e: e.memset(ones[:], 1.0), writes=[bones])
    ep, bep = P.sb([128, 1])
    S.op("dve", lambda e: e.memset(ep[:], EPS), writes=[bep])
    epsb[0], epsb[1] = ep, bep
    return ones, bones


def tile_w(w, tw):
    k, f = w.shape
    return np.ascontiguousarray(w.reshape(k // 128, 128, f // tw, tw).transpose(2, 1, 0, 3))


def emit_proj(P, w_in, ncols, col0, hT, bh, kc_n, blocks, evac, psums, wtile=512):
    S = P.S
    wt = [P.sb([128, kc_n, wtile], BF16) for _ in range(2)]
    assert ncols % wtile == 0 and col0 % wtile == 0
    nt = ncols // wtile
    pi = 0
    for t in range(nt):
        w0 = t * wtile
        wtt, bwt = wt[t % 2]
        S.dma("sp", wtt[:], w_in[0][col0 // wtile + t], reads=[w_in[1]], writes=[bwt])
        for mm in range(wtile // 128):
            m = (w0 // 128) + mm
            for bi, (c0, n, kind) in enumerate(blocks):
                ps, bps = psums[pi % len(psums)]
                pi += 1
                for k in range(kc_n):
                    S.op("pe", "matmul", reads=[bwt, bh], writes=[bps], out=ps[:, :n], lhsT=wtt[:, k, mm * 128:(mm + 1) * 128], rhs=hT[:, k, c0:c0 + n], start=(k == 0), stop=(k == kc_n - 1))
                evac(m, bi, ps, bps, c0, n, kind)


def build_k0():
    P = Prog()
    S = P.S
    cc_in = P.inp("cc", [128, KC, 2])
    w_in = P.inp("w", [D, 3072])
    b_in = P.inp("b", [2, 3072])
    o = P.out("mod", [2, 3072])
    cc, bcc = P.load(cc_in, [128, KC, 2])
    sc, bsc = P.sb([128, KC, 2])
    S.op("act", lambda e: e.activation(out=sc[:], in_=cc[:], func=AF.Silu), reads=[bcc], writes=[bsc])
    b2, bb2 = P.load(b_in, [2, 3072])
    res, bres = P.sb([2, 3072])
    wt = [P.sb([128, KC, 512]) for _ in range(2)]
    pss = [P.ps() for _ in range(2)]
    wv = w_in[0].rearrange("(kc p) f -> p kc f", p=128)
    for nb in range(6):
        wtt, bwt = wt[nb % 2]
        S.dma("sp", wtt[:], wv[:, :, nb * 512:(nb + 1) * 512], reads=[w_in[1]], writes=[bwt])
        ps, bps = pss[nb % 2]
        for k in range(KC):
            S.op("pe", lambda e, ps=ps, wtt=wtt, k=k: e.matmul(ps[0:2, :], sc[:, k, :], wtt[:, k, :], start=(k == 0), stop=(k == KC - 1)), reads=[bsc, bwt], writes=[bps])
        S.op("dve", lambda e, ps=ps, nb=nb: e.tensor_tensor(out=res[:, nb * 512:(nb + 1) * 512], in0=ps[0:2, :], in1=b2[:, nb * 512:(nb + 1) * 512], op=ALU.add), reads=[bps, bb2], writes=[bres])
    S.dma("sp", o[0], res[:], reads=[bres])
    return P


def run_k0(P, c, c_ctx, w_ada, b_ada):
    cc = np.stack([vec_fm(c.reshape(-1)), vec_fm(c_ctx.reshape(-1))], axis=-1).astype(np.float32)
    maps = []
    for i in range(NCORES):
        l, c0 = i // 4, (i % 4) * 3072
        maps.append({"cc": cc, "w": np.ascontiguousarray(w_ada[l][:, c0:c0 + 3072]),
                     "b": np.ascontiguousarray(np.broadcast_to(b_ada[l][c0:c0 + 3072], (2, 3072)))})
    res = P.run(maps)
    mod = np.zeros((2, 2, 12288), np.float32)
    for i in range(NCORES):
        l, c0 = i // 4, (i % 4) * 3072
        mod[l][:, c0:c0 + 3072] = res[i]["mod"]
    return mod


def mod_fm(mod_l, j):
    a = mod_l[:, j * D:(j + 1) * D]
    return np.ascontiguousarray(np.stack([vec_fm(a[0]), vec_fm(a[1])], axis=-1))


TK = 1056
BLK1 = [(0, 512, 0), (512, 512, 0), (1024, 32, 1)]


def build_k1():
    P = Prog()
    S = P.S
    x_in = P.inp("xT", [128, KC, TK])
    g_in = P.inp("g", [128, KC])
    sc_in = P.inp("sc", [128, KC, 2])
    sh_in = P.inp("sh", [128, KC, 2])
    w_in = P.inp("w", [NIN1 // 256, 128, KC, 256], BF16)
    z_out = P.out("zT", [NIN1, TK])
    ones, bones = make_consts(P)
    xT, bx = P.load(x_in, [128, KC, TK])
    hT, bh = P.sb([128, KC, TK], BF16)
    psums = [P.ps() for _ in range(4)]
    emit_norm_mod(P, xT, bx, hT, bh, BLK1, g_in, sc_in, sh_in, ones, bones, psums)
    stg = [P.sb([128, 512]) for _ in range(3)]
    cnt = [0]

    def evac(m, bi, ps, bps, c0, n, kind):
        st, bst = stg[cnt[0] % 3]
        cnt[0] += 1
        if cnt[0] % 2:
            S.op("act", lambda e: e.copy(out=st[:, :n], in_=ps[:, :n]), reads=[bps], writes=[bst])
        else:
            S.op("dve", lambda e: e.tensor_copy(out=st[:, :n], in_=ps[:, :n]), reads=[bps], writes=[bst])
        S.dma("pool", z_out[0][m * 128:(m + 1) * 128, c0:c0 + n], st[:, :n], reads=[bst])

    emit_proj(P, w_in, NIN1, 0, hT, bh, KC, BLK1, evac, psums, wtile=256)
    return P


def shard_tokens_T(x_lat, x_ctx):
    outs = []
    for i in range(NCORES):
        a = np.concatenate([x_lat[i * 1024:(i + 1) * 1024], x_ctx[i * 32:(i + 1) * 32]], axis=0)
        outs.append(np.ascontiguousarray(a.T))
    return outs


def unshard_tokens_T(parts):
    lat = np.concatenate([p[:, :1024].T for p in parts], axis=0)
    ctx = np.concatenate([p[:, 1024:].T for p in parts], axis=0)
    return np.ascontiguousarray(lat), np.ascontiguousarray(ctx)


def run_k1(P, x_lat, x_ctx, g, mod_l, w_in_b):
    xs = shard_tokens_T(x_lat, x_ctx)
    gf = vec_fm(g)
    sc, sh = mod_fm(mod_l, 1), mod_fm(mod_l, 0)
    w = tile_w(w_in_b[:, :NIN1], 256)
    maps = [{"xT": fm(xs[i]), "g": gf, "sc": sc, "sh": sh, "w": w} for i in range(NCORES)]
    res = P.run(maps)
    return unshard_tokens_T([r["zT"] for r in res])


NA_TILES = 66
NA_R0 = [0, 2, 60, 124, 126]


def natten_bias(rpb_h):
    out = np.full((5, 128, 576), NEG, np.float32)
    for vi, r0 in enumerate(NA_R0):
        base = min(max(r0 - 4, 0), 119)
        for qr in range(2):
            r = r0 + qr
            rs = min(max(r - 4, 0), 120)
            for qc in range(64):
                cs = min(max(qc - 8, 0), 48)
                kcs = np.arange(cs, cs + 16)
                for kr in range(9):
                    ar = base + kr
                    if rs <= ar < rs + 8:
                        out[vi, qr * 64 + qc, kr * 64 + kcs] = rpb_h[ar - r + 7, kcs - qc + 15]
    return np.ascontiguousarray(out.transpose(1, 0, 2))


def build_k2a():
    P = Prog()
    S = P.S
    q_in = P.inp("qT", [64, 8448])
    k_in = P.inp("kT", [64, 8448])
    val_in = P.inp("val", [128, 64, 64])
    vsh_in = P.inp("vsh", [128, 5, 64])
    vc_in = P.inp("vc", [128, 2, 64])
    bias_in = P.inp("bias", [128, 5, 576])
    id_in = P.inp("ident", [128, 128])
    o_out = P.out("o", [128, NA_TILES, 64])
    qb, bqb = P.sb([128, 8448], BF16)
    kb, bkb = P.sb([128, 8448], BF16)
    S.op("pool", "memset", writes=[bqb], ap=qb[:], constant=0.0)
    S.op("pool", "memset", writes=[bkb], ap=kb[:], constant=0.0)
    S.dma("pool", qb[0:64, :], q_in[0], reads=[q_in[1]], writes=[bqb], max_dma_last_dim=4096)
    S.dma("pool", kb[0:64, :], k_in[0], reads=[k_in[1]], writes=[bkb], max_dma_last_dim=4096)
    val, bval = P.load(val_in, [128, 64, 64], BF16, eng="pool")
    vsh, bvsh = P.load(vsh_in, [128, 5, 64], BF16, eng="pool")
    vc, bvc = P.load(vc_in, [128, 2, 64], BF16, eng="pool")
    ident, bid = P.load(id_in, [128, 128], BF16, eng="pool")
    bias, bbias = P.load(bias_in, [128, 5, 576])
    o_all, bo = P.sb([128, NA_TILES, 64])
    psA = [P.ps() for _ in range(2)]
    psB = [P.ps() for _ in range(2)]
    psT = [P.ps([128, 7, 128], BF16) for _ in range(2)]
    psO = [P.ps() for _ in range(2)]
    sbs = [P.sb([128, 832]) for _ in range(2)]
    pbf = [P.sb([128, 832], BF16) for _ in range(2)]
    pTs = [P.sb([128, 7, 128], BF16) for _ in range(2)]
    small = [P.sb([128, 4]) for _ in range(2)]
    for i in range(2):
        S.op("pool", "memset", writes=[pTs[i][1]], ap=pTs[i][0][:], constant=0.0)
        S.op("pool", "memset", writes=[pbf[i][1]], ap=pbf[i][0][:], constant=0.0)

    def stages(t):
        i = t % 2
        (pa, bpa), (pb, bpb), (pt, bpt), (po, bpo) = psA[i], psB[i], psT[i], psO[i]
        (ss, bss), (pf, bpf), (pT, bpT), (sm, bsm) = sbs[i], pbf[i], pTs[i], small[i]
        local = t < 64
        qs = slice(t * 128, (t + 1) * 128)
        lo = 0 if local else 576
        if local:
            r0 = 2 * t
            base = min(max(r0 - 4, 0), 119)
            k0 = base * 64
            var = {0: 0, 2: 1, 124: 3, 126: 4}.get(r0, 2)

        def st_qk():
            if local:
                S.op("pe", "matmul", reads=[bqb, bkb], writes=[bpa], out=pa[:, 0:512], lhsT=qb[:, qs], rhs=kb[:, k0:k0 + 512], start=True, stop=True)
                S.op("pe", "matmul", reads=[bqb, bkb], writes=[bpb], out=pb[:, 0:64], lhsT=qb[:, qs], rhs=kb[:, k0 + 512:k0 + 576], start=True, stop=True)
            S.op("pe", "matmul", reads=[bqb, bkb], writes=[bpb], out=pb[:, 64:320], lhsT=qb[:, qs], rhs=kb[:, 8192:8448], start=True, stop=True)

        def st_bias():
            if local:
                S.op("dve", "scalar_tensor_tensor", reads=[bpa, bbias], writes=[bss], out=ss[:, 0:512], in0=pa[:, 0:512], scalar=0.125, in1=bias[:, var, 0:512], op0=ALU.mult, op1=ALU.add)
                S.op("dve", "scalar_tensor_tensor", reads=[bpb, bbias], writes=[bss], out=ss[:, 512:576], in0=pb[:, 0:64], scalar=0.125, in1=bias[:, var, 512:576], op0=ALU.mult, op1=ALU.add)
            S.op("act", "activation", reads=[bpb], writes=[bss], out=ss[:, 576:832], in_=pb[:, 64:320], func=AF.Identity, scale=0.125)

        def st_max():
            S.op("dve", "tensor_reduce", reads=[bss], writes=[bsm], out=sm[:, 0:1], in_=ss[:, lo:832], axis=AX.X, op=ALU.max)
            S.op("dve", "tensor_scalar", reads=[bsm], writes=[bsm], out=sm[:, 1:2], in0=sm[:, 0:1], scalar1=-1.0, scalar2=None, op0=ALU.mult)

        def st_exp():
            S.op("act", "activation", reads=[bss, bsm], writes=[bpf, bsm], out=pf[:, lo:832], in_=ss[:, lo:832], func=AF.Exp, bias=sm[:, 1:2], scale=1.0, accum_out=sm[:, 2:3])

        def st_tr():
            for j, c0 in (([(0, 0), (1, 128), (2, 256), (3, 384), (4, 512)] if local else []) + [(5, 576), (6, 704)]):
                S.op("pe", "transpose", reads=[bpf, bid], writes=[bpt], out=pt[:, j, :], in_=pf[:, c0:c0 + 128], identity=ident[:])

        def st_cp():
            if local:
                S.op("dve", "tensor_copy", reads=[bpt], writes=[bpT], out=pT[:, 0:4, :], in_=pt[:, 0:4, :])
                S.op("act", "copy", reads=[bpt], writes=[bpT], out=pT[0:64, 4, :], in_=pt[0:64, 4, :])
            S.op("act", "copy", reads=[bpt], writes=[bpT], out=pT[:, 5:7, :], in_=pt[:, 5:7, :])

        def st_pv():
            mm = []
            if local:
                even = (base % 2 == 0)
                for j in range(5):
                    mm.append((pT[:, j, :], val[:, k0 // 128 + j, :] if even else vsh[:, j, :]))
            mm.append((pT[:, 5, :], vc[:, 0, :]))
            mm.append((pT[:, 6, :], vc[:, 1, :]))
            for n, (l, r) in enumerate(mm):
                S.op("pe", "matmul", reads=[bpT, bval, bvsh, bvc], writes=[bpo], out=po[:, 0:64], lhsT=l, rhs=r, start=(n == 0), stop=(n == len(mm) - 1))

        def st_out():
            S.op("dve", "reciprocal", reads=[bsm], writes=[bsm], out=sm[:, 3:4], in_=sm[:, 2:3])
            S.op("dve", "tensor_scalar", reads=[bpo, bsm], writes=[bo], out=o_all[:, t, :], in0=po[:, 0:64], scalar1=sm[:, 3:4], scalar2=None, op0=ALU.mult)

        return [st_qk, st_bias, st_max, st_exp, st_tr, st_cp, st_pv, st_out]

    for t0 in range(0, NA_TILES, 2):
        sa, sb_ = stages(t0), stages(t0 + 1)
        for fa, fb in zip(sa, sb_):
            fa()
            fb()
    S.dma("sp", o_out[0], o_all[:], reads=[bo])
    return P


def tok_chunks(a, p=128):
    t, f = a.shape
    return np.ascontiguousarray(a.reshape(t // p, p, f).transpose(1, 0, 2))


def run_k2a(P, zl, zc, rpb):
    ident = np.eye(128, dtype=np.float32)
    maps = []
    for h in range(NCORES):
        hs = slice(h * 64, (h + 1) * 64)
        q = np.concatenate([zl[:, 0:512][:, hs], zc[:, 0:512][:, hs]], 0)
        k = np.concatenate([zl[:, 512:1024][:, hs], zc[:, 512:1024][:, hs]], 0)
        v = zl[:, 1024:1536][:, hs]
        vcx = zc[:, 1024:1536][:, hs]
        vs = np.zeros((5 * 128, 64), np.float32)
        vs[:576] = v[119 * 64:119 * 64 + 576]
        maps.append({"qT": np.ascontiguousarray(q.T), "kT": np.ascontiguousarray(k.T), "val": tok_chunks(v),
                     "vsh": tok_chunks(vs), "vc": tok_chunks(vcx), "bias": natten_bias(rpb[h]), "ident": ident})
    res = P.run(maps)
    a_lat = np.zeros((SEQ, 512), np.float32)
    a_ctx = np.zeros((CTX, 512), np.float32)
    for h in range(NCORES):
        o = res[h]["o"]
        o = o.transpose(1, 0, 2).reshape(66 * 128, 64)
        a_lat[:, h * 64:(h + 1) * 64] = o[:SEQ]
        a_ctx[:, h * 64:(h + 1) * 64] = o[SEQ:]
    return a_lat, a_ctx


QCOLS = 8448
NCH = 17


def rope_tables():
    inv = (np.float32(10000.0) ** (-np.arange(16, dtype=np.float32) / np.float32(16))).astype(np.float32)
    pos = np.arange(SEQ)
    pr = (pos // GW).astype(np.float32)
    pc = (pos % GW).astype(np.float32)
    cos = np.zeros((64, SEQ), np.float32)
    sin = np.zeros((64, SEQ), np.float32)
    for m in range(64):
        p = pr if m < 32 else pc
        ang = (p * inv[m % 16]).astype(np.float32)
        cos[m] = np.cos(ang).astype(np.float32)
        sin[m] = np.sin(ang).astype(np.float32)
    return cos, sin


def rot_matrix():
    r = np.zeros((64, 64), np.float32)
    for m in range(64):
        if (m % 32) < 16:
            r[m + 16, m] = -1.0
        else:
            r[m - 16, m] = 1.0
    return r


def build_k2d():
    P = Prog()
    S = P.S
    q_in = P.inp("q", [64, QCOLS])
    k_in = P.inp("k", [64, QCOLS])
    cq_in = P.inp("cosq", [64, QCOLS])
    sq_in = P.inp("sinq", [64, QCOLS])
    ck_in = P.inp("cosk", [64, QCOLS])
    sk_in = P.inp("sink", [64, QCOLS])
    v_in = P.inp("vaug", [128, 66, 128])
    gq_in = P.inp("gq", [64, 1])
    gk_in = P.inp("gk", [64, 1])
    grow_in = P.inp("grow", [1, 2, 64])
    rot_in = P.inp("rot", [64, 64])
    o_out = P.out("o", [64, 17, 512])
    gq, bgq = P.load(gq_in, [64, 1])
    gk, bgk = P.load(gk_in, [64, 1])
    grow, bgrow = P.load(grow_in, [1, 2, 64])
    rot, brot = P.load(rot_in, [64, 64])
    vaug, bv = P.load(v_in, [128, 66, 128], BF16, eng="pool")
    ones, bones = P.sb([64, 128])
    S.op("dve", lambda e: e.memset(ones[:], 1.0 / 64.0), writes=[bones])
    one1, bone1 = P.sb([1, 128])
    S.op("dve", lambda e: e.memset(one1[:], 1.0), writes=[bone1])
    ep, bep = P.sb([128, 1])
    S.op("dve", lambda e: e.memset(ep[:], EPS), writes=[bep])
    mm_, bmm = P.sb([1, 4])
    S.op("dve", lambda e: e.tensor_reduce(out=mm_[:, 0:2], in_=grow[:], axis=AX.X, op=ALU.max, apply_absolute_value=True), reads=[bgrow], writes=[bmm])
    S.op("dve", lambda e: e.tensor_tensor(out=mm_[:, 2:3], in0=mm_[:, 0:1], in1=mm_[:, 1:2], op=ALU.mult), reads=[bmm], writes=[bmm])
    S.op("dve", lambda e: e.tensor_scalar(out=mm_[:, 3:4], in0=mm_[:, 2:3], scalar1=-8.0, scalar2=None, op0=ALU.mult), reads=[bmm], writes=[bmm])
    psn = [P.ps() for _ in range(2)]
    negM, bnegM = P.sb([128, 1])
    S.op("pe", lambda e: e.matmul(psn[0][0][:, 0:1], one1[:], mm_[:, 3:4], start=True, stop=True), reads=[bone1, bmm], writes=[psn[0][1]])
    S.op("dve", lambda e: e.tensor_copy(out=negM[:], in_=psn[0][0][:, 0:1]), reads=[psn[0][1]], writes=[bnegM])

    qb, bqb = P.sb([128, QCOLS], BF16)
    kb, bkb = P.sb([128, QCOLS], BF16)
    S.op("pool", "memset", writes=[bqb], ap=qb[:], constant=0.0)
    S.op("pool", "memset", writes=[bkb], ap=kb[:], constant=0.0)
    xin = [P.sb([64, 3, 512]) for _ in range(2)]
    w1 = [P.sb([64, 4, 512]) for _ in range(2)]
    it = 0
    for (src, cs, sn, g, bg, dst, bdst) in ((q_in, cq_in, sq_in, gq, bgq, qb, bqb), (k_in, ck_in, sk_in, gk, bgk, kb, bkb)):
        for ch in range(NCH):
            c0 = ch * 512
            n = min(512, QCOLS - c0)
            (xi, bxi), (w, bw) = xin[it % 2], w1[it % 2]
            (pm, bpm), (pr, bpr) = psn[0], psn[1]
            it += 1
            S.dma("sp", xi[:, 0, :n], src[0][:, c0:c0 + n], reads=[src[1]], writes=[bxi])
            S.dma("sp", xi[:, 1, :n], cs[0][:, c0:c0 + n], reads=[cs[1]], writes=[bxi])
            S.dma("sp", xi[:, 2, :n], sn[0][:, c0:c0 + n], reads=[sn[1]], writes=[bxi])
            S.op("act", lambda e, w=w, xi=xi, n=n: e.activation(out=w[:, 0, :n], in_=xi[:, 0, :n], func=AF.Square), reads=[bxi], writes=[bw])
            S.op("pe", lambda e, pm=pm, w=w, n=n: e.matmul(pm[0:64, :n], ones[:, 0:64], w[:, 0, :n], start=True, stop=True), reads=[bones, bw], writes=[bpm])
            S.op("act", lambda e, pm=pm, w=w, n=n: e.activation(out=w[:, 1, :n], in_=pm[0:64, :n], func=AF.Sqrt, bias=ep[0:64, :], scale=1.0), reads=[bpm, bep], writes=[bw])
            S.op("dve", lambda e, w=w, n=n: e.reciprocal(out=w[:, 1, :n], in_=w[:, 1, :n]), reads=[bw], writes=[bw])
            S.op("dve", lambda e, w=w, xi=xi, n=n, g=g: e.scalar_tensor_tensor(out=w[:, 2, :n], in0=xi[:, 0, :n], scalar=g[:, 0:1], in1=w[:, 1, :n], op0=ALU.mult, op1=ALU.mult), reads=[bxi, bw, bg], writes=[bw])
            S.op("pe", lambda e, pr=pr, w=w, n=n: e.matmul(pr[0:64, :n], rot[:], w[:, 2, :n], start=True, stop=True), reads=[brot, bw], writes=[bpr])
            S.op("dve", lambda e, w=w, xi=xi, n=n: e.tensor_tensor(out=w[:, 3, :n], in0=w[:, 2, :n], in1=xi[:, 1, :n], op=ALU.mult), reads=[bw, bxi], writes=[bw])
            S.op("dve", lambda e, w=w, xi=xi, pr=pr, n=n: e.tensor_tensor(out=w[:, 0, :n], in0=pr[0:64, :n], in1=xi[:, 2, :n], op=ALU.mult), reads=[bpr, bxi], writes=[bw])
            S.op("dve", lambda e, w=w, dst=dst, c0=c0, n=n: e.tensor_tensor(out=dst[0:64, c0:c0 + n], in0=w[:, 3, :n], in1=w[:, 0, :n], op=ALU.add), reads=[bw], writes=[bdst])

    oT_all, bo = P.sb([64, 17, 512])
    S.op("pool", "memset", writes=[bo], ap=oT_all[:], constant=0.0)
    sel, bsel = P.sb([65, 64])
    S.op("dve", "memset", writes=[bsel], ap=sel[:], constant=0.0)
    S.op("dve", "memset", writes=[bsel], ap=sel[64:65, :], constant=1.0)
    psS = [P.ps([128, 1024]) for _ in range(2)]
    psO = [P.ps() for _ in range(2)]
    psD, bpsD = psn[0]
    pTs = [P.sb([128, 1024], BF16) for _ in range(3)]
    osbs = [P.sb([65, 512]) for _ in range(2)]
    rinv, brinv = P.sb([64, 512])
    work = []
    for t in range(17):
        ntok = 128 if t < 16 else 64
        chunks = list(range(66)) if t < 16 else [64, 65]
        for ci in range(0, len(chunks), 2):
            work.append((t, 4 * ntok, t * 512, ci, chunks[ci], len(chunks)))

    def emit_S(i):
        t, ncol, c0, ci, kc, nch = work[i]
        pS, bpS = psS[i % 2]
        for j in range(2):
            S.op("pe", "matmul", reads=[bkb, bqb], writes=[bpS], out=pS[:, j * 512:j * 512 + ncol], lhsT=kb[:, (kc + j) * 128:(kc + j + 1) * 128], rhs=qb[:, c0:c0 + ncol], start=True, stop=True)

    emit_S(0)
    for i in range(len(work)):
        t, ncol, c0, ci, kc, nch = work[i]
        if i + 1 < len(work):
            emit_S(i + 1)
        (pS, bpS), (pT, bpT) = psS[i % 2], pTs[i % 3]
        po, bpo = psO[t % 2]
        osb, bosb = osbs[t % 2]
        if ncol == 512:
            S.op("act", "activation", reads=[bpS, bnegM], writes=[bpT], out=pT[:, :], in_=pS[:, :], func=AF.Exp, bias=negM[:], scale=0.125)
        else:
            for j in range(2):
                S.op("act", "activation", reads=[bpS, bnegM], writes=[bpT], out=pT[:, j * 512:j * 512 + ncol], in_=pS[:, j * 512:j * 512 + ncol], func=AF.Exp, bias=negM[:], scale=0.125)
        for j in range(2):
            S.op("pe", "matmul", reads=[bpT, bv], writes=[bpo], out=po[:, :ncol], lhsT=vaug[:, kc + j, :], rhs=pT[:, j * 512:j * 512 + ncol], start=(ci == 0 and j == 0), stop=(ci + 2 >= nch and j == 1))
        if ci + 2 >= nch:
            S.op("act", "copy", reads=[bpo], writes=[bosb], out=osb[:, :ncol], in_=po[0:65, :ncol])
            S.op("pe", "matmul", reads=[bsel, bosb], writes=[bpsD], out=psD[0:64, :ncol], lhsT=sel[:], rhs=osb[:, :ncol], start=True, stop=True)
            S.op("dve", "reciprocal", reads=[bpsD], writes=[brinv], out=rinv[:, :ncol], in_=psD[0:64, :ncol])
            S.op("dve", "tensor_tensor", reads=[bosb, brinv], writes=[bo], out=oT_all[:, t, :ncol], in0=osb[0:64, :ncol], in1=rinv[:, :ncol], op=ALU.mult)
    S.dma("sp", o_out[0], oT_all[:], reads=[bo])
    return P


def run_k2d(P, zl, zc, gq, gk):
    cos, sin = rope_tables()
    rot = rot_matrix()
    maps = []
    for core in range(NCORES):
        g, qt = core // 4, core % 4
        ql = zl[qt * 2048:(qt + 1) * 2048, 3584:4096].reshape(16, 128, 8, 64)[:, :, 4 * g:4 * g + 4, :]
        qlT = ql.transpose(3, 0, 2, 1).reshape(64, 16 * 4 * 128)
        qc = zc[qt * 64:(qt + 1) * 64, 3584:4096].reshape(64, 8, 64)[:, 4 * g:4 * g + 4, :]
        qcT = qc.transpose(2, 1, 0).reshape(64, 256)
        q = np.ascontiguousarray(np.concatenate([qlT, qcT], 1))
        cl = cos[:, qt * 2048:(qt + 1) * 2048].reshape(64, 16, 1, 128)
        sl = sin[:, qt * 2048:(qt + 1) * 2048].reshape(64, 16, 1, 128)
        cosq = np.concatenate([np.broadcast_to(cl, (64, 16, 4, 128)).reshape(64, 8192), np.ones((64, 256), np.float32)], 1)
        sinq = np.concatenate([np.broadcast_to(sl, (64, 16, 4, 128)).reshape(64, 8192), np.zeros((64, 256), np.float32)], 1)
        k = np.concatenate([zl[:, 4096:4224][:, g * 64:(g + 1) * 64], zc[:, 4096:4224][:, g * 64:(g + 1) * 64]], 0)
        cosk = np.concatenate([cos, np.ones((64, 256), np.float32)], 1)
        sink = np.concatenate([sin, np.zeros((64, 256), np.float32)], 1)
        v = np.concatenate([zl[:, 4224:4352][:, g * 64:(g + 1) * 64], zc[:, 4224:4352][:, g * 64:(g + 1) * 64]], 0)
        vaug = np.concatenate([v, np.ones((8448, 1), np.float32), np.zeros((8448, 63), np.float32)], 1)
        maps.append({"q": q, "k": np.ascontiguousarray(k.T), "cosq": np.ascontiguousarray(cosq), "sinq": np.ascontiguousarray(sinq),
                     "cosk": np.ascontiguousarray(cosk), "sink": np.ascontiguousarray(sink), "vaug": tok_chunks(vaug),
                     "gq": np.ascontiguousarray(gq.reshape(64, 1)), "gk": np.ascontiguousarray(gk.reshape(64, 1)),
                     "grow": np.ascontiguousarray(np.stack([gq, gk])[None]), "rot": rot})
    res = P.run(maps)
    d_lat = np.zeros((SEQ, 512), np.float32)
    d_ctx = np.zeros((CTX, 512), np.float32)
    for core in range(NCORES):
        g, qt = core // 4, core % 4
        o = res[core]["o"]
        lat = o[:, :16, :].reshape(64, 16, 4, 128).transpose(1, 3, 2, 0).reshape(2048, 256)
        d_lat[qt * 2048:(qt + 1) * 2048, g * 256:(g + 1) * 256] = lat
        d_ctx[qt * 64:(qt + 1) * 64, g * 256:(g + 1) * 256] = o[:, 16, :256].reshape(64, 4, 64).transpose(2, 1, 0).reshape(64, 256)
    return d_lat, d_ctx


TALL = CTX + SEQ
PADW = TALL + 6


def rev(ap):
    return ap[:, ::-1]


def build_k2b():
    P = Prog()
    S = P.S
    x_in = P.inp("x3", [64, TALL])
    z_in = P.inp("z4", [64, TALL])
    tap_in = P.inp("taps", [64, 4])
    w_in = P.inp("wrg", [64, 4, 64])
    b_in = P.inp("brg", [64, 4])
    lam_in = P.inp("lam", [64, 2])
    o_out = P.out("o", [64, TALL])
    taps, btaps = P.load(tap_in, [64, 4])
    wrg, bwrg = P.load(w_in, [64, 4, 64], BF16, eng="pool")
    brg, bbrg = P.load(b_in, [64, 4])
    lam, blam = P.load(lam_in, [64, 2])
    T1, bT1 = P.sb([64, PADW])
    XC, bXC = P.sb([64, TALL])
    A, bA = P.sb([64, TALL])
    H = [P.sb([64, TALL]) for _ in range(2)]
    S.op("pool", "memset", writes=[bT1], ap=T1[:], constant=0.0)
    S.dma("sp", T1[:, 2:258], x_in[0][:, 0:256], reads=[x_in[1]], writes=[bT1])
    S.dma("sp", T1[:, 261:261 + SEQ], x_in[0][:, 256:TALL], reads=[x_in[1]], writes=[bT1])
    cd, bcd = P.sb([64, 2])
    S.op("act", "activation", reads=[blam], writes=[bcd], out=cd[:], in_=lam[:], func=AF.Exp, scale=-1.0)
    S.op("act", "activation", reads=[bcd], writes=[bcd], out=cd[:], in_=cd[:], func=AF.Ln, bias=1.0, scale=1.0)
    S.op("dve", "tensor_scalar", reads=[bcd], writes=[bcd], out=cd[:], in0=cd[:], scalar1=-8.0, scalar2=None, op0=ALU.mult)
    for (o0, n, s0) in ((0, CTX, 0), (CTX, SEQ, 259)):
        S.op("dve", "tensor_scalar", reads=[bT1, btaps], writes=[bXC], out=XC[:, o0:o0 + n], in0=T1[:, s0:s0 + n], scalar1=taps[:, 0:1], scalar2=None, op0=ALU.mult)
        for j in range(1, 4):
            S.op("dve", "scalar_tensor_tensor", reads=[bT1, btaps, bXC], writes=[bXC], out=XC[:, o0:o0 + n], in0=T1[:, s0 + j:s0 + j + n],
                 scalar=taps[:, j:j + 1], in1=XC[:, o0:o0 + n], op0=ALU.mult, op1=ALU.add)
    psr = [P.ps() for _ in range(2)]
    psi = [P.ps() for _ in range(2)]
    xb = [P.sb([64, 512], BF16) for _ in range(2)]
    wk = [P.sb([64, 4, 512]) for _ in range(2)]
    it = 0
    for d in range(2):
        Hd, bH = H[d]
        for ch in range(17):
            c0 = ch * 512
            n = min(512, TALL - c0)
            (xbb, bxb), (w, bw), (pr, bpr), (pi_, bpi) = xb[it % 2], wk[it % 2], psr[it % 2], psi[it % 2]
            it += 1
            S.op("act", "copy", reads=[bXC], writes=[bxb], out=xbb[:, :n], in_=XC[:, c0:c0 + n])
            S.op("pe", "matmul", reads=[bwrg, bxb], writes=[bpr], out=pr[0:64, :n], lhsT=wrg[:, 2 * d, :], rhs=xbb[:, :n], start=True, stop=True)
            S.op("pe", "matmul", reads=[bwrg, bxb], writes=[bpi], out=pi_[0:64, :n], lhsT=wrg[:, 2 * d + 1, :], rhs=xbb[:, :n], start=True, stop=True)
            S.op("act", "activation", reads=[bpr, bbrg], writes=[bw], out=w[:, 0, :n], in_=pr[0:64, :n], func=AF.Sigmoid, bias=brg[:, 2 * d:2 * d + 1], scale=1.0)
            S.op("act", "activation", reads=[bpi, bbrg], writes=[bw], out=w[:, 1, :n], in_=pi_[0:64, :n], func=AF.Sigmoid, bias=brg[:, 2 * d + 1:2 * d + 2], scale=1.0)
            S.op("act", "activation", reads=[bw, bcd], writes=[bA], out=A[:, c0:c0 + n], in_=w[:, 0, :n], func=AF.Exp, scale=cd[:, d:d + 1])
            S.op("dve", "tensor_tensor", reads=[bA], writes=[bw], out=w[:, 2, :n], in0=A[:, c0:c0 + n], in1=A[:, c0:c0 + n], op=ALU.mult)
            S.op("dve", "tensor_scalar", reads=[bw], writes=[bw], out=w[:, 2, :n], in0=w[:, 2, :n], scalar1=-1.0, scalar2=1.0, op0=ALU.mult, op1=ALU.add)
            S.op("act", "activation", reads=[bw], writes=[bw], out=w[:, 3, :n], in_=w[:, 2, :n], func=AF.Sqrt)
            S.op("dve", "tensor_tensor", reads=[bw], writes=[bw], out=w[:, 3, :n], in0=w[:, 3, :n], in1=w[:, 1, :n], op=ALU.mult)
            S.op("dve", "tensor_tensor", reads=[bw, bXC], writes=[bH], out=Hd[:, c0:c0 + n], in0=w[:, 3, :n], in1=XC[:, c0:c0 + n], op=ALU.mult)
        if d == 0:
            S.op("dve", "tensor_tensor_scan", reads=[bA, bH], writes=[bH], out=Hd[:, 0:CTX], data0=A[:, 0:CTX], data1=Hd[:, 0:CTX], initial=0.0, op0=ALU.mult, op1=ALU.add)
            for k in range(4):
                a0 = CTX + k * 2048
                S.op("dve", "tensor_tensor_scan", reads=[bA, bH], writes=[bH], out=Hd[:, a0:a0 + 2048], data0=A[:, a0:a0 + 2048], data1=Hd[:, a0:a0 + 2048],
                     initial=Hd[:, a0 - 1:a0], op0=ALU.mult, op1=ALU.add)
        else:
            S.op("dve", "tensor_tensor_scan", reads=[bA, bH], writes=[bH], out=rev(Hd[:, 0:CTX]), data0=rev(A[:, 0:CTX]), data1=rev(Hd[:, 0:CTX]), initial=0.0, op0=ALU.mult, op1=ALU.add)
            for k in range(4):
                a0 = TALL - (k + 1) * 2048
                init = Hd[:, 0:1] if k == 0 else Hd[:, a0 + 2048:a0 + 2049]
                S.op("dve", "tensor_tensor_scan", reads=[bA, bH], writes=[bH], out=rev(Hd[:, a0:a0 + 2048]), data0=rev(A[:, a0:a0 + 2048]), data1=rev(Hd[:, a0:a0 + 2048]),
                     initial=init, op0=ALU.mult, op1=ALU.add)
    (Hf, bHf), (Hb, bHb) = H
    S.op("dve", "tensor_tensor", reads=[bHf, bHb], writes=[bHf], out=Hf[:], in0=Hf[:], in1=Hb[:], op=ALU.add)
    Z = T1
    S.dma("sp", Z[:, 0:TALL], z_in[0], reads=[z_in[1]], writes=[bT1])
    for c0 in range(0, TALL, 2112):
        n = 2112
        sl = slice(c0, c0 + n)
        S.op("dve", "tensor_tensor", reads=[bT1], writes=[bA], out=A[:, sl], in0=Z[:, sl], in1=Z[:, sl], op=ALU.mult)
        S.op("dve", "tensor_scalar", reads=[bA], writes=[bA], out=A[:, sl], in0=A[:, sl], scalar1=0.044715, scalar2=1.0, op0=ALU.mult, op1=ALU.add)
        S.op("dve", "tensor_tensor", reads=[bA, bT1], writes=[bA], out=A[:, sl], in0=A[:, sl], in1=Z[:, sl], op=ALU.mult)
        S.op("act", "activation", reads=[bA], writes=[bA], out=A[:, sl], in_=A[:, sl], func=AF.Sigmoid, scale=1.5957691216057308)
        S.op("dve", "tensor_tensor", reads=[bA, bT1], writes=[bA], out=A[:, sl], in0=A[:, sl], in1=Z[:, sl], op=ALU.mult)
        S.op("dve", "tensor_tensor", reads=[bA, bHf], writes=[bHb], out=Hb[:, sl], in0=A[:, sl], in1=Hf[:, sl], op=ALU.mult)
    S.dma("sp", o_out[0], Hb[:], reads=[bHb])
    return P


def run_k2b(P, zl, zc, rg_conv, w_rg, b_rg, rg_lambda):
    maps = []
    for n in range(NCORES):
        cs = slice(n * 64, (n + 1) * 64)
        x3 = np.concatenate([zc[:, 1536:2048][:, cs], zl[:, 1536:2048][:, cs]], 0).T
        z4 = np.concatenate([zc[:, 2048:2560][:, cs], zl[:, 2048:2560][:, cs]], 0).T
        wr = w_rg[:, :, n].reshape(4, 64, 64).transpose(1, 0, 2)
        maps.append({"x3": np.ascontiguousarray(x3), "z4": np.ascontiguousarray(z4), "taps": np.ascontiguousarray(rg_conv[:, cs].T),
                     "wrg": np.ascontiguousarray(wr), "brg": np.ascontiguousarray(b_rg.reshape(4, 512)[:, cs].T),
                     "lam": np.ascontiguousarray(rg_lambda[:, cs].T)})
    res = P.run(maps)
    o = np.concatenate([r["o"] for r in res], 0).T
    return np.ascontiguousarray(o[CTX:]), np.ascontiguousarray(o[:CTX])


CW = 1116
CWO = CW - 30


def build_k2c():
    P = Prog()
    S = P.S
    v_in = P.inp("val", [128, 4, CW])
    g_in = P.inp("gt", [128, 4, CW])
    cw_in = P.inp("cw", [128, 4, 31])
    lg_in = P.inp("lng", [128, 4])
    lb_in = P.inp("lnb", [128, 4])
    o_out = P.out("o", [128, 4, CWO])
    val, bval = P.load(v_in, [128, 4, CW])
    gt, bgt = P.load(g_in, [128, 4, CW])
    cw, bcw = P.load(cw_in, [128, 4, 31])
    lng, blg = P.load(lg_in, [128, 4])
    lnb, blb = P.load(lb_in, [128, 4])
    ones, bones = P.sb([128, 128])
    S.op("dve", "memset", writes=[bones], ap=ones[:], constant=1.0 / 512.0)
    ep, bep = P.sb([128, 1])
    S.op("dve", "memset", writes=[bep], ap=ep[:], constant=EPS)
    S.op("act", "activation", reads=[bgt], writes=[bgt], out=gt[:], in_=gt[:], func=AF.Sigmoid)
    Y = [P.sb([128, CW]) for _ in range(4)]
    O = [P.sb([128, CWO]) for _ in range(4)]
    for c in range(4):
        eng = "dve"
        S.op(eng, "tensor_tensor", reads=[bval, bgt], writes=[Y[c][1]], out=Y[c][0][:], in0=val[:, c, :], in1=gt[:, c, :], op=ALU.mult)
        S.op(eng, "tensor_scalar", reads=[Y[c][1], bcw], writes=[O[c][1]], out=O[c][0][:], in0=Y[c][0][:, 0:CWO], scalar1=cw[:, c, 0:1], scalar2=None, op0=ALU.mult)
        for k in range(1, 31):
            S.op(eng, "scalar_tensor_tensor", reads=[Y[c][1], bcw, O[c][1]], writes=[O[c][1]], out=O[c][0][:], in0=Y[c][0][:, k:k + CWO],
                 scalar=cw[:, c, k:k + 1], in1=O[c][0][:], op0=ALU.mult, op1=ALU.add)
    psm, bpsm = P.ps()
    psq, bpsq = P.ps()
    sq = [P.sb([128, 512]) for _ in range(2)]
    st, bst = P.sb([128, 4, 512])
    res, bres = P.sb([128, 4, CWO])
    for (c0, n) in ((0, 512), (512, 512), (1024, CWO - 1024)):
        for c in range(4):
            S.op("pe", "matmul", reads=[bones, O[c][1]], writes=[bpsm], out=psm[:, :n], lhsT=ones[:], rhs=O[c][0][:, c0:c0 + n], start=(c == 0), stop=(c == 3))
        for c in range(4):
            s_, bs_ = sq[c % 2]
            S.op("act", "activation", reads=[O[c][1]], writes=[bs_], out=s_[:, :n], in_=O[c][0][:, c0:c0 + n], func=AF.Square)
            S.op("pe", "matmul", reads=[bones, bs_], writes=[bpsq], out=psq[:, :n], lhsT=ones[:], rhs=s_[:, :n], start=(c == 0), stop=(c == 3))
        S.op("act", "copy", reads=[bpsm], writes=[bst], out=st[:, 0, :n], in_=psm[:, :n])
        S.op("act", "activation", reads=[bst], writes=[bst], out=st[:, 1, :n], in_=st[:, 0, :n], func=AF.Square)
        S.op("dve", "tensor_tensor", reads=[bpsq, bst], writes=[bst], out=st[:, 2, :n], in0=psq[:, :n], in1=st[:, 1, :n], op=ALU.subtract)
        S.op("act", "activation", reads=[bst, bep], writes=[bst], out=st[:, 2, :n], in_=st[:, 2, :n], func=AF.Sqrt, bias=ep[:], scale=1.0)
        S.op("dve", "reciprocal", reads=[bst], writes=[bst], out=st[:, 2, :n], in_=st[:, 2, :n])
        for c in range(4):
            s_, bs_ = sq[c % 2]
            S.op("dve", "tensor_tensor", reads=[O[c][1], bst], writes=[bs_], out=s_[:, :n], in0=O[c][0][:, c0:c0 + n], in1=st[:, 0, :n], op=ALU.subtract)
            S.op("dve", "tensor_tensor", reads=[bs_, bst], writes=[bs_], out=s_[:, :n], in0=s_[:, :n], in1=st[:, 2, :n], op=ALU.mult)
            S.op("act", "activation", reads=[bs_, blg, blb], writes=[bres], out=res[:, c, c0:c0 + n], in_=s_[:, :n], func=AF.Silu, bias=lnb[:, c:c + 1], scale=lng[:, c:c + 1])
    S.dma("sp", o_out[0], res[:], reads=[bres])
    return P


def halo_rows(a, s, n, h):
    out = np.zeros((n + 2 * h, a.shape[1]), a.dtype)
    lo, hi = max(s - h, 0), min(s + n + h, a.shape[0])
    out[lo - (s - h):hi - (s - h)] = a[lo:hi]
    return out


def run_k2c(P, zl, zc, conf_dw, ln_g, ln_b):
    maps = []
    cwf = np.ascontiguousarray(conf_dw.T.reshape(4, 128, 31).transpose(1, 0, 2))
    for i in range(NCORES):
        z = np.concatenate([halo_rows(zl[:, 2560:3584], i * 1024, 1024, 15), halo_rows(zc[:, 2560:3584], i * 32, 32, 15)], 0)
        maps.append({"val": fm(np.ascontiguousarray(z[:, :512].T)), "gt": fm(np.ascontiguousarray(z[:, 512:].T)), "cw": cwf,
                     "lng": vec_fm(ln_g), "lnb": vec_fm(ln_b)})
    res = P.run(maps)
    lat, ctx = [], []
    for i in range(NCORES):
        o = unfm(res[i]["o"]).T
        lat.append(o[:1024])
        ctx.append(o[1054:1086])
    return np.concatenate(lat, 0), np.concatenate(ctx, 0)


T3 = 528
BLK3 = [(0, 512, 0), (512, 16, 1)]


def build_k3():
    P = Prog()
    S = P.S
    x_in = P.inp("xT", [128, KC, T3])
    g_in = P.inp("g", [128, KC])
    sc_in = P.inp("sc", [128, KC, 2])
    sh_in = P.inp("sh", [128, KC, 2])
    ga_in = P.inp("gate", [128, KC, 2])
    br_in = P.inp("brT", [128, 4, 4, T3])
    wg_in = P.inp("wg", [16, 128, KC, 512], BF16)
    wb_in = P.inp("wb", [16, 128, 4, 512], BF16)
    wo_in = P.inp("wo", [4, 128, KC, 512], BF16)
    o_out = P.out("xmT", [D, T3])
    ones, bones = make_consts(P)
    xT, bx = P.load(x_in, [128, KC, T3])
    ga, bga = P.load(ga_in, [128, KC, 2])
    brT, bbr = P.load(br_in, [128, 4, 4, T3], BF16, eng="pool")
    hT, bh = P.sb([128, KC, T3], BF16)
    psums = [P.ps() for _ in range(4)]
    emit_norm_mod(P, xT, bx, hT, bh, BLK3, g_in, sc_in, sh_in, ones, bones, psums)
    yT, byT = P.sb([128, KC, T3], BF16)
    acc = [P.sb([128, T3]) for _ in range(4)]
    wgt = [P.sb([128, KC, 512], BF16) for _ in range(2)]
    wbt = [P.sb([128, 4, 512], BF16) for _ in range(2)]
    sg = [P.sb([128, 512]) for _ in range(2)]
    psb = [P.ps() for _ in range(2)]
    it = 0
    pi = 0
    for fg in range(4):
        for n in range(4):
            (wgg, bwg), (wbb, bwb) = wgt[it % 2], wbt[it % 2]
            it += 1
            S.dma("sp", wgg[:], wg_in[0][n * 4 + fg], reads=[wg_in[1]], writes=[bwg])
            S.dma("sp", wbb[:], wb_in[0][n * 4 + fg], reads=[wb_in[1]], writes=[bwb])
            for (c0, nn, kind) in BLK3:
                for f in range(4):
                    (pg, bpg), (pb, bpb), (sgg, bsg) = psums[pi % 4], psb[pi % 2], sg[pi % 2]
                    pi += 1
                    for k in range(KC):
                        S.op("pe", "matmul", reads=[bwg, bh], writes=[bpg], out=pg[:, :nn], lhsT=wgg[:, k, f * 128:(f + 1) * 128], rhs=hT[:, k, c0:c0 + nn], start=(k == 0), stop=(k == KC - 1))
                    for k in range(4):
                        S.op("pe", "matmul", reads=[bwb, bbr], writes=[bpb], out=pb[:, :nn], lhsT=wbb[:, k, f * 128:(f + 1) * 128], rhs=brT[:, n, k, c0:c0 + nn], start=(k == 0), stop=(k == 3))
                    S.op("act", "activation", reads=[bpg], writes=[bsg], out=sgg[:, :nn], in_=pg[:, :nn], func=AF.Sigmoid)
                    a_, ba_ = acc[f]
                    if n == 0:
                        S.op("dve", "tensor_tensor", reads=[bsg, bpb], writes=[ba_], out=a_[:, c0:c0 + nn], in0=sgg[:, :nn], in1=pb[:, :nn], op=ALU.mult)
                    else:
                        S.op("dve", "tensor_tensor", reads=[bsg, bpb], writes=[bsg], out=sgg[:, :nn], in0=sgg[:, :nn], in1=pb[:, :nn], op=ALU.mult)
                        S.op("pool", "tensor_tensor", reads=[bsg, ba_], writes=[ba_], out=a_[:, c0:c0 + nn], in0=a_[:, c0:c0 + nn], in1=sgg[:, :nn], op=ALU.add)
        for f in range(4):
            S.op("act", "copy", reads=[acc[f][1]], writes=[byT], out=yT[:, fg * 4 + f, :], in_=acc[f][0][:])
    stg = [P.sb([128, 512]) for _ in range(3)]
    cnt = [0]

    def evac(m, bi, ps, bps, c0, n, kind):
        st, bst = stg[cnt[0] % 3]
        cnt[0] += 1
        S.op("dve", "scalar_tensor_tensor", reads=[bps, bga, bx], writes=[bst], out=st[:, :n], in0=ps[:, :n], scalar=ga[:, m, kind:kind + 1], in1=xT[:, m, c0:c0 + n], op0=ALU.mult, op1=ALU.add)
        S.dma("pool", o_out[0][m * 128:(m + 1) * 128, c0:c0 + n], st[:, :n], reads=[bst])

    emit_proj(P, wo_in, D, 0, yT, byT, KC, BLK3, evac, psums)
    return P


def run_k3(P, x_lat, x_ctx, br_lat, br_ctx, g, mod_l, w_in_b, w_branch_b, w_out_b):
    gf = vec_fm(g)
    sc, sh, ga = mod_fm(mod_l, 1), mod_fm(mod_l, 0), mod_fm(mod_l, 2)
    wg = tile_w(w_in_b[:, NIN1:], 512)
    wb = np.ascontiguousarray(np.concatenate([tile_w(w_branch_b[n], 512) for n in range(4)], 0))
    wo = tile_w(w_out_b, 512)
    lat_out = np.zeros((SEQ, D), np.float32)
    ctx_out = np.zeros((CTX, D), np.float32)
    for hf in range(2):
        maps = []
        for i in range(NCORES):
            ls = slice(i * 1024 + hf * 512, i * 1024 + hf * 512 + 512)
            cs = slice(i * 32 + hf * 16, i * 32 + hf * 16 + 16)
            xs = np.concatenate([x_lat[ls], x_ctx[cs]], 0).T
            br = np.stack([np.concatenate([br_lat[n][ls], br_ctx[n][cs]], 0).T for n in range(4)], 0)
            brf = np.ascontiguousarray(br.reshape(4, 4, 128, T3).transpose(2, 0, 1, 3))
            maps.append({"xT": fm(np.ascontiguousarray(xs)), "g": gf, "sc": sc, "sh": sh, "gate": ga, "brT": brf, "wg": wg, "wb": wb, "wo": wo})
        res = P.run(maps)
        for i in range(NCORES):
            o = res[i]["xmT"].T
            lat_out[i * 1024 + hf * 512:i * 1024 + hf * 512 + 512] = o[:512]
            ctx_out[i * 32 + hf * 16:i * 32 + hf * 16 + 16] = o[512:]
    return lat_out, ctx_out


SEGS = [(0, 512, 0), (512, 512, 0), (0, 32, 1)]
SEG_C0 = [0, 514, 1028]
T4 = 1062
PASSES = [dict(c0=0, W=514, blocks=[(0, 257, 0), (257, 257, 0)]),
          dict(c0=514, W=548, blocks=[(0, 257, 0), (257, 257, 0), (514, 34, 1)])]
WMAX = 548


def build_k4():
    P = Prog()
    S = P.S
    x_in = P.inp("xT", [128, KC, T4])
    g_in = P.inp("g", [128, KC])
    sc_in = P.inp("sc", [128, KC, 2])
    sh_in = P.inp("sh", [128, KC, 2])
    ga_in = P.inp("gate", [128, KC, 2])
    tap_in = P.inp("taps", [128, 88, 3])
    mask_in = P.inp("mask", [128, T4])
    wu_in = P.inp("wu", [44, 128, KC, 256], BF16)
    wd_in = P.inp("wd", [8, 4, 128, 11, 256], BF16)
    o_out = P.out("xoT", [D, T4])
    ones, bones = make_consts(P)
    ga, bga = P.load(ga_in, [128, KC, 2])
    taps, btaps = P.load(tap_in, [128, 88, 3])
    mask, bmask = P.load(mask_in, [128, T4])
    C = norm_setup(P, g_in, sc_in, sh_in)
    xT, bx = P.sb([128, KC, WMAX])
    hT, bh = P.sb([128, KC, WMAX], BF16)
    actT, bact = P.sb([128, 44, WMAX], BF16)
    psums = [P.ps() for _ in range(8)]
    wut = [P.sb([128, KC, 256], BF16) for _ in range(4)]
    wdt = [P.sb([128, 11, 256], BF16) for _ in range(2)]
    ug = [P.sb([128, WMAX]) for _ in range(2)]
    uv = [P.sb([128, WMAX]) for _ in range(2)]
    cg = [P.sb([128, WMAX]) for _ in range(2)]
    cv = [P.sb([128, WMAX]) for _ in range(2)]
    stg = [P.sb([128, 257]) for _ in range(3)]
    it = 0
    pi = 0
    mi_ = 0
    sc_ = 0
    for ps_ in PASSES:
        p0, W, blocks = ps_["c0"], ps_["W"], ps_["blocks"]
        n = W - 2
        S.dma("sp", xT[:, :, :W], x_in[0][:, :, p0:p0 + W], reads=[x_in[1]], writes=[bx])
        norm_run(P, C, xT, bx, hT, bh, blocks, ones, bones, psums)
        for c in range(KC):
            S.op("dve", "tensor_tensor", reads=[bh, bmask], writes=[bh], out=hT[:, c, :W], in0=hT[:, c, :W], in1=mask[:, p0:p0 + W], op=ALU.mult)
        S.op("pool", "memset", writes=[bact], ap=actT[:], constant=0.0)
        for mg in range(22):
            (wg_, bwg), (wv_, bwv) = wut[(it % 2) * 2], wut[(it % 2) * 2 + 1]
            it += 1
            S.dma("sp", wg_[:], wu_in[0][mg], reads=[wu_in[1]], writes=[bwg])
            S.dma("sp", wv_[:], wu_in[0][22 + mg], reads=[wu_in[1]], writes=[bwv])
            for mm in range(2):
                m = mg * 2 + mm
                (ug_, bug), (uv_, buv), (g_, bg_), (v_, bv_) = ug[mi_ % 2], uv[mi_ % 2], cg[mi_ % 2], cv[mi_ % 2]
                mi_ += 1
                for (c0, nn, kind) in blocks:
                    (pg, bpg), (pv, bpv) = psums[(pi % 4) * 2], psums[(pi % 4) * 2 + 1]
                    pi += 1
                    for k in range(KC):
                        S.op("pe", "matmul", reads=[bwg, bh], writes=[bpg], out=pg[:, :nn], lhsT=wg_[:, k, mm * 128:(mm + 1) * 128], rhs=hT[:, k, c0:c0 + nn], start=(k == 0), stop=(k == KC - 1))
                    for k in range(KC):
                        S.op("pe", "matmul", reads=[bwv, bh], writes=[bpv], out=pv[:, :nn], lhsT=wv_[:, k, mm * 128:(mm + 1) * 128], rhs=hT[:, k, c0:c0 + nn], start=(k == 0), stop=(k == KC - 1))
                    S.op("act", "copy", reads=[bpg], writes=[bug], out=ug_[:, c0:c0 + nn], in_=pg[:, :nn])
                    S.op("act", "copy", reads=[bpv], writes=[buv], out=uv_[:, c0:c0 + nn], in_=pv[:, :nn])
                for (u_, bu, o_, bo_, tm) in ((ug_, bug, g_, bg_, m), (uv_, buv, v_, bv_, 44 + m)):
                    S.op("dve", "tensor_scalar", reads=[bu, btaps], writes=[bo_], out=o_[:, :n], in0=u_[:, 0:n], scalar1=taps[:, tm, 0:1], scalar2=None, op0=ALU.mult)
                    for t in (1, 2):
                        S.op("dve", "scalar_tensor_tensor", reads=[bu, btaps, bo_], writes=[bo_], out=o_[:, :n], in0=u_[:, t:t + n], scalar=taps[:, tm, t:t + 1], in1=o_[:, :n], op0=ALU.mult, op1=ALU.add)
                S.op("act", "activation", reads=[bg_], writes=[bg_], out=g_[:, :n], in_=g_[:, :n], func=AF.Silu)
                S.op("dve", "tensor_tensor", reads=[bg_, bv_], writes=[bact], out=actT[:, m, 1:1 + n], in0=g_[:, :n], in1=v_[:, :n], op=ALU.mult)
        for fg in range(8):
            for kg in range(4):
                wd_, bwd = wdt[it % 2]
                it += 1
                S.dma("sp", wd_[:], wd_in[0][fg, kg], reads=[wd_in[1]], writes=[bwd])
                for f in range(2):
                    for bi, (c0, nn, kind) in enumerate(blocks):
                        pf, bpf = psums[f * 3 + bi]
                        for mi in range(11):
                            S.op("pe", "matmul", reads=[bwd, bact], writes=[bpf], out=pf[:, :nn], lhsT=wd_[:, mi, f * 128:(f + 1) * 128], rhs=actT[:, kg * 11 + mi, c0:c0 + nn],
                                 start=(kg == 0 and mi == 0), stop=(kg == 3 and mi == 10))
            for f in range(2):
                mfe = fg * 2 + f
                for bi, (c0, nn, kind) in enumerate(blocks):
                    pf, bpf = psums[f * 3 + bi]
                    st, bst = stg[sc_ % 3]
                    sc_ += 1
                    S.op("dve", "scalar_tensor_tensor", reads=[bpf, bga, bx], writes=[bst], out=st[:, :nn], in0=pf[:, :nn], scalar=ga[:, mfe, kind:kind + 1], in1=xT[:, mfe, c0:c0 + nn], op0=ALU.mult, op1=ALU.add)
                    S.dma("pool", o_out[0][mfe * 128:(mfe + 1) * 128, p0 + c0:p0 + c0 + nn], st[:, :nn], reads=[bst])
    return P


def run_k4(P, x_lat, x_ctx, g, mod_l, w_up_b, ffn_dw_l, w_down_b):
    gf = vec_fm(g)
    sc, sh, ga = mod_fm(mod_l, 4), mod_fm(mod_l, 3), mod_fm(mod_l, 5)
    taps = np.ascontiguousarray(ffn_dw_l.T.reshape(88, 128, 3).transpose(1, 0, 2))
    wu = tile_w(w_up_b, 256)
    wd = np.ascontiguousarray(w_down_b.reshape(4, 11, 128, 8, 256).transpose(3, 0, 2, 1, 4))
    maps = []
    for i in range(NCORES):
        cols, mk = [], []
        for (s, n, kind) in SEGS:
            src, t0 = (x_lat, i * 1024 + s) if kind == 0 else (x_ctx, i * 32 + s)
            cols.append(halo_rows(src, t0, n, 1))
            pos = np.arange(t0 - 1, t0 + n + 1)
            mk.append(((pos >= 0) & (pos < src.shape[0])).astype(np.float32))
        xs = np.concatenate(cols, 0).T
        mask = np.ascontiguousarray(np.broadcast_to(np.concatenate(mk)[None, :], (128, T4)))
        maps.append({"xT": fm(np.ascontiguousarray(xs)), "g": gf, "sc": sc, "sh": sh, "gate": ga, "taps": taps, "mask": mask, "wu": wu, "wd": wd})
    res = P.run(maps)
    lat = np.zeros((SEQ, D), np.float32)
    ctx = np.zeros((CTX, D), np.float32)
    for i in range(NCORES):
        o = res[i]["xoT"].T
        for si, (s, n, kind) in enumerate(SEGS):
            seg = o[SEG_C0[si] + 1:SEG_C0[si] + 1 + n]
            if kind == 0:
                lat[i * 1024 + s:i * 1024 + s + n] = seg
            else:
                ctx[i * 32 + s:i * 32 + s + n] = seg
    return lat, ctx


def build_k5():
    P = Prog()
    S = P.S
    x_in = P.inp("xT", [128, KC, 1024])
    g_in = P.inp("g", [128, KC])
    o_out = P.out("oT", [128, KC, 1024])
    ones, bones = make_consts(P)
    xT, bx = P.load(x_in, [128, KC, 1024])
    g, bg = P.load(g_in, [128, KC])
    ps, bps = P.ps()
    sq = [P.sb([128, 512]) for _ in range(2)]
    rstd, brs = P.sb([128, 512])
    res, bres = P.sb([128, KC, 1024])
    for c0 in (0, 512):
        for c in range(KC):
            s_, bs_ = sq[c % 2]
            S.op("act", "activation", reads=[bx], writes=[bs_], out=s_[:], in_=xT[:, c, c0:c0 + 512], func=AF.Square)
            S.op("pe", "matmul", reads=[bones, bs_], writes=[bps], out=ps[:], lhsT=ones[:], rhs=s_[:], start=(c == 0), stop=(c == KC - 1))
        S.op("act", "activation", reads=[bps, epsb[1]], writes=[brs], out=rstd[:], in_=ps[:], func=AF.Sqrt, bias=epsb[0][:], scale=1.0 / D)
        S.op("dve", "reciprocal", reads=[brs], writes=[brs], out=rstd[:], in_=rstd[:])
        for c in range(KC):
            S.op("dve", "scalar_tensor_tensor", reads=[bx, bg, brs], writes=[bres], out=res[:, c, c0:c0 + 512], in0=xT[:, c, c0:c0 + 512], scalar=g[:, c:c + 1], in1=rstd[:], op0=ALU.mult, op1=ALU.mult)
    S.dma("sp", o_out[0], res[:], reads=[bres])
    return P


def run_k5(P, x_lat, g):
    gf = vec_fm(g)
    maps = [{"xT": fm(np.ascontiguousarray(x_lat[i * 1024:(i + 1) * 1024].T)), "g": gf} for i in range(NCORES)]
    res = P.run(maps)
    return np.concatenate([unfm(r["oT"]).T for r in res], 0)


KW_ROWS = 8384


def build_kw():
    P = Prog()
    S = P.S
    w_in = P.inp("w", [KW_ROWS, 2048])
    o = P.out("wb", [KW_ROWS, 2048], BF16)
    for r0 in range(0, KW_ROWS, 512):
        r1 = min(r0 + 512, KW_ROWS)
        S.dma("pool", o[0][r0:r1], w_in[0][r0:r1], reads=[w_in[1]], max_dma_last_dim=4096)
    return P


def run_kw(P, mats):
    flat = np.concatenate([m.reshape(-1) for m in mats]).reshape(-1, 2048)
    assert flat.shape[0] == KW_ROWS * NCORES
    res = P.run([{"w": flat[i * KW_ROWS:(i + 1) * KW_ROWS]} for i in range(NCORES)])
    fb = np.concatenate([r["wb"] for r in res], 0).reshape(-1)
    out, o0 = [], 0
    for m in mats:
        out.append(fb[o0:o0 + m.size].reshape(m.shape))
        o0 += m.size
    return out


_PROGS = {}


def _prog(name, builder):
    if name not in _PROGS:
        _PROGS[name] = builder()
    return _PROGS[name]


def kernel(x, c, ctx, c_ctx, w_ada, b_ada, g_mix, w_in, na_rpb, rg_conv, w_rg, b_rg, rg_lambda,
           conf_dw, conf_ln_g, conf_ln_b, q_norm_g, k_norm_g, w_branch, w_out, g_ffn, w_up,
           ffn_dw, w_down, g_final):
    f = lambda a: np.ascontiguousarray(np.asarray(a, dtype=np.float32))
    x, c, ctx, c_ctx, w_ada, b_ada, g_mix, w_in, na_rpb, rg_conv, w_rg, b_rg, rg_lambda = map(f, (x, c, ctx, c_ctx, w_ada, b_ada, g_mix, w_in, na_rpb, rg_conv, w_rg, b_rg, rg_lambda))
    conf_dw, conf_ln_g, conf_ln_b, q_norm_g, k_norm_g, w_branch, w_out, g_ffn, w_up, ffn_dw, w_down, g_final = map(
        f, (conf_dw, conf_ln_g, conf_ln_b, q_norm_g, k_norm_g, w_branch, w_out, g_ffn, w_up, ffn_dw, w_down, g_final))
    xl, xc = x[0], ctx[0]
    depth = 2
    mats = []
    for l in range(depth):
        mats += [w_in[l], w_branch[l], w_out[l], w_up[l], w_down[l]]
    wb = run_kw(_prog("kw", build_kw), mats)
    mod = run_k0(_prog("k0", build_k0), c, c_ctx, w_ada, b_ada)
    for l in range(depth):
        w_in_b, w_br_b, w_out_b, w_up_b, w_dn_b = wb[5 * l:5 * l + 5]
        zl, zc = run_k1(_prog("k1", build_k1), xl, xc, g_mix[l], mod[l], w_in_b)
        al, ac = run_k2a(_prog("k2a", build_k2a), zl, zc, na_rpb[l])
        bl, bc = run_k2b(_prog("k2b", build_k2b), zl, zc, rg_conv[l], w_rg[l], b_rg[l], rg_lambda[l])
        cl, cc = run_k2c(_prog("k2c", build_k2c), zl, zc, conf_dw[l], conf_ln_g[l], conf_ln_b[l])
        dl, dc = run_k2d(_prog("k2d", build_k2d), zl, zc, q_norm_g[l], k_norm_g[l])
        xm, cm = run_k3(_prog("k3", build_k3), xl, xc, [al, bl, cl, dl], [ac, bc, cc, dc], g_mix[l], mod[l], w_in_b, w_br_b, w_out_b)
        xl, xc2 = run_k4(_prog("k4", build_k4), xm, cm, g_ffn[l], mod[l], w_up_b, ffn_dw[l], w_dn_b)
        if l < depth - 1:
            xc = xc2
    out = run_k5(_prog("k5", build_k5), xl, g_final)
    return np.ascontiguousarray(out[None].astype(np.float32))
```
